# Optimizing a Trainium2 kernel written in Bass

```python
import jax, jax.numpy as jnp
from jax import lax
import numpy as np

D_MODEL = 1024
BATCH = 32
SEQ = 2048
DEPTH = 2

GRID_W = 64
CTX_LEN = 256
N_MIXERS = 2
N_MOD = 9
D_FF = 2816
FFN_RES_WEIGHT = 0.5
NORM_EPS = 1e-6
NEG_INF = -1e30
D_RNN = D_MODEL
RG_BLOCKS = 4
RG_BLOCK_W = D_RNN // RG_BLOCKS
CONV_W = 4
CONV_LEFT = 2
LRU_C = 8.0
HEAD_DIM = 64
N_HEADS = D_MODEL // HEAD_DIM
N_KV_HEADS = 4
GROUP = N_HEADS // N_KV_HEADS
WINDOW = 128
Q_BLOCK = 128
ROPE_BASE = 10000.0
ROPE_AXIS_DIM = HEAD_DIM // 2
N_A = (DEPTH + 1) // 2
N_B = DEPTH // 2

kernel_name = 'hybrid_rglru_swa_diffusion'


def rms_norm(x, g):
    xf = x.astype(jnp.float32)
    y = xf * lax.rsqrt(jnp.mean(xf * xf, axis=-1, keepdims=True) + NORM_EPS)
    return (y * g.astype(jnp.float32)).astype(x.dtype)


def modulate(h, g_pre, shift, scale):
    return rms_norm(h, g_pre) * (1 + scale) + shift


def modulation(cond, w, b):
    m = jax.nn.silu(cond) @ w + b
    return jnp.moveaxis(m.reshape(cond.shape[0], N_MOD, D_MODEL), 1, 0)[:, :, None, :]


def swiglu(h, w_in, w_out):
    gate, up = jnp.split(h @ w_in, 2, axis=-1)
    return (jax.nn.silu(gate) * up) @ w_out


def half_ffn_update(h, mod, k, g_pre, g_post, w_in, w_out):
    y = swiglu(modulate(h, g_pre, mod[3 * k], mod[3 * k + 1]), w_in, w_out)
    return h + FFN_RES_WEIGHT * mod[3 * k + 2] * rms_norm(y, g_post)


def centred_dwconv(x, w, b):
    L = x.shape[1]
    xp = jnp.pad(x, ((0, 0), (CONV_LEFT, CONV_W - 1 - CONV_LEFT), (0, 0)))
    out = b
    for k in range(CONV_W):
        out = out + xp[:, k:k + L] * w[k]
    return out


def _scan_combine(e1, e2):
    a1, b1 = e1
    a2, b2 = e2
    return a1 * a2, a2 * b1 + b2


def linear_scan(a, b, h0):
    a_cum, b_cum = lax.associative_scan(_scan_combine, (a, b), axis=1)
    return a_cum * h0[:, None] + b_cum


def rglru_coeffs(xr, gate_w, gate_b, lam):
    B_, L = xr.shape[:2]
    xb = xr.reshape(B_, L, RG_BLOCKS, RG_BLOCK_W)
    g = jnp.einsum('blnj,gnjk->gblnk', xb, gate_w).reshape(2, B_, L, D_RNN) + gate_b[:, None, None]
    g = jax.nn.sigmoid(g.astype(jnp.float32))
    r, i = g[0], g[1]
    log_a = -LRU_C * r * jax.nn.softplus(-lam.astype(jnp.float32))
    a = jnp.exp(log_a)
    b = jnp.sqrt(1.0 - jnp.exp(2.0 * log_a)) * (i * xr.astype(jnp.float32))
    return a, b


def rglru_mixer(hx, hc, w_in, conv_w, conv_b, gate_w, gate_b, lam, w_out, ctx_out):
    def branches(h):
        gate, xr = jnp.split(h @ w_in, 2, axis=-1)
        return jax.nn.gelu(gate), centred_dwconv(xr, conv_w, conv_b)
    gx, rx = branches(hx)
    gc, rc = branches(hc)
    h0 = jnp.zeros((hx.shape[0], D_RNN), jnp.float32)
    a, b = rglru_coeffs(rc, gate_w[0], gate_b[0], lam[0])
    sc_f = linear_scan(a, b, h0)
    a, b = rglru_coeffs(rx, gate_w[0], gate_b[0], lam[0])
    sx_f = linear_scan(a, b, sc_f[:, -1])
    a, b = rglru_coeffs(rc[:, ::-1], gate_w[1], gate_b[1], lam[1])
    sc_b = linear_scan(a, b, h0)
    a, b = rglru_coeffs(rx[:, ::-1], gate_w[1], gate_b[1], lam[1])
    sx_b = linear_scan(a, b, sc_b[:, -1])[:, ::-1]
    yx = (gx * (sx_f + sx_b).astype(hx.dtype)) @ w_out
    yc = (gc * (sc_f + sc_b[:, ::-1]).astype(hc.dtype)) @ w_out if ctx_out else None
    return yx, yc


def axial_rope_angles(L):
    rows = L // GRID_W
    row = jnp.repeat(jnp.arange(rows, dtype=jnp.float32), GRID_W)
    col = jnp.tile(jnp.arange(GRID_W, dtype=jnp.float32), rows)
    inv = 1.0 / (ROPE_BASE ** (jnp.arange(0, ROPE_AXIS_DIM, 2, dtype=jnp.float32) / ROPE_AXIS_DIM))
    return row[:, None] * inv, col[:, None] * inv


def rope_half(x, ang):
    x1, x2 = jnp.split(x, 2, axis=-1)
    cos = jnp.cos(ang)[:, None]
    sin = jnp.sin(ang)[:, None]
    return jnp.concatenate([x1 * cos - x2 * sin, x2 * cos + x1 * sin], axis=-1)


def apply_axial_rope(x, ang_row, ang_col):
    xf = x.astype(jnp.float32)
    x_row, x_col = jnp.split(xf, 2, axis=-1)
    return jnp.concatenate([rope_half(x_row, ang_row), rope_half(x_col, ang_col)], axis=-1).astype(x.dtype)


def softmax_with_sink(scores, sink):
    sink_col = jnp.broadcast_to(sink[None, :, :, None, None], scores.shape[:-1] + (1,))
    p = jax.nn.softmax(jnp.concatenate([scores, sink_col], axis=-1), axis=-1)
    return p[..., :-1]


def window_attention_mixer(hx, hc, w_qkv, w_o, sink, ctx_out):
    B_, L, _ = hx.shape
    C = hc.shape[1]

    def project(h):
        q, k, v = jnp.split(h @ w_qkv, [N_HEADS * HEAD_DIM, (N_HEADS + N_KV_HEADS) * HEAD_DIM], axis=-1)
        n = h.shape[1]
        return (q.reshape(B_, n, N_HEADS, HEAD_DIM), k.reshape(B_, n, N_KV_HEADS, HEAD_DIM),
                v.reshape(B_, n, N_KV_HEADS, HEAD_DIM))

    qx, kx, vx = project(hx)
    qc, kc, vc = project(hc)
    ang_row, ang_col = axial_rope_angles(L)
    qx = apply_axial_rope(qx, ang_row, ang_col).reshape(B_, L, N_KV_HEADS, GROUP, HEAD_DIM)
    kx = apply_axial_rope(kx, ang_row, ang_col)
    qc = qc.reshape(B_, C, N_KV_HEADS, GROUP, HEAD_DIM)
    scale = HEAD_DIM ** -0.5
    sink_kg = sink.reshape(N_KV_HEADS, GROUP).astype(jnp.float32)

    span = Q_BLOCK + 2 * WINDOW
    kp = jnp.pad(kx, ((0, 0), (WINDOW, WINDOW), (0, 0), (0, 0)))
    vp = jnp.pad(vx, ((0, 0), (WINDOW, WINDOW), (0, 0), (0, 0)))
    rel = jnp.arange(span)[None, :] - WINDOW - jnp.arange(Q_BLOCK)[:, None]
    band = jnp.abs(rel) <= WINDOW

    def block(n):
        start = n * Q_BLOCK
        qb = lax.dynamic_slice_in_dim(qx, start, Q_BLOCK, axis=1)
        kb = lax.dynamic_slice_in_dim(kp, start, span, axis=1)
        vb = lax.dynamic_slice_in_dim(vp, start, span, axis=1)
        key_pos = start - WINDOW + jnp.arange(span)
        valid = band & ((key_pos >= 0) & (key_pos < L))[None, :]
        s_lat = jnp.einsum('bqkgd,bskd->bkgqs', qb, kb).astype(jnp.float32) * scale
        s_lat = jnp.where(valid, s_lat, NEG_INF)
        s_ctx = jnp.einsum('bqkgd,bckd->bkgqc', qb, kc).astype(jnp.float32) * scale
        p = softmax_with_sink(jnp.concatenate([s_lat, s_ctx], axis=-1), sink_kg).astype(vx.dtype)
        return (jnp.einsum('bkgqs,bskd->bqkgd', p[..., :span], vb)
                + jnp.einsum('bkgqc,bckd->bqkgd', p[..., span:], vc))

    out = lax.map(block, jnp.arange(L // Q_BLOCK))
    yx = jnp.moveaxis(out, 0, 1).reshape(B_, L, N_HEADS * HEAD_DIM) @ w_o
    if not ctx_out:
        return yx, None
    s_c = jnp.einsum('bqkgd,bckd->bkgqc', qc, kc).astype(jnp.float32) * scale
    p_c = softmax_with_sink(s_c, sink_kg).astype(vc.dtype)
    yc = jnp.einsum('bkgqc,bckd->bqkgd', p_c, vc).reshape(B_, C, N_HEADS * HEAD_DIM) @ w_o
    return yx, yc


def setup_inputs(seed: int = 0) -> dict:
    key = jax.random.key(seed)
    ks = jax.random.split(key, 19)

    def normal(k, shape):
        return jax.random.normal(k, shape, jnp.float32)

    def dense(k, shape, fan_in):
        return normal(k, shape) * fan_in ** -0.5

    a0 = jax.random.uniform(ks[14], (N_A, 2, D_RNN), jnp.float32, minval=0.9, maxval=0.999)
    return {
        'x': normal(ks[0], (BATCH, SEQ, D_MODEL)),
        'c': normal(ks[1], (BATCH, D_MODEL)),
        'ctx': normal(ks[2], (BATCH, CTX_LEN, D_MODEL)),
        'c_ctx': normal(ks[3], (D_MODEL,)),
        'w_mod': 0.5 * dense(ks[4], (DEPTH, D_MODEL, N_MOD * D_MODEL), D_MODEL),
        'b_mod': 0.01 * normal(ks[5], (DEPTH, N_MOD * D_MODEL)),
        'norm_g': 1.0 + 0.02 * normal(ks[6], (DEPTH, 6, D_MODEL)),
        'ffn_w_in': dense(ks[7], (DEPTH, 2, D_MODEL, 2 * D_FF), D_MODEL),
        'ffn_w_out': dense(ks[8], (DEPTH, 2, D_FF, D_MODEL), D_FF),
        'rg_w_in': dense(ks[9], (N_A, D_MODEL, 2 * D_RNN), D_MODEL),
        'rg_conv_w': dense(ks[10], (N_A, CONV_W, D_RNN), CONV_W),
        'rg_conv_b': 0.01 * normal(ks[11], (N_A, D_RNN)),
        'rg_gate_w': dense(ks[12], (N_A, 2, 2, RG_BLOCKS, RG_BLOCK_W, RG_BLOCK_W), RG_BLOCK_W),
        'rg_gate_b': 0.01 * normal(ks[13], (N_A, 2, 2, D_RNN)),
        'rg_lambda': jnp.log(a0) - jnp.log1p(-a0),
        'rg_w_out': dense(ks[15], (N_A, D_RNN, D_MODEL), D_RNN),
        'attn_w_qkv': dense(ks[16], (N_B, D_MODEL, (N_HEADS + 2 * N_KV_HEADS) * HEAD_DIM), D_MODEL),
        'attn_w_o': dense(ks[17], (N_B, N_HEADS * HEAD_DIM, D_MODEL), N_HEADS * HEAD_DIM),
        'attn_sink': normal(ks[18], (N_B, N_HEADS)),
    }


def reference(x, c, ctx, c_ctx, w_mod, b_mod, norm_g, ffn_w_in, ffn_w_out,
              rg_w_in, rg_conv_w, rg_conv_b, rg_gate_w, rg_gate_b, rg_lambda, rg_w_out,
              attn_w_qkv, attn_w_o, attn_sink):
    xc = ctx
    for i in range(DEPTH):
        last = i == DEPTH - 1
        mx = modulation(c, w_mod[i], b_mod[i])
        mc = modulation(c_ctx[None], w_mod[i], b_mod[i])
        g = norm_g[i]
        x = half_ffn_update(x, mx, 0, g[0], g[3], ffn_w_in[i, 0], ffn_w_out[i, 0])
        xc = half_ffn_update(xc, mc, 0, g[0], g[3], ffn_w_in[i, 0], ffn_w_out[i, 0])
        hx = modulate(x, g[1], mx[3], mx[4])
        hc = modulate(xc, g[1], mc[3], mc[4])
        j = i // N_MIXERS
        if i % N_MIXERS == 0:
            yx, yc = rglru_mixer(hx, hc, rg_w_in[j], rg_conv_w[j], rg_conv_b[j], rg_gate_w[j],
                                 rg_gate_b[j], rg_lambda[j], rg_w_out[j], not last)
        else:
            yx, yc = window_attention_mixer(hx, hc, attn_w_qkv[j], attn_w_o[j], attn_sink[j], not last)
        x = x + mx[5] * rms_norm(yx, g[4])
        x = half_ffn_update(x, mx, 2, g[2], g[5], ffn_w_in[i, 1], ffn_w_out[i, 1])
        if not last:
            xc = xc + mc[5] * rms_norm(yc, g[4])
            xc = half_ffn_update(xc, mc, 2, g[2], g[5], ffn_w_in[i, 1], ffn_w_out[i, 1])
    return x
```

```python
import os
import numpy as np
import concourse.bass as bass
import concourse.mybir as mybir
from concourse.bass_utils import run_bass_kernel_spmd

F32 = mybir.dt.float32
BF16 = mybir.dt.bfloat16
AF = mybir.ActivationFunctionType
ALU = mybir.AluOpType

D = 1024
SEQ = 2048
CTX = 256
NTOK = SEQ + CTX
NT = NTOK // 128
DFF = 2816
NFC = DFF // 128
NCORES = 8
EPS = 1e-6
NV = 328
O_BMOD, O_G, O_CW, O_CB, O_GB, O_LAM = 0, 144, 240, 272, 280, 312


class Buf:
    __slots__ = ("name", "lw", "rd")

    def __init__(self, name=""):
        self.name = name
        self.lw = None
        self.rd = []


class DSem:
    __slots__ = ("h", "count", "key")

    def __init__(self, h, key):
        self.h = h
        self.count = 0
        self.key = key


class Op:
    __slots__ = ("eng", "idx", "fn", "waits", "marked", "clock", "dsem", "dval", "semval")

    def __init__(self, eng, idx, fn):
        self.eng = eng
        self.idx = idx
        self.fn = fn
        self.waits = []
        self.marked = False
        self.clock = None
        self.dsem = None
        self.dval = 0
        self.semval = 0


class Prog:
    ENG = ("pe", "act", "dve", "pool", "sp")

    def __init__(self, nc):
        self.nc = nc
        self.q = {k: [] for k in self.ENG}
        self.clock = {k: {} for k in self.ENG}

    @staticmethod
    def _kv(op):
        if op.dsem is not None:
            return op.dsem.key, op.dval
        return op.eng, op.idx + 1

    def op(self, eng, fn, R=(), W=(), dsem=None, extra=()):
        q = self.q[eng]
        o = Op(eng, len(q), fn)
        is_dma = dsem is not None
        if is_dma:
            dsem.count += 16
            o.dsem = dsem
            o.dval = dsem.count
        deps = []
        for b in R:
            if b.lw is not None:
                deps.append((b.lw, True))
        for b in W:
            if b.lw is not None:
                deps.append((b.lw, False))
            for r in b.rd:
                deps.append((r, False))
        for d in extra:
            deps.append((d, True))
        clk = self.clock[eng]
        best = {}
        for (d, raw) in deps:
            if d is o:
                continue
            d_dma = d.dsem is not None
            if (not d_dma) and d.eng == eng and not is_dma:
                if eng == "pe" or (eng != "pool" and not raw):
                    continue
            k, v = self._kv(d)
            if clk.get(k, 0) >= v:
                continue
            cur = best.get(k)
            if cur is None or cur[0] < v:
                best[k] = (v, d)
        for k, (v, d) in best.items():
            if clk.get(k, 0) >= v:
                continue
            o.waits.append(d)
            d.marked = True
            for kk, vv in d.clock.items():
                if clk.get(kk, 0) < vv:
                    clk[kk] = vv
        c = dict(clk)
        k, v = self._kv(o)
        c[k] = v
        o.clock = c
        for b in R:
            b.rd.append(o)
        for b in W:
            b.lw = o
            b.rd = []
        q.append(o)
        return o

    def barrier(self, flush=()):
        lasts = []
        for e in self.ENG:
            q = self.q[e]
            for o in reversed(q):
                if o.dsem is None:
                    lasts.append(o)
                    break
        fl = list(flush)
        for e in self.ENG:
            self.op(e, lambda en: en.nop(), W=fl if e == "sp" else (), extra=[o for o in lasts if o.eng != e])
        spl = self.q["sp"][-1]
        for e in self.ENG:
            if e != "sp":
                self.op(e, lambda en: en.nop(), extra=[spl])

    def emit(self, sems):
        for e, q in self.q.items():
            n = 0
            for o in q:
                if o.dsem is None and o.marked:
                    n += 1
                    o.semval = n

        def run(ename, engine):
            for o in self.q[ename]:
                for d in o.waits:
                    if d.dsem is not None:
                        engine.wait_ge(d.dsem.h, d.dval)
                    else:
                        engine.wait_ge(sems[d.eng], d.semval)
                ins = o.fn(engine)
                if o.dsem is not None:
                    ins.then_inc(o.dsem.h, 16)
                elif o.marked:
                    ins.then_inc(sems[ename], 1)
        return run


class Rot:
    def __init__(self, items):
        self.items = items
        self.i = 0

    def next(self):
        it = self.items[self.i % len(self.items)]
        self.i += 1
        return it


class SBAlloc:
    LO = 16640
    HI = 229376

    def __init__(self, nc):
        self.nc = nc
        self.off = self.LO
        self.n = 0
        self.peak = 0

    def alloc(self, shape, dt):
        esz = 4 if dt == F32 else 2
        nb = esz
        for s in shape[1:]:
            nb *= s
        off = (self.off + 63) // 64 * 64
        assert off + nb <= self.HI, "SBUF overflow: need %d at %d" % (nb, off)
        self.off = off + nb
        self.peak = max(self.peak, self.off)
        self.n += 1
        return self.nc.alloc_sbuf_tensor_at("t%d" % self.n, list(shape), dt, offset=off)

    def mark(self):
        return self.off

    def reset(self, m):
        self.off = m


def build_program(nseq, nstages=6):
    nc = bass.Bass("TRN2", target_bir_lowering=False)
    R = nseq + 1
    full = nstages >= 6

    def din(name, shape, dt=F32):
        return nc.dram_tensor(name, list(shape), dt, kind="ExternalInput").ap()

    def dscr(name, shape, dt=BF16):
        return nc.dram_tensor(name, list(shape), dt, kind="Internal").ap()

    xin = din("xin", [nseq, NTOK, D])
    cond = din("cond", [R * 8, 128])
    small = din("small", [NV, 128])
    sink = din("sink", [1, 16])
    ropec = din("ropec", [128, SEQ])
    ropes = din("ropes", [128, SEQ])
    w_mod = din("w_mod", [2, D, 9 * D])
    ffn_w_in = din("ffn_w_in", [2, 2, D, 2 * DFF])
    ffn_w_out = din("ffn_w_out", [2, 2, DFF, D])
    rg_w_in = din("rg_w_in", [D, 2 * D])
    rg_gate_w = din("rg_gate_w", [2, 2, 4, 256, 256])
    rg_w_out = din("rg_w_out", [D, D])
    wqk2 = din("wqk2", [D, 2, 1536])
    wv = din("wv", [D, 256])
    attn_w_o = din("attn_w_o", [D, D])
    if full:
        outp = nc.dram_tensor("out", [nseq, SEQ, D], F32, kind="ExternalOutput").ap()
    else:
        outp = nc.dram_tensor("dbg", [nseq, NTOK, D], F32, kind="ExternalOutput").ap()
    resD = dscr("resD", [nseq, NTOK, D], F32)
    win_s = [[dscr("win_s%d%d" % (l, j), [11, 128, 8, 2, 256]) for j in range(2)] for l in range(2)]
    wout_s = [[dscr("wout_s%d%d" % (l, j), [128, NFC, D]) for j in range(2)] for l in range(2)]
    rgin_s = dscr("rgin_s", [4, 128, 8, 2, 256])
    rggw_s = dscr("rggw_s", [4, 128, 2, 2, 2, 256])
    rgwo_s = dscr("rgwo_s", [128, 8, D])
    wqk_s = dscr("wqk_s", [6, 128, 8, 2, 256])
    wv_s = dscr("wv_s", [128, 8, 256])
    awo_s = dscr("awo_s", [128, 8, D])

    P = Prog(nc)
    sb = SBAlloc(nc)
    sem_handles = []

    def new_sem(name):
        h = nc.alloc_semaphore(name)
        sem_handles.append(h)
        return h

    nds = [0]

    all_ds = []

    def new_dsem():
        nds[0] += 1
        d = DSem(new_sem("d%d" % nds[0]), "d%d" % nds[0])
        all_ds.append(d)
        return d

    esems = {k: new_sem("s_" + k) for k in Prog.ENG}

    ps = nc.alloc_psum_tensor("ps", [128, 4096], F32)
    BK = [ps[:, b * 512:(b + 1) * 512] for b in range(8)]
    BKb = [Buf("bank%d" % b) for b in range(8)]

    def bank_bf(b):
        return BK[b].bitcast(BF16)

    def pair(b):
        return ps[:, b * 512:(b + 2) * 512]

    ident_b = sb.alloc([128, 128], BF16)
    ident_f = sb.alloc([128, 128], F32)
    ones_f = sb.alloc([128, 128], F32)
    mask_prev = sb.alloc([128, 4, 128], BF16)
    mask_next = sb.alloc([128, 4, 128], BF16)
    smallT = sb.alloc([128, NV], F32)
    sT = sb.alloc([128, R * 8], F32)
    modT = sb.alloc([128, 2, 72, R], F32)
    gmT = sb.alloc([128, 2, 3, R, 8], F32)
    gcT = sb.alloc([128, 2, 3, R, 8], F32)
    coef = sb.alloc([128, 16], F32)
    esink = sb.alloc([128, 16], F32)
    hgb = sb.alloc([128, 32], F32)
    hcoef = sb.alloc([128, 16], F32)
    quartT = sb.alloc([128, 1], F32)
    epsT = sb.alloc([128, 1], F32)
    oneT = sb.alloc([128, 1], F32)
    wslots_t = sb.alloc([128, 3, 8, 2, 256], BF16)
    wslot = Rot([(wslots_t[:, i], Buf("ws%d" % i), new_dsem()) for i in range(3)])
    ws_bufs = [it[1] for it in wslot.items]
    gcbc = Rot([(sb.alloc([128, D], F32), Buf("gcbc%d" % i)) for i in range(2)])
    stat_t = sb.alloc([128, 16, 4], F32)
    stat = Rot([(stat_t[:, i, :], Buf("st%d" % i)) for i in range(16)])
    Bconst = Buf("const")
    Bmod = Buf("mod")
    persist_mark = sb.mark()

    Bres = {(s, t): Buf("res%d_%d" % (s, t)) for s in range(nseq) for t in range(NT)}

    def mm(out, lhsT, rhs, start, stop, R_, W_):
        return P.op("pe", lambda e: e.matmul(out, lhsT=lhsT, rhs=rhs, start=start, stop=stop), R=R_, W=W_)

    def tr(out, in_, ident, R_, W_):
        return P.op("pe", lambda e: e.transpose(out=out, in_=in_, identity=ident), R=R_, W=W_)

    def act(out, in_, func, R_, W_, bias=None, scale=None, accum=None):
        kw = {}
        if bias is not None:
            kw["bias"] = bias
        if scale is not None:
            kw["scale"] = scale
        if accum is not None:
            kw["accum_out"] = accum
        return P.op("act", lambda e: e.activation(out=out, in_=in_, func=func, **kw), R=R_, W=W_)

    def ts(eng, out, in0, s1, s2, op0, op1, R_, W_):
        if s2 is None:
            return P.op(eng, lambda e: e.tensor_scalar(out=out, in0=in0, scalar1=s1, scalar2=None, op0=op0), R=R_, W=W_)
        return P.op(eng, lambda e: e.tensor_scalar(out=out, in0=in0, scalar1=s1, scalar2=s2, op0=op0, op1=op1), R=R_, W=W_)

    def tt(eng, out, in0, in1, op, R_, W_):
        return P.op(eng, lambda e: e.tensor_tensor(out=out, in0=in0, in1=in1, op=op), R=R_, W=W_)

    def stt(eng, out, in0, s, in1, op0, op1, R_, W_):
        return P.op(eng, lambda e: e.scalar_tensor_tensor(out=out, in0=in0, scalar=s, in1=in1, op0=op0, op1=op1), R=R_, W=W_)

    def cp(eng, out, in_, R_, W_):
        return P.op(eng, lambda e: e.tensor_copy(out=out, in_=in_), R=R_, W=W_)

    def dma(q, out, in_, R_, W_, dsem):
        return P.op(q, lambda e: e.dma_start(out=out, in_=in_), R=R_, W=W_, dsem=dsem)

    def rstd_from(ssq_ap, out_ap, stbuf):
        ts("dve", out_ap, ssq_ap, 1.0 / D, EPS, ALU.mult, ALU.add, [stbuf], [stbuf])
        act(out_ap, out_ap, AF.Sqrt, [stbuf], [stbuf])
        P.op("dve", lambda e: e.reciprocal(out=out_ap, in_=out_ap), R=[stbuf], W=[stbuf])

    ov = sb.mark()
    P.op("pool", lambda e: e.memset(ident_f[:], 1.0), W=[Bconst])
    P.op("pool", lambda e: e.affine_select(out=ident_f[:], in_=ident_f[:], pattern=[[-1, 128]], compare_op=ALU.is_equal,
                                           fill=0.0, base=0, channel_multiplier=1), R=[Bconst], W=[Bconst])
    P.op("pool", lambda e: e.memset(ones_f[:], 1.0), W=[Bconst])
    P.op("pool", lambda e: e.memset(epsT[:], EPS), W=[Bconst])
    P.op("pool", lambda e: e.memset(oneT[:], 1.0), W=[Bconst])
    cp("dve", ident_b[:], ident_f[:], [Bconst], [Bconst])
    P.op("pool", lambda e: e.memset(mask_prev[:], -30000.0), W=[Bconst])
    P.op("pool", lambda e: e.memset(mask_next[:], -30000.0), W=[Bconst])
    P.op("pool", lambda e: e.affine_select(out=mask_prev[:], in_=mask_prev[:], pattern=[[0, 4], [1, 128]], compare_op=ALU.is_gt,
                                           fill=0.0, base=0, channel_multiplier=-1), R=[Bconst], W=[Bconst])
    P.op("pool", lambda e: e.affine_select(out=mask_next[:], in_=mask_next[:], pattern=[[0, 4], [-1, 128]], compare_op=ALU.is_gt,
                                           fill=0.0, base=0, channel_multiplier=1), R=[Bconst], W=[Bconst])
    sm_in = sb.alloc([128, 3, 128], F32)
    Bsm = Buf("sm")
    ds_sm = new_dsem()
    dma("sp", sm_in[:, 0, :], small[0:128, :], [], [], ds_sm)
    dma("sp", sm_in[:, 1, :], small[128:256, :], [], [], ds_sm)
    dma("sp", sm_in[0:72, 2, :], small[256:328, :], [], [Bsm], ds_sm)
    for i, n in enumerate((128, 128, 72)):
        tr(BK[0][:, i * 128:i * 128 + n], sm_in[0:n, i, :], ident_f[0:n, 0:n], [Bsm, Bconst], [BKb[0]])
    cp("dve", smallT[:, :], BK[0][:, 0:NV], [BKb[0]], [Bmod])
    cd_in = sb.alloc([128, 128], F32)
    ds_cd = new_dsem()
    Bcd = Buf("cd")
    dma("sp", cd_in[0:R * 8, :], cond, [], [Bcd], ds_cd)
    act(cd_in[0:R * 8, :], cd_in[0:R * 8, :], AF.Silu, [Bcd], [Bcd])
    tr(BK[1][:, 0:R * 8], cd_in[0:R * 8, :], ident_f[0:R * 8, 0:R * 8], [Bcd, Bconst], [BKb[1]])
    cp("dve", sT[:, :], BK[1][:, 0:R * 8], [BKb[1]], [Bmod])
    wm_t = sb.alloc([128, 2, 8, 512], F32)
    wms = Rot([(wm_t[:, i], Buf("wm%d" % i), new_dsem()) for i in range(2)])
    sT3 = sT[:, :].rearrange("p (r k) -> p k r", k=8)
    for l in range(2):
        bank = 2 + l
        for g in range(18):
            wt_, wb_, wd_ = wms.next()
            dma("sp", wt_, w_mod[l, :, g * 512:(g + 1) * 512].rearrange("(kc p) c -> p kc c", p=128), [], [wb_], wd_)
            for c4 in range(4):
                j = g * 4 + c4
                for kc in range(8):
                    mm(BK[bank][:, j * R:(j + 1) * R], wt_[:, kc, c4 * 128:(c4 + 1) * 128], sT3[:, kc, :],
                       kc == 0, kc == 7, [wb_, Bmod], [BKb[bank]])
        for r in range(R):
            tt("dve", modT[:, l, :, r], BK[bank][:, 0:72 * R].rearrange("p (j r) -> p j r", r=R)[:, :, r],
               smallT[:, O_BMOD + l * 72:O_BMOD + (l + 1) * 72], ALU.add, [BKb[bank], Bmod], [Bmod])
    for l in range(2):
        for k in range(3):
            wt_k = 1.0 if k == 1 else 0.5
            gpre = smallT[:, O_G + (l * 6 + k) * 8:O_G + (l * 6 + k) * 8 + 8]
            gpost = smallT[:, O_G + (l * 6 + 3 + k) * 8:O_G + (l * 6 + 3 + k) * 8 + 8]
            for r in range(R):
                stt("dve", gmT[:, l, k, r, :], modT[:, l, (3 * k + 1) * 8:(3 * k + 2) * 8, r], 1.0, gpre, ALU.add, ALU.mult, [Bmod], [Bmod])
                stt("dve", gcT[:, l, k, r, :], modT[:, l, (3 * k + 2) * 8:(3 * k + 3) * 8, r], wt_k, gpost, ALU.mult, ALU.mult, [Bmod], [Bmod])
    act(coef[:, :], smallT[:, O_LAM:O_LAM + 16], AF.Exp, [Bmod], [Bmod], scale=-1.0)
    act(coef[:, :], coef[:, :], AF.Ln, [Bmod], [Bmod], bias=oneT[:, 0:1])
    ts("dve", coef[:, :], coef[:, :], -8.0, None, ALU.mult, None, [Bmod], [Bmod])
    ts("dve", hcoef[:, :], coef[:, :], 0.5, None, ALU.mult, None, [Bmod], [Bmod])
    ts("dve", hgb[:, :], smallT[:, O_GB:O_GB + 32], 0.5, None, ALU.mult, None, [Bmod], [Bmod])
    P.op("pool", lambda e: e.memset(quartT[:], 0.25), W=[Bmod])
    ds_sk = new_dsem()
    dma("sp", esink[:, :], sink.partition_broadcast(128), [], [Bmod], ds_sk)
    act(esink[:, :], esink[:, :], AF.Exp, [Bmod], [Bmod])

    def cast_pairs(dst, src3, ngroups):
        b = Buf("cast")
        d = new_dsem()
        n = ngroups * 2
        i = 0
        for g in range(ngroups):
            for u in range(2):
                i += 1
                dma("pool", dst[g, :, :, u, :], src3[:, u, g * 256:(g + 1) * 256].rearrange("(kc p) c -> p kc c", p=128),
                    [], [b] if i == n else [], d)
        return b

    def cast_rows(dst, src, kcn, step=4):
        b = Buf("cast")
        d = new_dsem()
        v = src.rearrange("(kc p) c -> p kc c", p=128)
        k0 = 0
        while k0 < kcn:
            k1 = min(kcn, k0 + step)
            dma("pool", dst[:, k0:k1, :], v[:, k0:k1, :], [], [b] if k1 == kcn else [], d)
            k0 = k1
        return b

    def cast_ffn(l, j):
        a = cast_pairs(win_s[l][j], ffn_w_in[l, j].rearrange("k (u f) -> k u f", u=2), 11)
        b = cast_rows(wout_s[l][j], ffn_w_out[l, j], NFC)
        return a, b

    Bc = {}
    Bc["ffn00"] = cast_ffn(0, 0)
    if nstages > 1:
        Bc["rgin"] = cast_pairs(rgin_s, rg_w_in.rearrange("k (u f) -> k u f", u=2), 4)
        bgw = Buf("cast")
        dgw = new_dsem()
        i = 0
        for n in range(4):
            for d_ in range(2):
                for g_ in range(2):
                    i += 1
                    dma("pool", rggw_s[n, :, d_, g_, :, :], rg_gate_w[d_, g_, n].rearrange("(jc p) k -> p jc k", p=128),
                        [], [bgw] if i == 16 else [], dgw)
        Bc["rggw"] = bgw
        Bc["rgwo"] = cast_rows(rgwo_s, rg_w_out, 8)
    if nstages > 2:
        Bc["ffn01"] = cast_ffn(0, 1)
    if nstages > 3:
        Bc["ffn10"] = cast_ffn(1, 0)
    if nstages > 4:
        Bc["wqk"] = cast_pairs(wqk_s, wqk2, 6)
        Bc["wv"] = cast_rows(wv_s, wv, 8, step=8)
        Bc["awo"] = cast_rows(awo_s, attn_w_o, 8)
    if nstages > 5:
        Bc["ffn11"] = cast_ffn(1, 1)

    P.barrier(flush=[Bsm, Bcd] + [it[1] for it in wms.items])
    sb.reset(ov)

    def role_of(s, t):
        return nseq if t < 2 else s

    def src_tile(stage, s, t):
        if stage == 0:
            return xin[s, t * 128:(t + 1) * 128, :], []
        return resD[s, t * 128:(t + 1) * 128, :], [Bres[(s, t)]]

    def dst_tile(stage, s, t):
        if full and stage == 5:
            return outp[s, (t - 2) * 128:(t - 1) * 128, :], []
        return resD[s, t * 128:(t + 1) * 128, :], [Bres[(s, t)]]

    class TileCtx:
        def __init__(self, nslots, nxs=2):
            self.slots = Rot([(sb.alloc([128, D], F32), Buf("slot%d" % i), new_dsem()) for i in range(nslots)])
            self.xs = Rot([(sb.alloc([128, D], BF16), Buf("xs%d" % i)) for i in range(nxs)])
            self.junk = sb.alloc([128, D], BF16)
            self.Bjunk = Buf("junk")
            self.tbank = Rot([0, 1])

            self.base_slots = self.slots
            self.extra_flush = []

        def flush(self):
            return [it[1] for it in self.base_slots.items] + list(self.extra_flush)

    def make_gcbc(l, k, r):
        t_, b_ = gcbc.next()
        dg = tc_cur[0]
        for half in range(2):
            bank = 6 + half
            for q4 in range(4):
                kc = half * 4 + q4
                dt_, db_ = dg.next()
                ts("dve", dt_, ident_f[:, :], gcT[:, l, k, r, kc:kc + 1], None, ALU.mult, None, [Bconst, Bmod], [db_])
                mm(BK[bank][:, q4 * 128:(q4 + 1) * 128], ones_f[:, :], dt_, True, True, [db_, Bconst], [BKb[bank]])
        cp("dve", t_[:, :], pair(6), [BKb[6], BKb[7]], [b_])
        return t_, b_

    tc_cur = [None]

    def prepA(tc, stage, s, t):
        sl, slb, sld = tc.slots.next()
        src, srcb = src_tile(stage, s, t)
        dma("pool", sl[:, :], src, srcb, [slb], sld)
        st, stb = stat.next()
        act(tc.junk[:, :], sl[:, :], AF.Square, [slb], [tc.Bjunk, stb], accum=st[:, 0:1])
        rstd_from(st[:, 0:1], st[:, 1:2], stb)
        xs, xsb = tc.xs.next()
        ts("dve", xs[:, :], sl[:, :], st[:, 1:2], None, ALU.mult, None, [slb, stb], [xsb])
        return xs, xsb

    def prepB(tc, l, k, r, xs, xsb, hT_ap, hTb):
        b = tc.tbank.next()
        pb = bank_bf(b)
        for kc in range(8):
            tr(pb[:, kc * 128:(kc + 1) * 128], xs[:, kc * 128:(kc + 1) * 128], ident_b[:, :], [xsb, Bconst], [BKb[b]])
        for kc in range(8):
            ts("dve", hT_ap[:, kc, :], pb[:, kc * 128:(kc + 1) * 128], gmT[:, l, k, r, kc:kc + 1],
               modT[:, l, 3 * k * 8 + kc, r:r + 1], ALU.mult, ALU.add, [BKb[b], Bmod], [hTb])

    def post_update(tc, stage, s, t, yb, gc_t, gc_b):
        st, stb = stat.next()
        act(tc.junk[:, 0:512], BK[yb], AF.Square, [BKb[yb]], [tc.Bjunk, stb], accum=st[:, 0:1])
        act(tc.junk[:, 512:1024], BK[yb + 1], AF.Square, [BKb[yb + 1]], [tc.Bjunk, stb], accum=st[:, 1:2])
        tt("dve", st[:, 2:3], st[:, 0:1], st[:, 1:2], ALU.add, [stb], [stb])
        rstd_from(st[:, 2:3], st[:, 3:4], stb)
        sl, slb, sld = tc.slots.next()
        src, srcb = src_tile(stage, s, t)
        dma("pool", sl[:, :], src, srcb, [slb], sld)
        stt("dve", pair(yb), pair(yb), st[:, 3:4], gc_t[:, :], ALU.mult, ALU.mult, [BKb[yb], BKb[yb + 1], stb, gc_b], [BKb[yb], BKb[yb + 1]])
        tt("dve", sl[:, :], sl[:, :], pair(yb), ALU.add, [slb, BKb[yb], BKb[yb + 1]], [slb])
        dst, dstb = dst_tile(stage, s, t)
        dma("pool", dst, sl[:, :], [slb], dstb, sld)

    class DiagRot:
        def __init__(self):
            self.r = Rot([(sb.alloc([128, 128], F32), Buf("dg%d" % i)) for i in range(2)])

        def next(self):
            t_, b_ = self.r.next()
            return t_[:, :], b_

    def ffn_stage(stage, l, j, cast_bufs):
        k = 0 if j == 0 else 2
        m0 = sb.mark()
        tc = TileCtx(4)
        tc_cur[0] = DiagRot()
        wout = sb.alloc([128, NFC, D], BF16)
        hT_t = [sb.alloc([128, 8, 1024], BF16) for _ in range(2)]
        hTb = [[Buf("hT%d_%d" % (a, i)) for i in range(8)] for a in range(2)]
        actT = sb.alloc([128, NFC, 1024], BF16)
        actb = [Buf("act%d" % i) for i in range(NFC)]
        sg = Rot([(sb.alloc([128, 512], F32), Buf("sg%d" % i)) for i in range(2)])
        gu = Rot([(2, 3), (4, 5)])
        ybank = Rot([6, 0])
        Bwin, Bwout = cast_bufs
        wob = []
        for (k0, k1) in ((0, 6), (6, 12), (12, 17), (17, 22)):
            b_ = Buf("wout")
            dma("sp", wout[:, k0:k1, :], wout_s[l][j][:, k0:k1, :], [Bwout], [b_], new_dsem())
            wob.append((k0, k1, b_))

        def wob_of(fc):
            for (k0, k1, b_) in wob:
                if k0 <= fc < k1:
                    return b_
        sbs = []
        x_only = full and stage == 5
        if not x_only:
            ctx_tiles = [(s, t) for s in range(nseq) for t in range(2)]
            for i in range(0, len(ctx_tiles), 8):
                sbs.append(ctx_tiles[i:i + 8])
        for s in range(nseq):
            sbs.append([(s, t) for t in range(2, 10)])
            sbs.append([(s, t) for t in range(10, 18)])
        nsb = len(sbs)

        def do_prepA(bi, i):
            s, t = sbs[bi][i]
            return prepA(tc, stage, s, t)

        def do_prepB(bi, i, xs, xsb):
            s, t = sbs[bi][i]
            prepB(tc, l, k, role_of(s, t), xs, xsb, hT_t[bi % 2][:, :, i * 128:(i + 1) * 128], hTb[bi % 2][i])

        for i in range(len(sbs[0])):
            xs, xsb = do_prepA(0, i)
            do_prepB(0, i, xs, xsb)
        for bi in range(nsb):
            tiles = sbs[bi]
            ntl = len(tiles)
            ncols = ntl * 128
            hT = hT_t[bi % 2]
            hb = hTb[bi % 2]
            role = role_of(*tiles[0])
            gc_t, gc_b = make_gcbc(l, k, role)
            halves = [(c0, min(512, ncols - c0)) for c0 in range(0, ncols, 512)]
            pend = None
            for g in range(11):
                wt_, wb_, wd_ = wslot.next()
                dma("sp", wt_, win_s[l][j][g], [Bwin], [wb_], wd_)
                for c in range(2):
                    fc = 2 * g + c
                    for (c0, cn) in halves:
                        gb, ub = gu.next()
                        hbs = hb[c0 // 128:(c0 + cn) // 128]
                        for kc in range(8):
                            mm(BK[gb][:, 0:cn], wt_[:, kc, 0, c * 128:(c + 1) * 128], hT[:, kc, c0:c0 + cn], kc == 0, kc == 7, [wb_] + hbs, [BKb[gb]])
                        for kc in range(8):
                            mm(BK[ub][:, 0:cn], wt_[:, kc, 1, c * 128:(c + 1) * 128], hT[:, kc, c0:c0 + cn], kc == 0, kc == 7, [wb_] + hbs, [BKb[ub]])
                        sg_t, sg_b = sg.next()
                        act(sg_t[:, 0:cn], BK[gb][:, 0:cn], AF.Silu, [BKb[gb]], [sg_b])
                        tt("dve", actT[:, fc, c0:c0 + cn], sg_t[:, 0:cn], BK[ub][:, 0:cn], ALU.mult, [sg_b, BKb[ub]], [actb[fc]])
                if bi + 1 < nsb:
                    nn = len(sbs[bi + 1])
                    if pend is not None:
                        do_prepB(bi + 1, pend[0], pend[1], pend[2])
                        pend = None
                    if g < nn:
                        xs, xsb = do_prepA(bi + 1, g)
                        pend = (g, xs, xsb)
            if pend is not None:
                do_prepB(bi + 1, pend[0], pend[1], pend[2])
            for i, (s, t) in enumerate(tiles):
                yb = ybank.next()
                for fc in range(NFC):
                    for h in range(2):
                        mm(BK[yb + h], actT[:, fc, i * 128:(i + 1) * 128], wout[:, fc, h * 512:(h + 1) * 512],
                           fc == 0, fc == NFC - 1, [actb[fc], wob_of(fc)], [BKb[yb + h]])
                post_update(tc, stage, s, t, yb, gc_t, gc_b)
        P.barrier(flush=tc.flush())
        sb.reset(m0)

    def out_proj_tile(tc, stage, s, t, lhs_of_kc, lhs_bufs, wo, wo_bufs, gc_t, gc_b, yb):
        for kc in range(8):
            for h in range(2):
                mm(BK[yb + h], lhs_of_kc(kc), wo[:, kc, h * 512:(h + 1) * 512], kc == 0, kc == 7, lhs_bufs + wo_bufs, [BKb[yb + h]])
        post_update(tc, stage, s, t, yb, gc_t, gc_b)

    def load_wo(scr, castbuf):
        wo = wslots_t[:, 0:2].rearrange("p a k u c -> p (a k u c)").rearrange("p (k c) -> p k c", k=8)
        it0, it1 = wslot.items[0], wslot.items[1]
        o = P.op("sp", lambda e: e.dma_start(out=wo, in_=scr), R=[castbuf], W=[it0[1], it1[1]], dsem=it0[2])
        wslot.i = 2
        return wo, it0[1], it1[1]

    def rglru_stage(stage, l):
        m0 = sb.mark()
        tc = TileCtx(2, 1)
        tc_cur[0] = DiagRot()
        hT = sb.alloc([128, 8, NTOK], BF16)
        hTb = [Buf("hTf%d" % i) for i in range(NT)]
        prodT = sb.alloc([128, 8, NTOK], BF16)
        prb = [Buf("prod%d" % i) for i in range(8)]
        gw_t = sb.alloc([128, 1, 8, 256], BF16)
        gws = Rot([(gw_t[:, i], Buf("gw%d" % i), new_dsem()) for i in range(1)])
        XW = 2310
        xr = sb.alloc([128, 2, XW], F32)
        xrb = [Buf("xr0"), Buf("xr1")]
        rx = sb.alloc([128, 2, XW], F32)
        rxb_t = sb.alloc([128, 2, XW], BF16)
        rxB = [Buf("rx0"), Buf("rx1")]
        AB = Rot([(sb.alloc([128, 2, 512], F32), sb.alloc([128, 2, 512], F32), Buf("ab%d" % i)) for i in range(2)])
        q_t = sb.alloc([128, 2, 512], F32)
        q_b = Buf("q")
        SB_ = Rot([(sb.alloc([128, 2, 512], F32), Buf("sbk%d" % i)) for i in range(2)])
        carry = sb.alloc([128, 2], F32)
        Bcarry = Buf("carry")
        alias_slots = [(rx[:, 0, 0:1024], rxB[0], new_dsem()), (rx[:, 1, 0:1024], rxB[1], new_dsem())]
        tc.extra_flush = [rxB[0], rxB[1]]
        P.op("pool", lambda e: e.memset(xr[:, :, :], 0.0), W=xrb)
        ranges = [(0, 256, 2, 0)] + [(256 + 512 * i, 512, 261 + 512 * i, 259 + 512 * i) for i in range(4)]
        pb = Rot([0, 1])
        gb4 = Rot([(2, 3), (4, 5)])
        def prep_tile(s_, t):
            xs, xsb = prepA(tc, stage, s_, t)
            prepB(tc, l, 1, role_of(s_, t), xs, xsb, hT[:, :, t * 128:(t + 1) * 128], hTb[t])

        for s in range(nseq):
            if s == 0:
                for t in range(NT):
                    prep_tile(s, t)
            gc_c = make_gcbc(l, 1, nseq)
            gc_x = make_gcbc(l, 1, s)
            for n in range(4):
                wt_, wb_, wd_ = wslot.next()
                dma("sp", wt_, rgin_s[n], [Bc["rgin"]], [wb_], wd_)
                gt_, gbuf_, gd_ = gws.next()
                dma("sp", gt_, rggw_s[n].rearrange("p d g j k -> p (d g j) k"), [Bc["rggw"]], [gbuf_], gd_)
                for c in range(2):
                    gch = 2 * n + c
                    for (h0, cn, dc0, u0) in ranges:
                        hbs = hTb[h0 // 128:(h0 + cn) // 128]
                        b = pb.next()
                        for kc in range(8):
                            mm(BK[b][:, 0:cn], wt_[:, kc, 0, c * 128:(c + 1) * 128], hT[:, kc, h0:h0 + cn], kc == 0, kc == 7, [wb_] + hbs, [BKb[b]])
                        act(prodT[:, gch, h0:h0 + cn], BK[b][:, 0:cn], AF.Gelu, [BKb[b]], [prb[gch]])
                        b = pb.next()
                        for kc in range(8):
                            mm(BK[b][:, 0:cn], wt_[:, kc, 1, c * 128:(c + 1) * 128], hT[:, kc, h0:h0 + cn], kc == 0, kc == 7, [wb_] + hbs, [BKb[b]])
                        cp("dve", xr[:, c, dc0:dc0 + cn], BK[b][:, 0:cn], [BKb[b]], [xrb[c]])
                for c in range(2):
                    gch = 2 * n + c
                    NU = 2307
                    cw = [smallT[:, O_CW + kk * 8 + gch:O_CW + kk * 8 + gch + 1] for kk in range(4)]
                    cb = smallT[:, O_CB + gch:O_CB + gch + 1]
                    ts("pool", rx[:, c, 0:NU], xr[:, c, 0:NU], cw[0], cb, ALU.mult, ALU.add, [xrb[c], Bmod], [rxB[c]])
                    for kk in range(1, 4):
                        stt("dve", rx[:, c, 0:NU], xr[:, c, kk:kk + NU], cw[kk], rx[:, c, 0:NU], ALU.mult, ALU.add, [xrb[c], rxB[c], Bmod], [rxB[c]])
                    cp("pool", rxb_t[:, c, 0:NU], rx[:, c, 0:NU], [rxB[c]], [rxB[c]])
                for d_ in range(2):
                    order = ranges if d_ == 0 else [ranges[0]] + ranges[:0:-1]
                    for ri, (h0, cn, dc0, u0) in enumerate(order):
                        a_t, i_t, ab_b = AB.next()
                        for k2 in range(2):
                            gch = 2 * n + k2
                            rb, ib = ((2, 3), (4, 5))[k2]
                            for g_ in range(2):
                                bb = rb if g_ == 0 else ib
                                for jc in range(2):
                                    mm(BK[bb][:, 0:cn], gt_[:, (d_ * 2 + g_) * 2 + jc, k2 * 128:(k2 + 1) * 128], rxb_t[:, jc, u0:u0 + cn],
                                       jc == 0, jc == 1, [gbuf_, rxB[jc]], [BKb[bb]])
                            hb_ = [hgb[:, (d_ * 2 + g_) * 8 + gch:(d_ * 2 + g_) * 8 + gch + 1] for g_ in range(2)]
                            act(a_t[:, k2, 0:cn], BK[rb][:, 0:cn], AF.Tanh, [BKb[rb], Bmod], [ab_b], bias=hb_[0], scale=0.5)
                            act(i_t[:, k2, 0:cn], BK[ib][:, 0:cn], AF.Tanh, [BKb[ib], Bmod], [ab_b], bias=hb_[1], scale=0.5)
                            hc_ = hcoef[:, d_ * 8 + gch:d_ * 8 + gch + 1]
                            act(a_t[:, k2, 0:cn], a_t[:, k2, 0:cn], AF.Exp, [ab_b, Bmod], [ab_b], scale=hc_, bias=hc_)
                        act(q_t[:, :, 0:cn], a_t[:, :, 0:cn], AF.Square, [ab_b], [q_b])
                        act(q_t[:, :, 0:cn], q_t[:, :, 0:cn], AF.Sqrt, [q_b, Bmod], [q_b], scale=-0.25, bias=quartT[:, 0:1])
                        stt("dve", i_t[:, :, 0:cn], i_t[:, :, 0:cn], 1.0, rx[:, :, u0:u0 + cn], ALU.add, ALU.mult, [ab_b, rxB[0], rxB[1]], [ab_b])
                        tt("pool", i_t[:, :, 0:cn], i_t[:, :, 0:cn], q_t[:, :, 0:cn], ALU.mult, [ab_b, q_b], [ab_b])
                        if d_ == 0:
                            for k2 in range(2):
                                if ri == 0:
                                    init = 0.0
                                else:
                                    pu = 255 if ri == 1 else u0 - 1
                                    init = xr[:, k2, pu:pu + 1]
                                P.op("dve", lambda e, o_=xr[:, k2, u0:u0 + cn], a_=a_t[:, k2, 0:cn], b_=i_t[:, k2, 0:cn], in_=init:
                                     e.tensor_tensor_scan(out=o_, data0=a_, data1=b_, initial=in_, op0=ALU.mult, op1=ALU.add),
                                     R=[ab_b, xrb[k2]], W=[xrb[k2]])
                        else:
                            s_t, s_b = SB_.next()
                            for k2 in range(2):
                                init = 0.0 if ri == 0 else carry[:, k2:k2 + 1]
                                P.op("dve", lambda e, o_=s_t[:, k2, 0:cn][:, ::-1], a_=a_t[:, k2, 0:cn][:, ::-1], b_=i_t[:, k2, 0:cn][:, ::-1], in_=init:
                                     e.tensor_tensor_scan(out=o_, data0=a_, data1=b_, initial=in_, op0=ALU.mult, op1=ALU.add),
                                     R=[ab_b, Bcarry], W=[s_b])
                                cp("dve", carry[:, k2:k2 + 1], s_t[:, k2, 0:1], [s_b], [Bcarry])
                            tt("pool", s_t[:, :, 0:cn], s_t[:, :, 0:cn], xr[:, :, u0:u0 + cn], ALU.add, [s_b, xrb[0], xrb[1]], [s_b])
                            tt("dve", prodT[:, 2 * n:2 * n + 2, h0:h0 + cn], prodT[:, 2 * n:2 * n + 2, h0:h0 + cn], s_t[:, :, 0:cn], ALU.mult,
                               [prb[2 * n], prb[2 * n + 1], s_b], [prb[2 * n], prb[2 * n + 1]])
                for c in range(2):
                    for (p0, pn) in ((0, 2), (258, 3), (2309, 1)):
                        P.op("pool", lambda e, a_=xr[:, c, p0:p0 + pn]: e.memset(a_, 0.0), W=[xrb[c]])
            if s == 0:
                print("rglru stage sbuf end", sb.off, flush=True)
            wo, wb0, wb1 = load_wo(rgwo_s, Bc["rgwo"])
            yb = Rot([6, 2, 4])
            tc.slots = Rot(list(tc.base_slots.items) + alias_slots)
            for t in range(NT):
                g_ = gc_c if t < 2 else gc_x
                out_proj_tile(tc, stage, s, t, lambda kc, t=t: prodT[:, kc, t * 128:(t + 1) * 128], prb, wo, [wb0, wb1], g_[0], g_[1], yb.next())
                if s + 1 < nseq:
                    prep_tile(s + 1, t)
            tc.slots = tc.base_slots
        P.barrier(flush=tc.flush())
        sb.reset(m0)

    def attn_stage(stage, l):
        m0 = sb.mark()
        tc = TileCtx(2)
        tc_cur[0] = DiagRot()
        hT = sb.alloc([128, 8, NTOK], BF16)
        hTb = [Buf("hTf%d" % i) for i in range(NT)]
        QT = sb.alloc([128, 8, SEQ], BF16)
        Qb = [Buf("q%d" % i) for i in range(8)]
        alias_slots = [(QT[:, j, :].bitcast(F32), Qb[j], new_dsem()) for j in range(2)]
        tc.extra_flush = list(Qb)
        Kd = sb.alloc([128, 4, 2, NTOK], BF16)
        Kb = [Buf("k%d" % i) for i in range(4)]
        Va = sb.alloc([128, NT, 4, 66], BF16)
        Vb = [Buf("v%d" % i) for i in range(NT)]
        cosT = sb.alloc([128, SEQ], F32)
        sinT = sb.alloc([128, SEQ], F32)
        Brope = Buf("rope")
        PT = Rot([(sb.alloc([128, 5, 4, 128], BF16), Buf("pt%d" % i)) for i in range(2)])
        rt = Rot([(sb.alloc([128, 512], F32), Buf("rt%d" % i)) for i in range(2)])
        attn = Rot([(sb.alloc([128, D], BF16), Buf("at%d" % i)) for i in range(2)])
        den = Rot([(sb.alloc([128, 8], F32), Buf("den%d" % i)) for i in range(2)])
        dma("sp", cosT[:, :], ropec, [], [Brope], new_dsem())
        dma("sp", sinT[:, :], ropes, [], [Brope], new_dsem())
        P.op("pool", lambda e: e.memset(Va[:, :, :, 64:66], 1.0), W=Vb)
        P.op("pool", lambda e: e.memset(Kd[:, :, :, :], 0.0), W=Kb)
        pb2 = Rot([(0, 1), (2, 3)])
        pb1 = Rot([0, 1, 2, 3])
        def prep_tile(s_, t):
            xs, xsb = prepA(tc, stage, s_, t)
            prepB(tc, l, 1, role_of(s_, t), xs, xsb, hT[:, :, t * 128:(t + 1) * 128], hTb[t])

        for s in range(nseq):
            if s == 0:
                for t in range(NT):
                    prep_tile(s, t)
            gc_x = make_gcbc(l, 1, s)
            for g in range(6):
                wt_, wb_, wd_ = wslot.next()
                dma("sp", wt_, wqk_s[g], [Bc["wqk"]], [wb_], wd_)
                for c in range(2):
                    if g < 4:
                        dst_of = lambda x0, cn, j=2 * g + c: [(0, 128, QT[:, j, x0:x0 + cn])]
                        dbuf = Qb[2 * g + c]
                    else:
                        kv = 2 * (g - 4) + c
                        dst_of = lambda x0, cn, kv=kv: [(0, 64, Kd[0:64, kv, 0, 256 + x0:256 + x0 + cn]),
                                                        (64, 128, Kd[64:128, kv, 1, 256 + x0:256 + x0 + cn])]
                        dbuf = Kb[kv]
                        b = pb1.next()
                        for kc in range(8):
                            mm(BK[b][:, 0:256], wt_[:, kc, 0, c * 128:(c + 1) * 128], hT[:, kc, 0:256], kc == 0, kc == 7, [wb_] + hTb[0:2], [BKb[b]])
                        cp("dve", Kd[0:64, kv, 0, 0:256], BK[b][0:64, 0:256], [BKb[b]], [dbuf])
                        cp("dve", Kd[64:128, kv, 1, 0:256], BK[b][64:128, 0:256], [BKb[b]], [dbuf])
                    for xi in range(4):
                        x0 = xi * 512
                        h0 = 256 + x0
                        hbs = hTb[h0 // 128:(h0 + 512) // 128]
                        bp, bs_ = pb2.next()
                        for kc in range(8):
                            mm(BK[bp], wt_[:, kc, 0, c * 128:(c + 1) * 128], hT[:, kc, h0:h0 + 512], kc == 0, kc == 7, [wb_] + hbs, [BKb[bp]])
                        for kc in range(8):
                            mm(BK[bs_], wt_[:, kc, 1, c * 128:(c + 1) * 128], hT[:, kc, h0:h0 + 512], kc == 0, kc == 7, [wb_] + hbs, [BKb[bs_]])
                        t1, t1b = rt.next()
                        t2, t2b = rt.next()
                        tt("dve", t1[:, :], BK[bp], cosT[:, x0:x0 + 512], ALU.mult, [BKb[bp], Brope], [t1b])
                        tt("dve", t2[:, :], BK[bs_], sinT[:, x0:x0 + 512], ALU.mult, [BKb[bs_], Brope], [t2b])
                        for (p0, p1, dap) in dst_of(x0, 512):
                            tt("pool", dap, t1[p0:p1, :], t2[p0:p1, :], ALU.add, [t1b, t2b], [dbuf])
            wv_t, Bwv, wvd_ = wslot.next()
            dma("sp", wv_t[:, :, 0, :], wv_s, [Bc["wv"]], [Bwv], wvd_)
            for t in range(NT):
                b = pb1.next()
                for kc in range(8):
                    mm(BK[b][:, 0:256], hT[:, kc, t * 128:(t + 1) * 128], wv_t[:, kc, 0, :], kc == 0, kc == 7, [hTb[t], Bwv], [BKb[b]])
                act(Va[:, t, :, 0:64], BK[b][:, 0:256].rearrange("p (k d) -> p k d", k=4), AF.Copy, [BKb[b]], [Vb[t]])
            wo, wb0, wb1 = load_wo(awo_s, Bc["awo"])
            spair = Rot([0, 2])
            obank = Rot([4, 5])

            def chunks_of(qb):
                tq = 2 + qb
                ch = [(0, None), (1, None)]
                if qb > 0:
                    ch.append((tq - 1, mask_prev))
                ch.append((tq, None))
                if qb < 15:
                    ch.append((tq + 1, mask_next))
                return ch

            def emit_scores(qb, kv):
                chunks = chunks_of(qb)
                pt_t, pt_b = PT.next()
                ci = 0
                while ci < len(chunks):
                    n2 = min(2 if os.environ.get("ATT_EXP2", "1") == "1" else 1, len(chunks) - ci)
                    b0 = spair.next()
                    for cc in range(n2):
                        kt, msk = chunks[ci + cc]
                        first = True
                        if msk is not None:
                            mm(BK[b0 + cc][:, 0:512], ident_b[:, :], msk[:, :, :].rearrange("p g q -> p (g q)"), True, False, [Bconst], [BKb[b0 + cc]])
                            first = False
                        for gq in range(4):
                            h = 4 * kv + gq
                            mm(BK[b0 + cc][:, gq * 128:(gq + 1) * 128], Kd[:, kv, h % 2, kt * 128:(kt + 1) * 128],
                               QT[:, h // 2, qb * 128:(qb + 1) * 128], first, gq == 3, [Kb[kv], Qb[h // 2]], [BKb[b0 + cc]])
                            first = False
                    act(pt_t[:, ci:ci + n2].rearrange("p c g q -> p (c g q)"), ps[:, b0 * 512:(b0 + n2) * 512], AF.Exp,
                        [BKb[b0 + cc_] for cc_ in range(n2)], [pt_b], scale=0.125)
                    ci += n2
                return (qb, kv, chunks, pt_t, pt_b)

            cur_at = [None]
            tbk = Rot([6, 7])

            def emit_pv(item):
                qb, kv, chunks, pt_t, pt_b = item
                nch = len(chunks)
                tq = 2 + qb
                if kv == 0:
                    cur_at[0] = attn.next()
                at_t, at_b = cur_at[0]
                ob = obank.next()
                for gq in range(4):
                    for ci, (kt, msk) in enumerate(chunks):
                        mm(BK[ob][:, gq * 65:(gq + 1) * 65], pt_t[:, ci, gq, :], Va[:, kt, kv, 0:65], ci == 0, ci == nch - 1, [pt_b, Vb[kt]], [BKb[ob]])
                dn_t, dn_b = den.next()
                ov_ = BK[ob][:, 0:260].rearrange("p (g d) -> p g d", d=65)
                tt("dve", dn_t[:, 0:4], ov_[:, :, 64], esink[:, 4 * kv:4 * kv + 4], ALU.add, [BKb[ob], Bmod], [dn_b])
                P.op("dve", lambda e, a_=dn_t[:, 4:8], b_=dn_t[:, 0:4]: e.reciprocal(out=a_, in_=b_), R=[dn_b], W=[dn_b])
                if os.environ.get("ATT_BCAST", "1") == "1":
                    tt("dve", at_t[:, kv * 256:(kv + 1) * 256].rearrange("p (g d) -> p g d", d=64), ov_[:, :, 0:64],
                       dn_t[:, 4:8].unsqueeze(2).to_broadcast([128, 4, 64]), ALU.mult, [BKb[ob], dn_b], [at_b])
                else:
                    for gq in range(4):
                        h = 4 * kv + gq
                        ts("dve", at_t[:, h * 64:(h + 1) * 64], ov_[:, gq, 0:64], dn_t[:, 4 + gq:5 + gq], None, ALU.mult, None, [BKb[ob], dn_b], [at_b])
                if kv == 3:
                    ybk = tbk.next()
                    pbt = bank_bf(ybk)
                    for kc in range(8):
                        tr(pbt[:, kc * 128:(kc + 1) * 128], at_t[:, kc * 128:(kc + 1) * 128], ident_b[:, :], [at_b, Bconst], [BKb[ybk]])
                    cp("dve", hT[:, :, tq * 128:(tq + 1) * 128], pbt[:, 0:1024].rearrange("p (k q) -> p k q", k=8), [BKb[ybk]], [hTb[tq]])

            prev = None
            for qb in range(16):
                for kv in range(4):
                    cur = emit_scores(qb, kv)
                    if os.environ.get("ATT_PIPE", "1") != "1":
                        emit_pv(cur)
                        continue
                    if prev is not None:
                        emit_pv(prev)
                    prev = cur
            if prev is not None:
                emit_pv(prev)
            ybr = Rot([6, 2, 4])
            tc.slots = Rot(list(tc.base_slots.items) + alias_slots)
            if s + 1 < nseq:
                prep_tile(s + 1, 0)
                prep_tile(s + 1, 1)
            for tq in range(2, NT):
                out_proj_tile(tc, stage, s, tq, lambda kc, tq=tq: hT[:, kc, tq * 128:(tq + 1) * 128], [hTb[tq]], wo, [wb0, wb1], gc_x[0], gc_x[1], ybr.next())
                if s + 1 < nseq:
                    prep_tile(s + 1, tq)
            tc.slots = tc.base_slots
        print("attn stage sbuf end", sb.off, flush=True)
        P.barrier(flush=tc.flush())
        sb.reset(m0)

    if nstages > 0:
        ffn_stage(0, 0, 0, Bc["ffn00"])
    if nstages > 1:
        rglru_stage(1, 0)
    if nstages > 2:
        ffn_stage(2, 0, 1, Bc["ffn01"])
    if nstages > 3:
        ffn_stage(3, 1, 0, Bc["ffn10"])
    if nstages > 4:
        attn_stage(4, 1)
    if nstages > 5:
        ffn_stage(5, 1, 1, Bc["ffn11"])

    if not full:
        dcp = new_dsem()
        for s in range(nseq):
            dma("sp", outp[s], resD[s], [Bres[(s, t)] for t in range(NT)], [], dcp)
    def final(e):
        for d in all_ds:
            if d.count:
                e.wait_ge(d.h, d.count)
        return e.nop()
    P.op("sp", final)

    run = P.emit(esems)
    with nc.Block() as block:
        @block.tensor
        def _(e):
            run("pe", e)

        @block.scalar
        def _(e):
            run("act", e)

        @block.vector
        def _(e):
            run("dve", e)

        @block.gpsimd
        def _(e):
            run("pool", e)

        @block.sync
        def _(e):
            run("sp", e)
    print("ops:", {k: len(v) for k, v in P.q.items()}, "marked:", {k: sum(1 for o in v if o.marked) for k, v in P.q.items()},
          "waits:", {k: sum(len(o.waits) for o in v) for k, v in P.q.items()}, "sbuf peak", sb.peak, "dsems", nds[0], flush=True)
    return nc


def rope_tables():
    inv = 1.0 / (10000.0 ** (np.arange(0, 32, 2, dtype=np.float32) / 32.0))
    pos = np.arange(SEQ)
    row = (pos // 64).astype(np.float32)
    col = (pos % 64).astype(np.float32)
    cosT = np.zeros((128, SEQ), np.float32)
    sinT = np.zeros((128, SEQ), np.float32)
    for p in range(128):
        d = p % 64
        base = row if d < 32 else col
        ang = (base * inv[d % 16]).astype(np.float32)
        cosT[p] = np.cos(ang)
        sg = -1.0 if (d % 32) < 16 else 1.0
        sinT[p] = sg * np.sin(ang)
    return cosT, sinT


def host_layout(inputs, nseq, core):
    f = lambda a: np.ascontiguousarray(np.asarray(a, dtype=np.float32))
    x = f(inputs["x"])
    ctx = f(inputs["ctx"])
    c = f(inputs["c"])
    b0 = core * nseq
    xin = np.concatenate([ctx[b0:b0 + nseq], x[b0:b0 + nseq]], axis=1)
    cond = np.concatenate([c[b0:b0 + nseq], f(inputs["c_ctx"])[None]], axis=0).reshape(-1, 128)
    small = np.concatenate([
        f(inputs["b_mod"]).reshape(-1, 128), f(inputs["norm_g"]).reshape(-1, 128),
        f(inputs["rg_conv_w"]).reshape(-1, 128), f(inputs["rg_conv_b"]).reshape(-1, 128),
        f(inputs["rg_gate_b"]).reshape(-1, 128), f(inputs["rg_lambda"]).reshape(-1, 128)], axis=0)
    assert small.shape[0] == NV
    wqkv = f(inputs["attn_w_qkv"])[0]
    wq = wqkv[:, :1024]
    wk = wqkv[:, 1024:1280]
    kdup = np.concatenate([np.concatenate([wk[:, kv * 64:(kv + 1) * 64]] * 2, axis=1) for kv in range(4)], axis=1)
    plain = np.concatenate([wq, kdup], axis=1)
    dd = np.arange(64)
    part = np.where((dd % 32) < 16, dd + 16, dd - 16)
    perm = (np.arange(1536) // 64) * 64 + part[np.arange(1536) % 64]
    sw = plain[:, perm]
    wqk2 = np.ascontiguousarray(np.stack([plain, sw], axis=1))
    cosT, sinT = rope_tables()
    return {
        "xin": np.ascontiguousarray(xin), "cond": np.ascontiguousarray(cond), "small": small,
        "sink": f(inputs["attn_sink"]).reshape(1, 16), "ropec": cosT, "ropes": sinT,
        "w_mod": f(inputs["w_mod"]), "ffn_w_in": f(inputs["ffn_w_in"]), "ffn_w_out": f(inputs["ffn_w_out"]),
        "rg_w_in": f(inputs["rg_w_in"])[0], "rg_gate_w": f(inputs["rg_gate_w"])[0], "rg_w_out": f(inputs["rg_w_out"])[0],
        "wqk2": wqk2, "wv": np.ascontiguousarray(wqkv[:, 1280:1536]), "attn_w_o": f(inputs["attn_w_o"])[0],
    }


def kernel(**inputs):
    nseq = 32 // NCORES
    nc = build_program(nseq, 6)
    in_maps = [host_layout(inputs, nseq, core) for core in range(NCORES)]
    res = run_bass_kernel_spmd(nc, in_maps, core_ids=list(range(NCORES)))
    out = np.concatenate([np.asarray(r["out"]) for r in res.results], axis=0)
    return out.astype(np.float32)
```

```python
import os
import numpy as np
import concourse.bass as bass
import concourse.mybir as mybir
from concourse.bass_utils import run_bass_kernel_spmd

F32 = mybir.dt.float32
BF16 = mybir.dt.bfloat16
AF = mybir.ActivationFunctionType
ALU = mybir.AluOpType

D = 1024
SEQ = 2048
CTX = 256
NTOK = SEQ + CTX
NT = NTOK // 128
DFF = 2816
NFC = DFF // 128
NCORES = 8
EPS = 1e-6
NV = 328
O_BMOD, O_G, O_CW, O_CB, O_GB, O_LAM = 0, 144, 240, 272, 280, 312


class Buf:
    __slots__ = ("name", "lw", "rd")

    def __init__(self, name=""):
        self.name = name
        self.lw = None
        self.rd = []


class DSem:
    __slots__ = ("h", "count", "key")

    def __init__(self, h, key):
        self.h = h
        self.count = 0
        self.key = key


class Op:
    __slots__ = ("eng", "idx", "fn", "waits", "marked", "clock", "dsem", "dval", "semval")

    def __init__(self, eng, idx, fn):
        self.eng = eng
        self.idx = idx
        self.fn = fn
        self.waits = []
        self.marked = False
        self.clock = None
        self.dsem = None
        self.dval = 0
        self.semval = 0


class Prog:
    ENG = ("pe", "act", "dve", "pool", "sp")

    def __init__(self, nc):
        self.nc = nc
        self.q = {k: [] for k in self.ENG}
        self.clock = {k: {} for k in self.ENG}

    @staticmethod
    def _kv(op):
        if op.dsem is not None:
            return op.dsem.key, op.dval
        return op.eng, op.idx + 1

    def op(self, eng, fn, R=(), W=(), dsem=None, extra=()):
        q = self.q[eng]
        o = Op(eng, len(q), fn)
        is_dma = dsem is not None
        if is_dma:
            dsem.count += 16
            o.dsem = dsem
            o.dval = dsem.count
        deps = []
        for b in R:
            if b.lw is not None:
                deps.append((b.lw, 0))
        for b in W:
            if b.lw is not None:
                deps.append((b.lw, 1))
            for r in b.rd:
                deps.append((r, 2))
        for d in extra:
            deps.append((d, 0))
        clk = self.clock[eng]
        best = {}
        for (d, raw) in deps:
            if d is o:
                continue
            d_dma = d.dsem is not None
            if (not d_dma) and d.eng == eng and not is_dma:
                if eng == "pe" or (eng != "pool" and raw == 2):
                    continue
            k, v = self._kv(d)
            if clk.get(k, 0) >= v:
                continue
            cur = best.get(k)
            if cur is None or cur[0] < v:
                best[k] = (v, d)
        for k, (v, d) in best.items():
            if clk.get(k, 0) >= v:
                continue
            o.waits.append(d)
            d.marked = True
            for kk, vv in d.clock.items():
                if clk.get(kk, 0) < vv:
                    clk[kk] = vv
        c = dict(clk)
        k, v = self._kv(o)
        c[k] = v
        o.clock = c
        for b in R:
            b.rd.append(o)
        for b in W:
            b.lw = o
            b.rd = []
        q.append(o)
        return o

    def barrier(self, flush=()):
        lasts = []
        for e in self.ENG:
            q = self.q[e]
            for o in reversed(q):
                if o.dsem is None:
                    lasts.append(o)
                    break
        fl = list(flush)
        for e in self.ENG:
            self.op(e, lambda en: en.nop(), W=fl if e == "sp" else (), extra=[o for o in lasts if o.eng != e])
        spl = self.q["sp"][-1]
        for e in self.ENG:
            if e != "sp":
                self.op(e, lambda en: en.nop(), extra=[spl])

    def emit(self, sems):
        for e, q in self.q.items():
            n = 0
            for o in q:
                if o.dsem is None and o.marked:
                    n += 1
                    o.semval = n

        def run(ename, engine):
            for o in self.q[ename]:
                for d in o.waits:
                    if d.dsem is not None:
                        engine.wait_ge(d.dsem.h, d.dval)
                    else:
                        engine.wait_ge(sems[d.eng], d.semval)
                ins = o.fn(engine)
                if o.dsem is not None:
                    ins.then_inc(o.dsem.h, 16)
                elif o.marked:
                    ins.then_inc(sems[ename], 1)
        return run


class Rot:
    def __init__(self, items):
        self.items = items
        self.i = 0

    def next(self):
        it = self.items[self.i % len(self.items)]
        self.i += 1
        return it


class SBAlloc:
    LO = 16640
    HI = 229376

    def __init__(self, nc):
        self.nc = nc
        self.off = self.LO
        self.n = 0
        self.peak = 0

    def alloc(self, shape, dt):
        esz = 4 if dt == F32 else 2
        nb = esz
        for s in shape[1:]:
            nb *= s
        off = (self.off + 63) // 64 * 64
        assert off + nb <= self.HI, "SBUF overflow: need %d at %d" % (nb, off)
        self.off = off + nb
        self.peak = max(self.peak, self.off)
        self.n += 1
        return self.nc.alloc_sbuf_tensor_at("t%d" % self.n, list(shape), dt, offset=off)

    def mark(self):
        return self.off

    def reset(self, m):
        self.off = m


def build_program(nseq, nstages=6):
    nc = bass.Bass("TRN2", target_bir_lowering=False)
    R = nseq + 1
    full = nstages >= 6

    def din(name, shape, dt=F32):
        return nc.dram_tensor(name, list(shape), dt, kind="ExternalInput").ap()

    def dscr(name, shape, dt=BF16):
        return nc.dram_tensor(name, list(shape), dt, kind="Internal").ap()

    xin = din("xin", [nseq, NTOK, D])
    cond = din("cond", [R * 8, 128])
    small = din("small", [NV, 128])
    sink = din("sink", [1, 16])
    ropec = din("ropec", [128, SEQ])
    ropes = din("ropes", [128, SEQ])
    w_mod = din("w_mod", [2, D, 9 * D])
    ffn_w_in = din("ffn_w_in", [2, 2, D, 2 * DFF])
    ffn_w_out = din("ffn_w_out", [2, 2, DFF, D])
    rg_w_in = din("rg_w_in", [D, 2 * D])
    rg_gate_w = din("rg_gate_w", [2, 2, 4, 256, 256])
    rg_w_out = din("rg_w_out", [D, D])
    wqk2 = din("wqk2", [D, 2, 1536])
    wv = din("wv", [D, 256])
    attn_w_o = din("attn_w_o", [D, D])
    if full:
        outp = nc.dram_tensor("out", [nseq, SEQ, D], F32, kind="ExternalOutput").ap()
    else:
        outp = nc.dram_tensor("dbg", [nseq, NTOK, D], F32, kind="ExternalOutput").ap()
    resD = dscr("resD", [nseq, NTOK, D], F32)
    win_s = [[dscr("win_s%d%d" % (l, j), [11, 128, 8, 2, 256]) for j in range(2)] for l in range(2)]
    wout_s = [[dscr("wout_s%d%d" % (l, j), [128, NFC, D]) for j in range(2)] for l in range(2)]
    rgin_s = dscr("rgin_s", [4, 128, 8, 2, 256])
    rggw_s = dscr("rggw_s", [4, 128, 2, 2, 2, 256])
    rgwo_s = dscr("rgwo_s", [128, 8, D])
    wqk_s = dscr("wqk_s", [6, 128, 8, 2, 256])
    wv_s = dscr("wv_s", [128, 8, 256])
    awo_s = dscr("awo_s", [128, 8, D])

    P = Prog(nc)
    sb = SBAlloc(nc)
    sem_handles = []

    def new_sem(name):
        h = nc.alloc_semaphore(name)
        sem_handles.append(h)
        return h

    nds = [0]

    all_ds = []

    def new_dsem():
        nds[0] += 1
        d = DSem(new_sem("d%d" % nds[0]), "d%d" % nds[0])
        all_ds.append(d)
        return d

    esems = {k: new_sem("s_" + k) for k in Prog.ENG}

    ps = nc.alloc_psum_tensor("ps", [128, 4096], F32)
    BK = [ps[:, b * 512:(b + 1) * 512] for b in range(8)]
    BKb = [Buf("bank%d" % b) for b in range(8)]

    def bank_bf(b):
        return BK[b].bitcast(BF16)

    def pair(b):
        return ps[:, b * 512:(b + 2) * 512]

    ident_b = sb.alloc([128, 128], BF16)
    ident_f = sb.alloc([128, 128], F32)
    ones_f = sb.alloc([128, 128], F32)
    mask_prev = sb.alloc([128, 4, 128], BF16)
    mask_next = sb.alloc([128, 4, 128], BF16)
    smallT = sb.alloc([128, NV], F32)
    sT = sb.alloc([128, R * 8], F32)
    modT = sb.alloc([128, 2, 72, R], F32)
    gmT = sb.alloc([128, 2, 3, R, 8], F32)
    gcT = sb.alloc([128, 2, 3, R, 8], F32)
    coef = sb.alloc([128, 16], F32)
    esink = sb.alloc([128, 16], F32)
    hgb = sb.alloc([128, 32], F32)
    hcoef = sb.alloc([128, 16], F32)
    quartT = sb.alloc([128, 1], F32)
    epsT = sb.alloc([128, 1], F32)
    oneT = sb.alloc([128, 1], F32)
    wslots_t = sb.alloc([128, 3, 8, 2, 256], BF16)
    wslot = Rot([(wslots_t[:, i], Buf("ws%d" % i), new_dsem()) for i in range(3)])
    ws_bufs = [it[1] for it in wslot.items]
    gcbc = Rot([(sb.alloc([128, D], F32), Buf("gcbc%d" % i)) for i in range(2)])
    stat_t = sb.alloc([128, 16, 4], F32)
    stat = Rot([(stat_t[:, i, :], Buf("st%d" % i)) for i in range(16)])
    Bconst = Buf("const")
    Bmod = Buf("mod")
    persist_mark = sb.mark()

    Bres = {(s, t): Buf("res%d_%d" % (s, t)) for s in range(nseq) for t in range(NT)}

    def mm(out, lhsT, rhs, start, stop, R_, W_):
        return P.op("pe", lambda e: e.matmul(out, lhsT=lhsT, rhs=rhs, start=start, stop=stop), R=R_, W=W_)

    def tr(out, in_, ident, R_, W_):
        return P.op("pe", lambda e: e.transpose(out=out, in_=in_, identity=ident), R=R_, W=W_)

    def act(out, in_, func, R_, W_, bias=None, scale=None, accum=None):
        kw = {}
        if bias is not None:
            kw["bias"] = bias
        if scale is not None:
            kw["scale"] = scale
        if accum is not None:
            kw["accum_out"] = accum
        return P.op("act", lambda e: e.activation(out=out, in_=in_, func=func, **kw), R=R_, W=W_)

    def ts(eng, out, in0, s1, s2, op0, op1, R_, W_):
        if s2 is None:
            return P.op(eng, lambda e: e.tensor_scalar(out=out, in0=in0, scalar1=s1, scalar2=None, op0=op0), R=R_, W=W_)
        return P.op(eng, lambda e: e.tensor_scalar(out=out, in0=in0, scalar1=s1, scalar2=s2, op0=op0, op1=op1), R=R_, W=W_)

    def tt(eng, out, in0, in1, op, R_, W_):
        return P.op(eng, lambda e: e.tensor_tensor(out=out, in0=in0, in1=in1, op=op), R=R_, W=W_)

    def stt(eng, out, in0, s, in1, op0, op1, R_, W_):
        return P.op(eng, lambda e: e.scalar_tensor_tensor(out=out, in0=in0, scalar=s, in1=in1, op0=op0, op1=op1), R=R_, W=W_)

    def cp(eng, out, in_, R_, W_):
        return P.op(eng, lambda e: e.tensor_copy(out=out, in_=in_), R=R_, W=W_)

    def dma(q, out, in_, R_, W_, dsem):
        return P.op(q, lambda e: e.dma_start(out=out, in_=in_), R=R_, W=W_, dsem=dsem)

    def rstd_from(ssq_ap, out_ap, stbuf):
        ts("dve", out_ap, ssq_ap, 1.0 / D, EPS, ALU.mult, ALU.add, [stbuf], [stbuf])
        act(out_ap, out_ap, AF.Sqrt, [stbuf], [stbuf])
        P.op("dve", lambda e: e.reciprocal(out=out_ap, in_=out_ap), R=[stbuf], W=[stbuf])

    ov = sb.mark()
    P.op("pool", lambda e: e.memset(ident_f[:], 1.0), W=[Bconst])
    P.op("pool", lambda e: e.affine_select(out=ident_f[:], in_=ident_f[:], pattern=[[-1, 128]], compare_op=ALU.is_equal,
                                           fill=0.0, base=0, channel_multiplier=1), R=[Bconst], W=[Bconst])
    P.op("pool", lambda e: e.memset(ones_f[:], 1.0), W=[Bconst])
    P.op("pool", lambda e: e.memset(epsT[:], EPS), W=[Bconst])
    P.op("pool", lambda e: e.memset(oneT[:], 1.0), W=[Bconst])
    cp("dve", ident_b[:], ident_f[:], [Bconst], [Bconst])
    P.op("pool", lambda e: e.memset(mask_prev[:], -30000.0), W=[Bconst])
    P.op("pool", lambda e: e.memset(mask_next[:], -30000.0), W=[Bconst])
    P.op("pool", lambda e: e.affine_select(out=mask_prev[:], in_=mask_prev[:], pattern=[[0, 4], [1, 128]], compare_op=ALU.is_gt,
                                           fill=0.0, base=0, channel_multiplier=-1), R=[Bconst], W=[Bconst])
    P.op("pool", lambda e: e.affine_select(out=mask_next[:], in_=mask_next[:], pattern=[[0, 4], [-1, 128]], compare_op=ALU.is_gt,
                                           fill=0.0, base=0, channel_multiplier=1), R=[Bconst], W=[Bconst])
    sm_in = sb.alloc([128, 3, 128], F32)
    Bsm = Buf("sm")
    ds_sm = new_dsem()
    dma("sp", sm_in[:, 0, :], small[0:128, :], [], [], ds_sm)
    dma("sp", sm_in[:, 1, :], small[128:256, :], [], [], ds_sm)
    dma("sp", sm_in[0:72, 2, :], small[256:328, :], [], [Bsm], ds_sm)
    for i, n in enumerate((128, 128, 72)):
        tr(BK[0][:, i * 128:i * 128 + n], sm_in[0:n, i, :], ident_f[0:n, 0:n], [Bsm, Bconst], [BKb[0]])
    cp("dve", smallT[:, :], BK[0][:, 0:NV], [BKb[0]], [Bmod])
    cd_in = sb.alloc([128, 128], F32)
    ds_cd = new_dsem()
    Bcd = Buf("cd")
    dma("sp", cd_in[0:R * 8, :], cond, [], [Bcd], ds_cd)
    act(cd_in[0:R * 8, :], cd_in[0:R * 8, :], AF.Silu, [Bcd], [Bcd])
    tr(BK[1][:, 0:R * 8], cd_in[0:R * 8, :], ident_f[0:R * 8, 0:R * 8], [Bcd, Bconst], [BKb[1]])
    cp("dve", sT[:, :], BK[1][:, 0:R * 8], [BKb[1]], [Bmod])
    wm_t = sb.alloc([128, 2, 8, 512], F32)
    wms = Rot([(wm_t[:, i], Buf("wm%d" % i), new_dsem()) for i in range(2)])
    sT3 = sT[:, :].rearrange("p (r k) -> p k r", k=8)
    for l in range(2):
        bank = 2 + l
        for g in range(18):
            wt_, wb_, wd_ = wms.next()
            dma("sp", wt_, w_mod[l, :, g * 512:(g + 1) * 512].rearrange("(kc p) c -> p kc c", p=128), [], [wb_], wd_)
            for c4 in range(4):
                j = g * 4 + c4
                for kc in range(8):
                    mm(BK[bank][:, j * R:(j + 1) * R], wt_[:, kc, c4 * 128:(c4 + 1) * 128], sT3[:, kc, :],
                       kc == 0, kc == 7, [wb_, Bmod], [BKb[bank]])
        for r in range(R):
            tt("dve", modT[:, l, :, r], BK[bank][:, 0:72 * R].rearrange("p (j r) -> p j r", r=R)[:, :, r],
               smallT[:, O_BMOD + l * 72:O_BMOD + (l + 1) * 72], ALU.add, [BKb[bank], Bmod], [Bmod])
    for l in range(2):
        for k in range(3):
            wt_k = 1.0 if k == 1 else 0.5
            gpre = smallT[:, O_G + (l * 6 + k) * 8:O_G + (l * 6 + k) * 8 + 8]
            gpost = smallT[:, O_G + (l * 6 + 3 + k) * 8:O_G + (l * 6 + 3 + k) * 8 + 8]
            for r in range(R):
                stt("dve", gmT[:, l, k, r, :], modT[:, l, (3 * k + 1) * 8:(3 * k + 2) * 8, r], 1.0, gpre, ALU.add, ALU.mult, [Bmod], [Bmod])
                stt("dve", gcT[:, l, k, r, :], modT[:, l, (3 * k + 2) * 8:(3 * k + 3) * 8, r], wt_k, gpost, ALU.mult, ALU.mult, [Bmod], [Bmod])
    act(coef[:, :], smallT[:, O_LAM:O_LAM + 16], AF.Exp, [Bmod], [Bmod], scale=-1.0)
    act(coef[:, :], coef[:, :], AF.Ln, [Bmod], [Bmod], bias=oneT[:, 0:1])
    ts("dve", coef[:, :], coef[:, :], -8.0, None, ALU.mult, None, [Bmod], [Bmod])
    ts("dve", hcoef[:, :], coef[:, :], 0.5, None, ALU.mult, None, [Bmod], [Bmod])
    ts("dve", hgb[:, :], smallT[:, O_GB:O_GB + 32], 0.5, None, ALU.mult, None, [Bmod], [Bmod])
    P.op("pool", lambda e: e.memset(quartT[:], 0.25), W=[Bmod])
    ds_sk = new_dsem()
    dma("sp", esink[:, :], sink.partition_broadcast(128), [], [Bmod], ds_sk)
    act(esink[:, :], esink[:, :], AF.Exp, [Bmod], [Bmod])

    def cast_pairs(dst, src3, ngroups):
        b = Buf("cast")
        d = new_dsem()
        n = ngroups * 2
        i = 0
        for g in range(ngroups):
            for u in range(2):
                i += 1
                dma("pool", dst[g, :, :, u, :], src3[:, u, g * 256:(g + 1) * 256].rearrange("(kc p) c -> p kc c", p=128),
                    [], [b] if i == n else [], d)
        return b

    def cast_rows(dst, src, kcn, step=4):
        b = Buf("cast")
        d = new_dsem()
        v = src.rearrange("(kc p) c -> p kc c", p=128)
        k0 = 0
        while k0 < kcn:
            k1 = min(kcn, k0 + step)
            dma("pool", dst[:, k0:k1, :], v[:, k0:k1, :], [], [b] if k1 == kcn else [], d)
            k0 = k1
        return b

    def cast_ffn(l, j):
        a = cast_pairs(win_s[l][j], ffn_w_in[l, j].rearrange("k (u f) -> k u f", u=2), 11)
        b = cast_rows(wout_s[l][j], ffn_w_out[l, j], NFC)
        return a, b

    Bc = {}
    Bc["ffn00"] = cast_ffn(0, 0)
    if nstages > 1:
        Bc["rgin"] = cast_pairs(rgin_s, rg_w_in.rearrange("k (u f) -> k u f", u=2), 4)
        bgw = Buf("cast")
        dgw = new_dsem()
        i = 0
        for n in range(4):
            for d_ in range(2):
                for g_ in range(2):
                    i += 1
                    dma("pool", rggw_s[n, :, d_, g_, :, :], rg_gate_w[d_, g_, n].rearrange("(jc p) k -> p jc k", p=128),
                        [], [bgw] if i == 16 else [], dgw)
        Bc["rggw"] = bgw
        Bc["rgwo"] = cast_rows(rgwo_s, rg_w_out, 8)
    if nstages > 2:
        Bc["ffn01"] = cast_ffn(0, 1)
    if nstages > 3:
        Bc["ffn10"] = cast_ffn(1, 0)
    if nstages > 4:
        Bc["wqk"] = cast_pairs(wqk_s, wqk2, 6)
        Bc["wv"] = cast_rows(wv_s, wv, 8, step=8)
        Bc["awo"] = cast_rows(awo_s, attn_w_o, 8)
    if nstages > 5:
        Bc["ffn11"] = cast_ffn(1, 1)

    P.barrier(flush=[Bsm, Bcd] + [it[1] for it in wms.items])
    sb.reset(ov)

    def role_of(s, t):
        return nseq if t < 2 else s

    def src_tile(stage, s, t):
        if stage == 0:
            return xin[s, t * 128:(t + 1) * 128, :], []
        return resD[s, t * 128:(t + 1) * 128, :], [Bres[(s, t)]]

    def dst_tile(stage, s, t):
        if full and stage == 5:
            return outp[s, (t - 2) * 128:(t - 1) * 128, :], []
        return resD[s, t * 128:(t + 1) * 128, :], [Bres[(s, t)]]

    class TileCtx:
        def __init__(self, nslots, nxs=2):
            self.slots = Rot([(sb.alloc([128, D], F32), Buf("slot%d" % i), new_dsem()) for i in range(nslots)])
            self.xs = Rot([(sb.alloc([128, D], BF16), Buf("xs%d" % i)) for i in range(nxs)])
            self.junks = Rot([(sb.alloc([128, D], BF16), Buf("junk%d" % i)) for i in range(2)])
            self.tbank = Rot([0, 1])

            self.base_slots = self.slots
            self.extra_flush = []

        def flush(self):
            return [it[1] for it in self.base_slots.items] + list(self.extra_flush)

    def make_gcbc(l, k, r):
        t_, b_ = gcbc.next()
        dg = tc_cur[0]
        for half in range(2):
            bank = 6 + half
            for q4 in range(4):
                kc = half * 4 + q4
                dt_, db_ = dg.next()
                ts("dve", dt_, ident_f[:, :], gcT[:, l, k, r, kc:kc + 1], None, ALU.mult, None, [Bconst, Bmod], [db_])
                mm(BK[bank][:, q4 * 128:(q4 + 1) * 128], ones_f[:, :], dt_, True, True, [db_, Bconst], [BKb[bank]])
        cp("dve", t_[:, :], pair(6), [BKb[6], BKb[7]], [b_])
        return t_, b_

    tc_cur = [None]

    def prepA(tc, stage, s, t):
        sl, slb, sld = tc.slots.next()
        src, srcb = src_tile(stage, s, t)
        dma("pool", sl[:, :], src, srcb, [slb], sld)
        st, stb = stat.next()
        jk, jkb = tc.junks.next()
        act(jk[:, :], sl[:, :], AF.Square, [slb], [jkb, stb], accum=st[:, 0:1])
        rstd_from(st[:, 0:1], st[:, 1:2], stb)
        xs, xsb = tc.xs.next()
        ts("dve", xs[:, :], sl[:, :], st[:, 1:2], None, ALU.mult, None, [slb, stb], [xsb])
        return xs, xsb

    def prepB(tc, l, k, r, xs, xsb, hT_ap, hTb):
        b = tc.tbank.next()
        pb = bank_bf(b)
        for kc in range(8):
            tr(pb[:, kc * 128:(kc + 1) * 128], xs[:, kc * 128:(kc + 1) * 128], ident_b[:, :], [xsb, Bconst], [BKb[b]])
        for kc in range(8):
            ts("dve", hT_ap[:, kc, :], pb[:, kc * 128:(kc + 1) * 128], gmT[:, l, k, r, kc:kc + 1],
               modT[:, l, 3 * k * 8 + kc, r:r + 1], ALU.mult, ALU.add, [BKb[b], Bmod], [hTb])

    def post_update(tc, stage, s, t, yb, gc_t, gc_b):
        st, stb = stat.next()
        jk, jkb = tc.junks.next()
        act(jk[:, 0:512], BK[yb], AF.Square, [BKb[yb]], [jkb, stb], accum=st[:, 0:1])
        act(jk[:, 512:1024], BK[yb + 1], AF.Square, [BKb[yb + 1]], [jkb, stb], accum=st[:, 1:2])
        tt("dve", st[:, 2:3], st[:, 0:1], st[:, 1:2], ALU.add, [stb], [stb])
        rstd_from(st[:, 2:3], st[:, 3:4], stb)
        sl, slb, sld = tc.slots.next()
        src, srcb = src_tile(stage, s, t)
        dma("pool", sl[:, :], src, srcb, [slb], sld)
        stt("dve", pair(yb), pair(yb), st[:, 3:4], gc_t[:, :], ALU.mult, ALU.mult, [BKb[yb], BKb[yb + 1], stb, gc_b], [BKb[yb], BKb[yb + 1]])
        tt("dve", sl[:, :], sl[:, :], pair(yb), ALU.add, [slb, BKb[yb], BKb[yb + 1]], [slb])
        dst, dstb = dst_tile(stage, s, t)
        dma("pool", dst, sl[:, :], [slb], dstb, sld)

    def prep_group(tc, stage, l, k, tiles, hT_full, hTbufs):
        G = len(tiles)
        st, stb = stat.next()
        sls = []
        for i, (s_, t) in enumerate(tiles):
            sl, slb, sld = tc.slots.next()
            src, srcb = src_tile(stage, s_, t)
            dma("pool", sl[:, :], src, srcb, [slb], sld)
            sls.append((sl, slb))
        for i, (sl, slb) in enumerate(sls):
            jk, jkb = tc.junks.next()
            act(jk[:, :], sl[:, :], AF.Square, [slb], [jkb, stb], accum=st[:, i:i + 1])
        rstd_from(st[:, 0:G], st[:, 2:2 + G], stb)
        for i, (s_, t) in enumerate(tiles):
            sl, slb = sls[i]
            xs, xsb = tc.xs.next()
            ts("dve", xs[:, :], sl[:, :], st[:, 2 + i:3 + i], None, ALU.mult, None, [slb, stb], [xsb])
            prepB(tc, l, k, role_of(s_, t), xs, xsb, hT_full[:, :, t * 128:(t + 1) * 128], hTbufs[t])

    def post_group(tc, stage, items):
        G = len(items)
        st, stb = stat.next()
        st2, stb2 = stat.next()
        for i, (s_, t, yb, gc_t, gc_b) in enumerate(items):
            jk, jkb = tc.junks.next()
            act(jk[:, 0:512], BK[yb], AF.Square, [BKb[yb]], [jkb, stb], accum=st[:, 2 * i:2 * i + 1])
            act(jk[:, 512:1024], BK[yb + 1], AF.Square, [BKb[yb + 1]], [jkb, stb], accum=st[:, 2 * i + 1:2 * i + 2])
        stv = st[:, 0:2 * G].rearrange("p (g two) -> p g two", two=2)
        tt("dve", st2[:, 0:G], stv[:, :, 0], stv[:, :, 1], ALU.add, [stb], [stb2])
        rstd_from(st2[:, 0:G], st2[:, 2:2 + G], stb2)
        sls = []
        for i, (s_, t, yb, gc_t, gc_b) in enumerate(items):
            sl, slb, sld = tc.slots.next()
            src, srcb = src_tile(stage, s_, t)
            dma("pool", sl[:, :], src, srcb, [slb], sld)
            sls.append((sl, slb, sld))
        for i, (s_, t, yb, gc_t, gc_b) in enumerate(items):
            sl, slb, sld = sls[i]
            stt("dve", pair(yb), pair(yb), st2[:, 2 + i:3 + i], gc_t[:, :], ALU.mult, ALU.mult, [BKb[yb], BKb[yb + 1], stb2, gc_b], [BKb[yb], BKb[yb + 1]])
            tt("dve", sl[:, :], sl[:, :], pair(yb), ALU.add, [slb, BKb[yb], BKb[yb + 1]], [slb])
            dst, dstb = dst_tile(stage, s_, t)
            dma("pool", dst, sl[:, :], [slb], dstb, sld)

    def out_proj_mm(lhs_of_kc, lhs_bufs, wo, wo_bufs, yb):
        for kc in range(8):
            for h in range(2):
                mm(BK[yb + h], lhs_of_kc(kc), wo[:, kc, h * 512:(h + 1) * 512], kc == 0, kc == 7, lhs_bufs + wo_bufs, [BKb[yb + h]])

    class DiagRot:
        def __init__(self):
            self.r = Rot([(sb.alloc([128, 128], F32), Buf("dg%d" % i)) for i in range(2)])

        def next(self):
            t_, b_ = self.r.next()
            return t_[:, :], b_

    def ffn_stage(stage, l, j, cast_bufs):
        k = 0 if j == 0 else 2
        m0 = sb.mark()
        tc = TileCtx(4)
        tc_cur[0] = DiagRot()
        wout = sb.alloc([128, NFC, D], BF16)
        hT_t = [sb.alloc([128, 8, 1024], BF16) for _ in range(2)]
        hTb = [[Buf("hT%d_%d" % (a, i)) for i in range(8)] for a in range(2)]
        actT = sb.alloc([128, NFC, 1024], BF16)
        actb = [Buf("act%d" % i) for i in range(NFC)]
        sg = Rot([(sb.alloc([128, 512], F32), Buf("sg%d" % i)) for i in range(2)])
        gu = Rot([(2, 3), (4, 5)])
        ybank = Rot([6, 0])
        Bwin, Bwout = cast_bufs
        wob = []
        for (k0, k1) in ((0, 6), (6, 12), (12, 17), (17, 22)):
            b_ = Buf("wout")
            dma("sp", wout[:, k0:k1, :], wout_s[l][j][:, k0:k1, :], [Bwout], [b_], new_dsem())
            wob.append((k0, k1, b_))

        def wob_of(fc):
            for (k0, k1, b_) in wob:
                if k0 <= fc < k1:
                    return b_
        sbs = []
        x_only = full and stage == 5
        if not x_only:
            ctx_tiles = [(s, t) for s in range(nseq) for t in range(2)]
            for i in range(0, len(ctx_tiles), 8):
                sbs.append(ctx_tiles[i:i + 8])
        for s in range(nseq):
            sbs.append([(s, t) for t in range(2, 10)])
            sbs.append([(s, t) for t in range(10, 18)])
        nsb = len(sbs)

        def do_prepA(bi, i):
            s, t = sbs[bi][i]
            return prepA(tc, stage, s, t)

        def do_prepB(bi, i, xs, xsb):
            s, t = sbs[bi][i]
            prepB(tc, l, k, role_of(s, t), xs, xsb, hT_t[bi % 2][:, :, i * 128:(i + 1) * 128], hTb[bi % 2][i])

        for i in range(len(sbs[0])):
            xs, xsb = do_prepA(0, i)
            do_prepB(0, i, xs, xsb)
        for bi in range(nsb):
            tiles = sbs[bi]
            ntl = len(tiles)
            ncols = ntl * 128
            hT = hT_t[bi % 2]
            hb = hTb[bi % 2]
            role = role_of(*tiles[0])
            gc_t, gc_b = make_gcbc(l, k, role)
            halves = [(c0, min(512, ncols - c0)) for c0 in range(0, ncols, 512)]
            pend = None
            for g in range(11):
                wt_, wb_, wd_ = wslot.next()
                dma("sp", wt_, win_s[l][j][g], [Bwin], [wb_], wd_)
                for c in range(2):
                    fc = 2 * g + c
                    for (c0, cn) in halves:
                        gb, ub = gu.next()
                        hbs = hb[c0 // 128:(c0 + cn) // 128]
                        for kc in range(8):
                            mm(BK[gb][:, 0:cn], wt_[:, kc, 0, c * 128:(c + 1) * 128], hT[:, kc, c0:c0 + cn], kc == 0, kc == 7, [wb_] + hbs, [BKb[gb]])
                        for kc in range(8):
                            mm(BK[ub][:, 0:cn], wt_[:, kc, 1, c * 128:(c + 1) * 128], hT[:, kc, c0:c0 + cn], kc == 0, kc == 7, [wb_] + hbs, [BKb[ub]])
                        sg_t, sg_b = sg.next()
                        act(sg_t[:, 0:cn], BK[gb][:, 0:cn], AF.Silu, [BKb[gb]], [sg_b])
                        tt("dve", actT[:, fc, c0:c0 + cn], sg_t[:, 0:cn], BK[ub][:, 0:cn], ALU.mult, [sg_b, BKb[ub]], [actb[fc]])
                if bi + 1 < nsb:
                    nn = len(sbs[bi + 1])
                    if pend is not None:
                        do_prepB(bi + 1, pend[0], pend[1], pend[2])
                        pend = None
                    if g < nn:
                        xs, xsb = do_prepA(bi + 1, g)
                        pend = (g, xs, xsb)
            if pend is not None:
                do_prepB(bi + 1, pend[0], pend[1], pend[2])
            for i, (s, t) in enumerate(tiles):
                yb = ybank.next()
                for fc in range(NFC):
                    for h in range(2):
                        mm(BK[yb + h], actT[:, fc, i * 128:(i + 1) * 128], wout[:, fc, h * 512:(h + 1) * 512],
                           fc == 0, fc == NFC - 1, [actb[fc], wob_of(fc)], [BKb[yb + h]])
                post_update(tc, stage, s, t, yb, gc_t, gc_b)
        P.barrier(flush=tc.flush())
        sb.reset(m0)

    def out_proj_tile(tc, stage, s, t, lhs_of_kc, lhs_bufs, wo, wo_bufs, gc_t, gc_b, yb):
        for kc in range(8):
            for h in range(2):
                mm(BK[yb + h], lhs_of_kc(kc), wo[:, kc, h * 512:(h + 1) * 512], kc == 0, kc == 7, lhs_bufs + wo_bufs, [BKb[yb + h]])
        post_update(tc, stage, s, t, yb, gc_t, gc_b)

    def load_wo(scr, castbuf):
        wo = wslots_t[:, 0:2].rearrange("p a k u c -> p (a k u c)").rearrange("p (k c) -> p k c", k=8)
        it0, it1 = wslot.items[0], wslot.items[1]
        o = P.op("sp", lambda e: e.dma_start(out=wo, in_=scr), R=[castbuf], W=[it0[1], it1[1]], dsem=it0[2])
        wslot.i = 2
        return wo, it0[1], it1[1]

    def rglru_stage(stage, l):
        m0 = sb.mark()
        tc = TileCtx(2, 1)
        tc_cur[0] = DiagRot()
        hT = sb.alloc([128, 8, NTOK], BF16)
        hTb = [Buf("hTf%d" % i) for i in range(NT)]
        prodT = sb.alloc([128, 8, NTOK], BF16)
        prb = [Buf("prod%d" % i) for i in range(8)]
        gw_t = sb.alloc([128, 1, 8, 256], BF16)
        gws = Rot([(gw_t[:, i], Buf("gw%d" % i), new_dsem()) for i in range(1)])
        XW = 2310
        xr = sb.alloc([128, 2, XW], F32)
        xrb = [Buf("xr0"), Buf("xr1")]
        rx = sb.alloc([128, 2, XW], F32)
        rxb_t = sb.alloc([128, 2, XW], BF16)
        rxB = [Buf("rx0"), Buf("rx1")]
        AB = Rot([(sb.alloc([128, 2, 512], F32), sb.alloc([128, 2, 512], F32), Buf("ab%d" % i)) for i in range(2)])
        q_t = sb.alloc([128, 2, 512], F32)
        q_b = Buf("q")
        SB_ = Rot([(sb.alloc([128, 2, 512], F32), Buf("sbk%d" % i)) for i in range(2)])
        carry = sb.alloc([128, 2], F32)
        Bcarry = Buf("carry")
        alias_slots = [(rx[:, 0, 0:1024], rxB[0], new_dsem()), (rx[:, 1, 0:1024], rxB[1], new_dsem())]
        tc.extra_flush = [rxB[0], rxB[1]]
        P.op("pool", lambda e: e.memset(xr[:, :, :], 0.0), W=xrb)
        ranges = [(0, 256, 2, 0)] + [(256 + 512 * i, 512, 261 + 512 * i, 259 + 512 * i) for i in range(4)]
        pb = Rot([0, 1])
        gb4 = Rot([(2, 3), (4, 5)])
        def prep_tile(s_, t):
            xs, xsb = prepA(tc, stage, s_, t)
            prepB(tc, l, 1, role_of(s_, t), xs, xsb, hT[:, :, t * 128:(t + 1) * 128], hTb[t])

        for s in range(nseq):
            if s == 0:
                for t in range(0, NT, 2):
                    prep_group(tc, stage, l, 1, [(s, t), (s, t + 1)], hT, hTb)
            gc_c = make_gcbc(l, 1, nseq)
            gc_x = make_gcbc(l, 1, s)
            for n in range(4):
                wt_, wb_, wd_ = wslot.next()
                dma("sp", wt_, rgin_s[n], [Bc["rgin"]], [wb_], wd_)
                gt_, gbuf_, gd_ = gws.next()
                dma("sp", gt_, rggw_s[n].rearrange("p d g j k -> p (d g j) k"), [Bc["rggw"]], [gbuf_], gd_)
                for c in range(2):
                    gch = 2 * n + c
                    for (h0, cn, dc0, u0) in ranges:
                        hbs = hTb[h0 // 128:(h0 + cn) // 128]
                        b = pb.next()
                        for kc in range(8):
                            mm(BK[b][:, 0:cn], wt_[:, kc, 0, c * 128:(c + 1) * 128], hT[:, kc, h0:h0 + cn], kc == 0, kc == 7, [wb_] + hbs, [BKb[b]])
                        act(prodT[:, gch, h0:h0 + cn], BK[b][:, 0:cn], AF.Gelu, [BKb[b]], [prb[gch]])
                        b = pb.next()
                        for kc in range(8):
                            mm(BK[b][:, 0:cn], wt_[:, kc, 1, c * 128:(c + 1) * 128], hT[:, kc, h0:h0 + cn], kc == 0, kc == 7, [wb_] + hbs, [BKb[b]])
                        cp("dve", xr[:, c, dc0:dc0 + cn], BK[b][:, 0:cn], [BKb[b]], [xrb[c]])
                for c in range(2):
                    gch = 2 * n + c
                    NU = 2307
                    cw = [smallT[:, O_CW + kk * 8 + gch:O_CW + kk * 8 + gch + 1] for kk in range(4)]
                    cb = smallT[:, O_CB + gch:O_CB + gch + 1]
                    ts("pool", rx[:, c, 0:NU], xr[:, c, 0:NU], cw[0], cb, ALU.mult, ALU.add, [xrb[c], Bmod], [rxB[c]])
                    for kk in range(1, 4):
                        stt("dve", rx[:, c, 0:NU], xr[:, c, kk:kk + NU], cw[kk], rx[:, c, 0:NU], ALU.mult, ALU.add, [xrb[c], rxB[c], Bmod], [rxB[c]])
                    cp("pool", rxb_t[:, c, 0:NU], rx[:, c, 0:NU], [rxB[c]], [rxB[c]])
                for d_ in range(2):
                    order = ranges if d_ == 0 else [ranges[0]] + ranges[:0:-1]
                    for ri, (h0, cn, dc0, u0) in enumerate(order):
                        a_t, i_t, ab_b = AB.next()
                        for k2 in range(2):
                            gch = 2 * n + k2
                            rb, ib = ((2, 3), (4, 5))[k2]
                            for g_ in range(2):
                                bb = rb if g_ == 0 else ib
                                for jc in range(2):
                                    mm(BK[bb][:, 0:cn], gt_[:, (d_ * 2 + g_) * 2 + jc, k2 * 128:(k2 + 1) * 128], rxb_t[:, jc, u0:u0 + cn],
                                       jc == 0, jc == 1, [gbuf_, rxB[jc]], [BKb[bb]])
                            hb_ = [hgb[:, (d_ * 2 + g_) * 8 + gch:(d_ * 2 + g_) * 8 + gch + 1] for g_ in range(2)]
                            act(a_t[:, k2, 0:cn], BK[rb][:, 0:cn], AF.Tanh, [BKb[rb], Bmod], [ab_b], bias=hb_[0], scale=0.5)
                            act(i_t[:, k2, 0:cn], BK[ib][:, 0:cn], AF.Tanh, [BKb[ib], Bmod], [ab_b], bias=hb_[1], scale=0.5)
                            hc_ = hcoef[:, d_ * 8 + gch:d_ * 8 + gch + 1]
                            act(a_t[:, k2, 0:cn], a_t[:, k2, 0:cn], AF.Exp, [ab_b, Bmod], [ab_b], scale=hc_, bias=hc_)
                        act(q_t[:, :, 0:cn], a_t[:, :, 0:cn], AF.Square, [ab_b], [q_b])
                        act(q_t[:, :, 0:cn], q_t[:, :, 0:cn], AF.Sqrt, [q_b, Bmod], [q_b], scale=-0.25, bias=quartT[:, 0:1])
                        stt("dve", i_t[:, :, 0:cn], i_t[:, :, 0:cn], 1.0, rx[:, :, u0:u0 + cn], ALU.add, ALU.mult, [ab_b, rxB[0], rxB[1]], [ab_b])
                        tt("pool", i_t[:, :, 0:cn], i_t[:, :, 0:cn], q_t[:, :, 0:cn], ALU.mult, [ab_b, q_b], [ab_b])
                        if d_ == 0:
                            for k2 in range(2):
                                if ri == 0:
                                    init = 0.0
                                else:
                                    pu = 255 if ri == 1 else u0 - 1
                                    init = xr[:, k2, pu:pu + 1]
                                P.op("dve", lambda e, o_=xr[:, k2, u0:u0 + cn], a_=a_t[:, k2, 0:cn], b_=i_t[:, k2, 0:cn], in_=init:
                                     e.tensor_tensor_scan(out=o_, data0=a_, data1=b_, initial=in_, op0=ALU.mult, op1=ALU.add),
                                     R=[ab_b, xrb[k2]], W=[xrb[k2]])
                        else:
                            s_t, s_b = SB_.next()
                            for k2 in range(2):
                                init = 0.0 if ri == 0 else carry[:, k2:k2 + 1]
                                P.op("dve", lambda e, o_=s_t[:, k2, 0:cn][:, ::-1], a_=a_t[:, k2, 0:cn][:, ::-1], b_=i_t[:, k2, 0:cn][:, ::-1], in_=init:
                                     e.tensor_tensor_scan(out=o_, data0=a_, data1=b_, initial=in_, op0=ALU.mult, op1=ALU.add),
                                     R=[ab_b, Bcarry], W=[s_b])
                                cp("dve", carry[:, k2:k2 + 1], s_t[:, k2, 0:1], [s_b], [Bcarry])
                            tt("pool", s_t[:, :, 0:cn], s_t[:, :, 0:cn], xr[:, :, u0:u0 + cn], ALU.add, [s_b, xrb[0], xrb[1]], [s_b])
                            tt("dve", prodT[:, 2 * n:2 * n + 2, h0:h0 + cn], prodT[:, 2 * n:2 * n + 2, h0:h0 + cn], s_t[:, :, 0:cn], ALU.mult,
                               [prb[2 * n], prb[2 * n + 1], s_b], [prb[2 * n], prb[2 * n + 1]])
                for c in range(2):
                    for (p0, pn) in ((0, 2), (258, 3), (2309, 1)):
                        P.op("pool", lambda e, a_=xr[:, c, p0:p0 + pn]: e.memset(a_, 0.0), W=[xrb[c]])
            if s == 0:
                print("rglru stage sbuf end", sb.off, flush=True)
            wo, wb0, wb1 = load_wo(rgwo_s, Bc["rgwo"])
            yb = Rot([6, 2, 4])
            tc.slots = Rot(list(tc.base_slots.items) + alias_slots)
            for t in range(0, NT, 2):
                g_ = gc_c if t < 2 else gc_x
                items = []
                for tt_ in (t, t + 1):
                    ybk = yb.next()
                    out_proj_mm(lambda kc, tt_=tt_: prodT[:, kc, tt_ * 128:(tt_ + 1) * 128], prb, wo, [wb0, wb1], ybk)
                    items.append((s, tt_, ybk, g_[0], g_[1]))
                post_group(tc, stage, items)
                if s + 1 < nseq:
                    prep_group(tc, stage, l, 1, [(s + 1, t), (s + 1, t + 1)], hT, hTb)
            tc.slots = tc.base_slots
        P.barrier(flush=tc.flush())
        sb.reset(m0)

    def attn_stage(stage, l):
        m0 = sb.mark()
        tc = TileCtx(2)
        tc_cur[0] = DiagRot()
        hT = sb.alloc([128, 8, NTOK], BF16)
        hTb = [Buf("hTf%d" % i) for i in range(NT)]
        QT = sb.alloc([128, 8, SEQ], BF16)
        Qb = [Buf("q%d" % i) for i in range(8)]
        alias_slots = [(QT[:, j, :].bitcast(F32), Qb[j], new_dsem()) for j in range(2)]
        tc.extra_flush = list(Qb)
        Kd = sb.alloc([128, 4, 2, NTOK], BF16)
        Kb = [Buf("k%d" % i) for i in range(4)]
        Va = sb.alloc([128, NT, 4, 66], BF16)
        Vb = [Buf("v%d" % i) for i in range(NT)]
        cosT = sb.alloc([128, SEQ], F32)
        sinT = sb.alloc([128, SEQ], F32)
        Brope = Buf("rope")
        PT = Rot([(sb.alloc([128, 5, 4, 128], BF16), Buf("pt%d" % i)) for i in range(2)])
        rt = Rot([(sb.alloc([128, 512], F32), Buf("rt%d" % i)) for i in range(2)])
        attn = Rot([(sb.alloc([128, D], BF16), Buf("at%d" % i)) for i in range(2)])
        den = Rot([(sb.alloc([128, 8], F32), Buf("den%d" % i)) for i in range(2)])
        dma("sp", cosT[:, :], ropec, [], [Brope], new_dsem())
        dma("sp", sinT[:, :], ropes, [], [Brope], new_dsem())
        P.op("pool", lambda e: e.memset(Va[:, :, :, 64:66], 1.0), W=Vb)
        P.op("pool", lambda e: e.memset(Kd[:, :, :, :], 0.0), W=Kb)
        pb2 = Rot([(0, 1), (2, 3)])
        pb1 = Rot([0, 1, 2, 3])
        def prep_tile(s_, t):
            xs, xsb = prepA(tc, stage, s_, t)
            prepB(tc, l, 1, role_of(s_, t), xs, xsb, hT[:, :, t * 128:(t + 1) * 128], hTb[t])

        for s in range(nseq):
            if s == 0:
                for t in range(0, NT, 2):
                    prep_group(tc, stage, l, 1, [(s, t), (s, t + 1)], hT, hTb)
            gc_x = make_gcbc(l, 1, s)
            for g in range(6):
                wt_, wb_, wd_ = wslot.next()
                dma("sp", wt_, wqk_s[g], [Bc["wqk"]], [wb_], wd_)
                for c in range(2):
                    if g < 4:
                        dst_of = lambda x0, cn, j=2 * g + c: [(0, 128, QT[:, j, x0:x0 + cn])]
                        dbuf = Qb[2 * g + c]
                    else:
                        kv = 2 * (g - 4) + c
                        dst_of = lambda x0, cn, kv=kv: [(0, 64, Kd[0:64, kv, 0, 256 + x0:256 + x0 + cn]),
                                                        (64, 128, Kd[64:128, kv, 1, 256 + x0:256 + x0 + cn])]
                        dbuf = Kb[kv]
                        b = pb1.next()
                        for kc in range(8):
                            mm(BK[b][:, 0:256], wt_[:, kc, 0, c * 128:(c + 1) * 128], hT[:, kc, 0:256], kc == 0, kc == 7, [wb_] + hTb[0:2], [BKb[b]])
                        cp("dve", Kd[0:64, kv, 0, 0:256], BK[b][0:64, 0:256], [BKb[b]], [dbuf])
                        cp("dve", Kd[64:128, kv, 1, 0:256], BK[b][64:128, 0:256], [BKb[b]], [dbuf])
                    for xi in range(4):
                        x0 = xi * 512
                        h0 = 256 + x0
                        hbs = hTb[h0 // 128:(h0 + 512) // 128]
                        bp, bs_ = pb2.next()
                        for kc in range(8):
                            mm(BK[bp], wt_[:, kc, 0, c * 128:(c + 1) * 128], hT[:, kc, h0:h0 + 512], kc == 0, kc == 7, [wb_] + hbs, [BKb[bp]])
                        for kc in range(8):
                            mm(BK[bs_], wt_[:, kc, 1, c * 128:(c + 1) * 128], hT[:, kc, h0:h0 + 512], kc == 0, kc == 7, [wb_] + hbs, [BKb[bs_]])
                        t1, t1b = rt.next()
                        t2, t2b = rt.next()
                        tt("dve", t1[:, :], BK[bp], cosT[:, x0:x0 + 512], ALU.mult, [BKb[bp], Brope], [t1b])
                        tt("dve", t2[:, :], BK[bs_], sinT[:, x0:x0 + 512], ALU.mult, [BKb[bs_], Brope], [t2b])
                        for (p0, p1, dap) in dst_of(x0, 512):
                            tt("pool", dap, t1[p0:p1, :], t2[p0:p1, :], ALU.add, [t1b, t2b], [dbuf])
            wv_t, Bwv, wvd_ = wslot.next()
            dma("sp", wv_t[:, :, 0, :], wv_s, [Bc["wv"]], [Bwv], wvd_)
            for t in range(NT):
                b = pb1.next()
                for kc in range(8):
                    mm(BK[b][:, 0:256], hT[:, kc, t * 128:(t + 1) * 128], wv_t[:, kc, 0, :], kc == 0, kc == 7, [hTb[t], Bwv], [BKb[b]])
                act(Va[:, t, :, 0:64], BK[b][:, 0:256].rearrange("p (k d) -> p k d", k=4), AF.Copy, [BKb[b]], [Vb[t]])
            wo, wb0, wb1 = load_wo(awo_s, Bc["awo"])
            spair = Rot([0, 2])
            obank = Rot([4, 5])

            def chunks_of(qb):
                tq = 2 + qb
                ch = [(0, None), (1, None)]
                if qb > 0:
                    ch.append((tq - 1, mask_prev))
                ch.append((tq, None))
                if qb < 15:
                    ch.append((tq + 1, mask_next))
                return ch

            def emit_scores(qb, kv):
                chunks = chunks_of(qb)
                pt_t, pt_b = PT.next()
                ci = 0
                while ci < len(chunks):
                    n2 = min(2 if os.environ.get("ATT_EXP2", "1") == "1" else 1, len(chunks) - ci)
                    b0 = spair.next()
                    for cc in range(n2):
                        kt, msk = chunks[ci + cc]
                        first = True
                        if msk is not None:
                            mm(BK[b0 + cc][:, 0:512], ident_b[:, :], msk[:, :, :].rearrange("p g q -> p (g q)"), True, False, [Bconst], [BKb[b0 + cc]])
                            first = False
                        for gq in range(4):
                            h = 4 * kv + gq
                            mm(BK[b0 + cc][:, gq * 128:(gq + 1) * 128], Kd[:, kv, h % 2, kt * 128:(kt + 1) * 128],
                               QT[:, h // 2, qb * 128:(qb + 1) * 128], first, gq == 3, [Kb[kv], Qb[h // 2]], [BKb[b0 + cc]])
                            first = False
                    act(pt_t[:, ci:ci + n2].rearrange("p c g q -> p (c g q)"), ps[:, b0 * 512:(b0 + n2) * 512], AF.Exp,
                        [BKb[b0 + cc_] for cc_ in range(n2)], [pt_b], scale=0.125)
                    ci += n2
                return (qb, kv, chunks, pt_t, pt_b)

            cur_at = [None]
            tbk = Rot([6, 7])

            def emit_pv(item):
                qb, kv, chunks, pt_t, pt_b = item
                nch = len(chunks)
                tq = 2 + qb
                if kv == 0:
                    cur_at[0] = attn.next()
                at_t, at_b = cur_at[0]
                ob = obank.next()
                for gq in range(4):
                    for ci, (kt, msk) in enumerate(chunks):
                        mm(BK[ob][:, gq * 65:(gq + 1) * 65], pt_t[:, ci, gq, :], Va[:, kt, kv, 0:65], ci == 0, ci == nch - 1, [pt_b, Vb[kt]], [BKb[ob]])
                dn_t, dn_b = den.next()
                ov_ = BK[ob][:, 0:260].rearrange("p (g d) -> p g d", d=65)
                tt("dve", dn_t[:, 0:4], ov_[:, :, 64], esink[:, 4 * kv:4 * kv + 4], ALU.add, [BKb[ob], Bmod], [dn_b])
                P.op("dve", lambda e, a_=dn_t[:, 4:8], b_=dn_t[:, 0:4]: e.reciprocal(out=a_, in_=b_), R=[dn_b], W=[dn_b])
                if os.environ.get("ATT_BCAST", "1") == "1":
                    tt("dve", at_t[:, kv * 256:(kv + 1) * 256].rearrange("p (g d) -> p g d", d=64), ov_[:, :, 0:64],
                       dn_t[:, 4:8].unsqueeze(2).to_broadcast([128, 4, 64]), ALU.mult, [BKb[ob], dn_b], [at_b])
                else:
                    for gq in range(4):
                        h = 4 * kv + gq
                        ts("dve", at_t[:, h * 64:(h + 1) * 64], ov_[:, gq, 0:64], dn_t[:, 4 + gq:5 + gq], None, ALU.mult, None, [BKb[ob], dn_b], [at_b])
                if kv == 3:
                    ybk = tbk.next()
                    pbt = bank_bf(ybk)
                    for kc in range(8):
                        tr(pbt[:, kc * 128:(kc + 1) * 128], at_t[:, kc * 128:(kc + 1) * 128], ident_b[:, :], [at_b, Bconst], [BKb[ybk]])
                    cp("dve", hT[:, :, tq * 128:(tq + 1) * 128], pbt[:, 0:1024].rearrange("p (k q) -> p k q", k=8), [BKb[ybk]], [hTb[tq]])

            prev = None
            for qb in range(16):
                for kv in range(4):
                    cur = emit_scores(qb, kv)
                    if os.environ.get("ATT_PIPE", "1") != "1":
                        emit_pv(cur)
                        continue
                    if prev is not None:
                        emit_pv(prev)
                    prev = cur
            if prev is not None:
                emit_pv(prev)
            ybr = Rot([6, 2, 4])
            tc.slots = Rot(list(tc.base_slots.items) + alias_slots)
            if s + 1 < nseq:
                prep_group(tc, stage, l, 1, [(s + 1, 0), (s + 1, 1)], hT, hTb)
            for tq in range(2, NT, 2):
                items = []
                for tt_ in (tq, tq + 1):
                    ybk = ybr.next()
                    out_proj_mm(lambda kc, tt_=tt_: hT[:, kc, tt_ * 128:(tt_ + 1) * 128], [hTb[tt_]], wo, [wb0, wb1], ybk)
                    items.append((s, tt_, ybk, gc_x[0], gc_x[1]))
                post_group(tc, stage, items)
                if s + 1 < nseq:
                    prep_group(tc, stage, l, 1, [(s + 1, tq), (s + 1, tq + 1)], hT, hTb)
            tc.slots = tc.base_slots
        print("attn stage sbuf end", sb.off, flush=True)
        P.barrier(flush=tc.flush())
        sb.reset(m0)

    if nstages > 0:
        ffn_stage(0, 0, 0, Bc["ffn00"])
    if nstages > 1:
        rglru_stage(1, 0)
    if nstages > 2:
        ffn_stage(2, 0, 1, Bc["ffn01"])
    if nstages > 3:
        ffn_stage(3, 1, 0, Bc["ffn10"])
    if nstages > 4:
        attn_stage(4, 1)
    if nstages > 5:
        ffn_stage(5, 1, 1, Bc["ffn11"])

    if not full:
        dcp = new_dsem()
        for s in range(nseq):
            dma("sp", outp[s], resD[s], [Bres[(s, t)] for t in range(NT)], [], dcp)
    def final(e):
        for d in all_ds:
            if d.count:
                e.wait_ge(d.h, d.count)
        return e.nop()
    P.op("sp", final)

    run = P.emit(esems)
    with nc.Block() as block:
        @block.tensor
        def _(e):
            run("pe", e)

        @block.scalar
        def _(e):
            run("act", e)

        @block.vector
        def _(e):
            run("dve", e)

        @block.gpsimd
        def _(e):
            run("pool", e)

        @block.sync
        def _(e):
            run("sp", e)
    print("ops:", {k: len(v) for k, v in P.q.items()}, "marked:", {k: sum(1 for o in v if o.marked) for k, v in P.q.items()},
          "waits:", {k: sum(len(o.waits) for o in v) for k, v in P.q.items()}, "sbuf peak", sb.peak, "dsems", nds[0], flush=True)
    return nc


def rope_tables():
    inv = 1.0 / (10000.0 ** (np.arange(0, 32, 2, dtype=np.float32) / 32.0))
    pos = np.arange(SEQ)
    row = (pos // 64).astype(np.float32)
    col = (pos % 64).astype(np.float32)
    cosT = np.zeros((128, SEQ), np.float32)
    sinT = np.zeros((128, SEQ), np.float32)
    for p in range(128):
        d = p % 64
        base = row if d < 32 else col
        ang = (base * inv[d % 16]).astype(np.float32)
        cosT[p] = np.cos(ang)
        sg = -1.0 if (d % 32) < 16 else 1.0
        sinT[p] = sg * np.sin(ang)
    return cosT, sinT


def host_layout(inputs, nseq, core):
    f = lambda a: np.ascontiguousarray(np.asarray(a, dtype=np.float32))
    x = f(inputs["x"])
    ctx = f(inputs["ctx"])
    c = f(inputs["c"])
    b0 = core * nseq
    xin = np.concatenate([ctx[b0:b0 + nseq], x[b0:b0 + nseq]], axis=1)
    cond = np.concatenate([c[b0:b0 + nseq], f(inputs["c_ctx"])[None]], axis=0).reshape(-1, 128)
    small = np.concatenate([
        f(inputs["b_mod"]).reshape(-1, 128), f(inputs["norm_g"]).reshape(-1, 128),
        f(inputs["rg_conv_w"]).reshape(-1, 128), f(inputs["rg_conv_b"]).reshape(-1, 128),
        f(inputs["rg_gate_b"]).reshape(-1, 128), f(inputs["rg_lambda"]).reshape(-1, 128)], axis=0)
    assert small.shape[0] == NV
    wqkv = f(inputs["attn_w_qkv"])[0]
    wq = wqkv[:, :1024]
    wk = wqkv[:, 1024:1280]
    kdup = np.concatenate([np.concatenate([wk[:, kv * 64:(kv + 1) * 64]] * 2, axis=1) for kv in range(4)], axis=1)
    plain = np.concatenate([wq, kdup], axis=1)
    dd = np.arange(64)
    part = np.where((dd % 32) < 16, dd + 16, dd - 16)
    perm = (np.arange(1536) // 64) * 64 + part[np.arange(1536) % 64]
    sw = plain[:, perm]
    wqk2 = np.ascontiguousarray(np.stack([plain, sw], axis=1))
    cosT, sinT = rope_tables()
    return {
        "xin": np.ascontiguousarray(xin), "cond": np.ascontiguousarray(cond), "small": small,
        "sink": f(inputs["attn_sink"]).reshape(1, 16), "ropec": cosT, "ropes": sinT,
        "w_mod": f(inputs["w_mod"]), "ffn_w_in": f(inputs["ffn_w_in"]), "ffn_w_out": f(inputs["ffn_w_out"]),
        "rg_w_in": f(inputs["rg_w_in"])[0], "rg_gate_w": f(inputs["rg_gate_w"])[0], "rg_w_out": f(inputs["rg_w_out"])[0],
        "wqk2": wqk2, "wv": np.ascontiguousarray(wqkv[:, 1280:1536]), "attn_w_o": f(inputs["attn_w_o"])[0],
    }


def kernel(**inputs):
    nseq = 32 // NCORES
    nc = build_program(nseq, 6)
    in_maps = [host_layout(inputs, nseq, core) for core in range(NCORES)]
    res = run_bass_kernel_spmd(nc, in_maps, core_ids=list(range(NCORES)))
    out = np.concatenate([np.asarray(r["out"]) for r in res.results], axis=0)
    return out.astype(np.float32)
```

```python
import os
import numpy as np
import concourse.bass as bass
import concourse.mybir as mybir
from concourse.bass_utils import run_bass_kernel_spmd

F32 = mybir.dt.float32
BF16 = mybir.dt.bfloat16
AF = mybir.ActivationFunctionType
ALU = mybir.AluOpType

D = 1024
SEQ = 2048
CTX = 256
NTOK = SEQ + CTX
NT = NTOK // 128
DFF = 2816
NFC = DFF // 128
NCORES = 8
EPS = 1e-6
NV = 328
O_BMOD, O_G, O_CW, O_CB, O_GB, O_LAM = 0, 144, 240, 272, 280, 312


class Buf:
    __slots__ = ("name", "lw", "rd")

    def __init__(self, name=""):
        self.name = name
        self.lw = None
        self.rd = []


class DSem:
    __slots__ = ("h", "count", "key")

    def __init__(self, h, key):
        self.h = h
        self.count = 0
        self.key = key


class Op:
    __slots__ = ("eng", "idx", "fn", "waits", "marked", "clock", "dsem", "dval", "semval")

    def __init__(self, eng, idx, fn):
        self.eng = eng
        self.idx = idx
        self.fn = fn
        self.waits = []
        self.marked = False
        self.clock = None
        self.dsem = None
        self.dval = 0
        self.semval = 0


class Prog:
    ENG = ("pe", "act", "dve", "pool", "sp")

    def __init__(self, nc):
        self.nc = nc
        self.q = {k: [] for k in self.ENG}
        self.clock = {k: {} for k in self.ENG}

    @staticmethod
    def _kv(op):
        if op.dsem is not None:
            return op.dsem.key, op.dval
        return op.eng, op.idx + 1

    def op(self, eng, fn, R=(), W=(), dsem=None, extra=()):
        q = self.q[eng]
        o = Op(eng, len(q), fn)
        is_dma = dsem is not None
        if is_dma:
            dsem.count += 16
            o.dsem = dsem
            o.dval = dsem.count
        deps = []
        for b in R:
            if b.lw is not None:
                deps.append((b.lw, 0))
        for b in W:
            if b.lw is not None:
                deps.append((b.lw, 1))
            for r in b.rd:
                deps.append((r, 2))
        for d in extra:
            deps.append((d, 0))
        clk = self.clock[eng]
        best = {}
        for (d, raw) in deps:
            if d is o:
                continue
            d_dma = d.dsem is not None
            if (not d_dma) and d.eng == eng and not is_dma:
                if eng == "pe" or (eng != "pool" and raw == 2):
                    continue
            k, v = self._kv(d)
            if clk.get(k, 0) >= v:
                continue
            cur = best.get(k)
            if cur is None or cur[0] < v:
                best[k] = (v, d)
        for k, (v, d) in best.items():
            if clk.get(k, 0) >= v:
                continue
            o.waits.append(d)
            d.marked = True
            for kk, vv in d.clock.items():
                if clk.get(kk, 0) < vv:
                    clk[kk] = vv
        c = dict(clk)
        k, v = self._kv(o)
        c[k] = v
        o.clock = c
        for b in R:
            b.rd.append(o)
        for b in W:
            b.lw = o
            b.rd = []
        q.append(o)
        return o

    def barrier(self, flush=()):
        lasts = []
        for e in self.ENG:
            q = self.q[e]
            for o in reversed(q):
                if o.dsem is None:
                    lasts.append(o)
                    break
        fl = list(flush)
        for e in self.ENG:
            self.op(e, lambda en: en.nop(), W=fl if e == "sp" else (), extra=[o for o in lasts if o.eng != e])
        spl = self.q["sp"][-1]
        for e in self.ENG:
            if e != "sp":
                self.op(e, lambda en: en.nop(), extra=[spl])

    def emit(self, sems):
        for e, q in self.q.items():
            n = 0
            for o in q:
                if o.dsem is None and o.marked:
                    n += 1
                    o.semval = n

        def run(ename, engine):
            for o in self.q[ename]:
                for d in o.waits:
                    if d.dsem is not None:
                        engine.wait_ge(d.dsem.h, d.dval)
                    else:
                        engine.wait_ge(sems[d.eng], d.semval)
                ins = o.fn(engine)
                if o.dsem is not None:
                    ins.then_inc(o.dsem.h, 16)
                elif o.marked:
                    ins.then_inc(sems[ename], 1)
        return run


class Rot:
    def __init__(self, items):
        self.items = items
        self.i = 0

    def next(self):
        it = self.items[self.i % len(self.items)]
        self.i += 1
        return it


class SBAlloc:
    LO = 16640
    HI = 229376

    def __init__(self, nc):
        self.nc = nc
        self.off = self.LO
        self.n = 0
        self.peak = 0

    def alloc(self, shape, dt):
        esz = 4 if dt == F32 else 2
        nb = esz
        for s in shape[1:]:
            nb *= s
        off = (self.off + 63) // 64 * 64
        assert off + nb <= self.HI, "SBUF overflow: need %d at %d" % (nb, off)
        self.off = off + nb
        self.peak = max(self.peak, self.off)
        self.n += 1
        return self.nc.alloc_sbuf_tensor_at("t%d" % self.n, list(shape), dt, offset=off)

    def mark(self):
        return self.off

    def reset(self, m):
        self.off = m


def build_program(nseq, nstages=6):
    nc = bass.Bass("TRN2", target_bir_lowering=False)
    R = nseq + 1
    full = nstages >= 6

    def din(name, shape, dt=F32):
        return nc.dram_tensor(name, list(shape), dt, kind="ExternalInput").ap()

    def dscr(name, shape, dt=BF16):
        return nc.dram_tensor(name, list(shape), dt, kind="Internal").ap()

    xin = din("xin", [nseq, NTOK, D])
    cond = din("cond", [R * 8, 128])
    small = din("small", [NV, 128])
    sink = din("sink", [1, 16])
    ropec = din("ropec", [128, SEQ])
    ropes = din("ropes", [128, SEQ])
    w_mod = din("w_mod", [2, D, 9 * D])
    ffn_w_in = din("ffn_w_in", [2, 2, D, 2 * DFF])
    ffn_w_out = din("ffn_w_out", [2, 2, DFF, D])
    rg_w_in = din("rg_w_in", [D, 2 * D])
    rg_gate_w = din("rg_gate_w", [2, 2, 4, 256, 256])
    rg_w_out = din("rg_w_out", [D, D])
    wqk2 = din("wqk2", [D, 2, 1536])
    wv = din("wv", [D, 256])
    attn_w_o = din("attn_w_o", [D, D])
    if full:
        outp = nc.dram_tensor("out", [nseq, SEQ, D], F32, kind="ExternalOutput").ap()
    else:
        outp = nc.dram_tensor("dbg", [nseq, NTOK, D], F32, kind="ExternalOutput").ap()
    resD = dscr("resD", [nseq, NTOK, D], F32)
    win_s = [[dscr("win_s%d%d" % (l, j), [11, 128, 8, 2, 256]) for j in range(2)] for l in range(2)]
    wout_s = [[dscr("wout_s%d%d" % (l, j), [128, NFC, D]) for j in range(2)] for l in range(2)]
    rgin_s = dscr("rgin_s", [4, 128, 8, 2, 256])
    rggw_s = dscr("rggw_s", [4, 128, 2, 2, 2, 256])
    rgwo_s = dscr("rgwo_s", [128, 8, D])
    wqk_s = dscr("wqk_s", [6, 128, 8, 2, 256])
    wv_s = dscr("wv_s", [128, 8, 256])
    awo_s = dscr("awo_s", [128, 8, D])

    P = Prog(nc)
    sb = SBAlloc(nc)
    sem_handles = []

    def new_sem(name):
        h = nc.alloc_semaphore(name)
        sem_handles.append(h)
        return h

    nds = [0]

    all_ds = []

    def new_dsem():
        nds[0] += 1
        d = DSem(new_sem("d%d" % nds[0]), "d%d" % nds[0])
        all_ds.append(d)
        return d

    esems = {k: new_sem("s_" + k) for k in Prog.ENG}

    ps = nc.alloc_psum_tensor("ps", [128, 4096], F32)
    BK = [ps[:, b * 512:(b + 1) * 512] for b in range(8)]
    BKb = [Buf("bank%d" % b) for b in range(8)]

    def bank_bf(b):
        return BK[b].bitcast(BF16)

    def pair(b):
        return ps[:, b * 512:(b + 2) * 512]

    ident_b = sb.alloc([128, 128], BF16)
    ident_f = sb.alloc([128, 128], F32)
    ones_f = sb.alloc([128, 128], F32)
    mask_prev = sb.alloc([128, 4, 128], BF16)
    mask_next = sb.alloc([128, 4, 128], BF16)
    smallT = sb.alloc([128, NV], F32)
    sT = sb.alloc([128, R * 8], F32)
    modT = sb.alloc([128, 2, 72, R], F32)
    gmT = sb.alloc([128, 2, 3, R, 8], F32)
    gcT = sb.alloc([128, 2, 3, R, 8], F32)
    coef = sb.alloc([128, 16], F32)
    esink = sb.alloc([128, 16], F32)
    hgb = sb.alloc([128, 32], F32)
    hcoef = sb.alloc([128, 16], F32)
    quartT = sb.alloc([128, 1], F32)
    epsT = sb.alloc([128, 1], F32)
    oneT = sb.alloc([128, 1], F32)
    wslots_t = sb.alloc([128, 3, 8, 2, 256], BF16)
    wslot = Rot([(wslots_t[:, i], Buf("ws%d" % i), new_dsem()) for i in range(3)])
    ws_bufs = [it[1] for it in wslot.items]
    gcbc = Rot([(sb.alloc([128, D], F32), Buf("gcbc%d" % i)) for i in range(2)])
    stat_t = sb.alloc([128, 16, 4], F32)
    stat = Rot([(stat_t[:, i, :], Buf("st%d" % i)) for i in range(16)])
    Bconst = Buf("const")
    Bmod = Buf("mod")
    persist_mark = sb.mark()

    Bres = {(s, t): Buf("res%d_%d" % (s, t)) for s in range(nseq) for t in range(NT)}

    def mm(out, lhsT, rhs, start, stop, R_, W_):
        return P.op("pe", lambda e: e.matmul(out, lhsT=lhsT, rhs=rhs, start=start, stop=stop), R=R_, W=W_)

    def tr(out, in_, ident, R_, W_):
        return P.op("pe", lambda e: e.transpose(out=out, in_=in_, identity=ident), R=R_, W=W_)

    def act(out, in_, func, R_, W_, bias=None, scale=None, accum=None):
        kw = {}
        if bias is not None:
            kw["bias"] = bias
        if scale is not None:
            kw["scale"] = scale
        if accum is not None:
            kw["accum_out"] = accum
        return P.op("act", lambda e: e.activation(out=out, in_=in_, func=func, **kw), R=R_, W=W_)

    def ts(eng, out, in0, s1, s2, op0, op1, R_, W_):
        if s2 is None:
            return P.op(eng, lambda e: e.tensor_scalar(out=out, in0=in0, scalar1=s1, scalar2=None, op0=op0), R=R_, W=W_)
        return P.op(eng, lambda e: e.tensor_scalar(out=out, in0=in0, scalar1=s1, scalar2=s2, op0=op0, op1=op1), R=R_, W=W_)

    def tt(eng, out, in0, in1, op, R_, W_):
        return P.op(eng, lambda e: e.tensor_tensor(out=out, in0=in0, in1=in1, op=op), R=R_, W=W_)

    def stt(eng, out, in0, s, in1, op0, op1, R_, W_):
        return P.op(eng, lambda e: e.scalar_tensor_tensor(out=out, in0=in0, scalar=s, in1=in1, op0=op0, op1=op1), R=R_, W=W_)

    def cp(eng, out, in_, R_, W_):
        return P.op(eng, lambda e: e.tensor_copy(out=out, in_=in_), R=R_, W=W_)

    def dma(q, out, in_, R_, W_, dsem):
        return P.op(q, lambda e: e.dma_start(out=out, in_=in_), R=R_, W=W_, dsem=dsem)

    def rstd_from(ssq_ap, out_ap, stbuf):
        ts("dve", out_ap, ssq_ap, 1.0 / D, EPS, ALU.mult, ALU.add, [stbuf], [stbuf])
        act(out_ap, out_ap, AF.Sqrt, [stbuf], [stbuf])
        P.op("dve", lambda e: e.reciprocal(out=out_ap, in_=out_ap), R=[stbuf], W=[stbuf])

    ov = sb.mark()
    P.op("pool", lambda e: e.memset(ident_f[:], 1.0), W=[Bconst])
    P.op("pool", lambda e: e.affine_select(out=ident_f[:], in_=ident_f[:], pattern=[[-1, 128]], compare_op=ALU.is_equal,
                                           fill=0.0, base=0, channel_multiplier=1), R=[Bconst], W=[Bconst])
    P.op("pool", lambda e: e.memset(ones_f[:], 1.0), W=[Bconst])
    P.op("pool", lambda e: e.memset(epsT[:], EPS), W=[Bconst])
    P.op("pool", lambda e: e.memset(oneT[:], 1.0), W=[Bconst])
    cp("dve", ident_b[:], ident_f[:], [Bconst], [Bconst])
    P.op("pool", lambda e: e.memset(mask_prev[:], -30000.0), W=[Bconst])
    P.op("pool", lambda e: e.memset(mask_next[:], -30000.0), W=[Bconst])
    P.op("pool", lambda e: e.affine_select(out=mask_prev[:], in_=mask_prev[:], pattern=[[0, 4], [1, 128]], compare_op=ALU.is_gt,
                                           fill=0.0, base=0, channel_multiplier=-1), R=[Bconst], W=[Bconst])
    P.op("pool", lambda e: e.affine_select(out=mask_next[:], in_=mask_next[:], pattern=[[0, 4], [-1, 128]], compare_op=ALU.is_gt,
                                           fill=0.0, base=0, channel_multiplier=1), R=[Bconst], W=[Bconst])
    sm_in = sb.alloc([128, 3, 128], F32)
    Bsm = Buf("sm")
    ds_sm = new_dsem()
    dma("sp", sm_in[:, 0, :], small[0:128, :], [], [], ds_sm)
    dma("sp", sm_in[:, 1, :], small[128:256, :], [], [], ds_sm)
    dma("sp", sm_in[0:72, 2, :], small[256:328, :], [], [Bsm], ds_sm)
    for i, n in enumerate((128, 128, 72)):
        tr(BK[0][:, i * 128:i * 128 + n], sm_in[0:n, i, :], ident_f[0:n, 0:n], [Bsm, Bconst], [BKb[0]])
    cp("dve", smallT[:, :], BK[0][:, 0:NV], [BKb[0]], [Bmod])
    cd_in = sb.alloc([128, 128], F32)
    ds_cd = new_dsem()
    Bcd = Buf("cd")
    dma("sp", cd_in[0:R * 8, :], cond, [], [Bcd], ds_cd)
    act(cd_in[0:R * 8, :], cd_in[0:R * 8, :], AF.Silu, [Bcd], [Bcd])
    tr(BK[1][:, 0:R * 8], cd_in[0:R * 8, :], ident_f[0:R * 8, 0:R * 8], [Bcd, Bconst], [BKb[1]])
    cp("dve", sT[:, :], BK[1][:, 0:R * 8], [BKb[1]], [Bmod])
    wm_t = sb.alloc([128, 2, 8, 512], F32)
    wms = Rot([(wm_t[:, i], Buf("wm%d" % i), new_dsem()) for i in range(2)])
    sT3 = sT[:, :].rearrange("p (r k) -> p k r", k=8)
    for l in range(2):
        bank = 2 + l
        for g in range(18):
            wt_, wb_, wd_ = wms.next()
            dma("sp", wt_, w_mod[l, :, g * 512:(g + 1) * 512].rearrange("(kc p) c -> p kc c", p=128), [], [wb_], wd_)
            for c4 in range(4):
                j = g * 4 + c4
                for kc in range(8):
                    mm(BK[bank][:, j * R:(j + 1) * R], wt_[:, kc, c4 * 128:(c4 + 1) * 128], sT3[:, kc, :],
                       kc == 0, kc == 7, [wb_, Bmod], [BKb[bank]])
        for r in range(R):
            tt("dve", modT[:, l, :, r], BK[bank][:, 0:72 * R].rearrange("p (j r) -> p j r", r=R)[:, :, r],
               smallT[:, O_BMOD + l * 72:O_BMOD + (l + 1) * 72], ALU.add, [BKb[bank], Bmod], [Bmod])
    for l in range(2):
        for k in range(3):
            wt_k = 1.0 if k == 1 else 0.5
            gpre = smallT[:, O_G + (l * 6 + k) * 8:O_G + (l * 6 + k) * 8 + 8]
            gpost = smallT[:, O_G + (l * 6 + 3 + k) * 8:O_G + (l * 6 + 3 + k) * 8 + 8]
            for r in range(R):
                stt("dve", gmT[:, l, k, r, :], modT[:, l, (3 * k + 1) * 8:(3 * k + 2) * 8, r], 1.0, gpre, ALU.add, ALU.mult, [Bmod], [Bmod])
                stt("dve", gcT[:, l, k, r, :], modT[:, l, (3 * k + 2) * 8:(3 * k + 3) * 8, r], wt_k, gpost, ALU.mult, ALU.mult, [Bmod], [Bmod])
    act(coef[:, :], smallT[:, O_LAM:O_LAM + 16], AF.Exp, [Bmod], [Bmod], scale=-1.0)
    act(coef[:, :], coef[:, :], AF.Ln, [Bmod], [Bmod], bias=oneT[:, 0:1])
    ts("dve", coef[:, :], coef[:, :], -8.0, None, ALU.mult, None, [Bmod], [Bmod])
    ts("dve", hcoef[:, :], coef[:, :], 0.5, None, ALU.mult, None, [Bmod], [Bmod])
    ts("dve", hgb[:, :], smallT[:, O_GB:O_GB + 32], 0.5, None, ALU.mult, None, [Bmod], [Bmod])
    P.op("pool", lambda e: e.memset(quartT[:], 0.25), W=[Bmod])
    ds_sk = new_dsem()
    dma("sp", esink[:, :], sink.partition_broadcast(128), [], [Bmod], ds_sk)
    act(esink[:, :], esink[:, :], AF.Exp, [Bmod], [Bmod])

    def cast_pairs(dst, src3, ngroups):
        b = Buf("cast")
        d = new_dsem()
        n = ngroups * 2
        i = 0
        for g in range(ngroups):
            for u in range(2):
                i += 1
                dma("pool", dst[g, :, :, u, :], src3[:, u, g * 256:(g + 1) * 256].rearrange("(kc p) c -> p kc c", p=128),
                    [], [b] if i == n else [], d)
        return b

    def cast_rows(dst, src, kcn, step=4):
        b = Buf("cast")
        d = new_dsem()
        v = src.rearrange("(kc p) c -> p kc c", p=128)
        k0 = 0
        while k0 < kcn:
            k1 = min(kcn, k0 + step)
            dma("pool", dst[:, k0:k1, :], v[:, k0:k1, :], [], [b] if k1 == kcn else [], d)
            k0 = k1
        return b

    def cast_ffn(l, j):
        a = cast_pairs(win_s[l][j], ffn_w_in[l, j].rearrange("k (u f) -> k u f", u=2), 11)
        b = cast_rows(wout_s[l][j], ffn_w_out[l, j], NFC)
        return a, b

    Bc = {}
    Bc["ffn00"] = cast_ffn(0, 0)
    if nstages > 1:
        Bc["rgin"] = cast_pairs(rgin_s, rg_w_in.rearrange("k (u f) -> k u f", u=2), 4)
        bgw = Buf("cast")
        dgw = new_dsem()
        i = 0
        for n in range(4):
            for d_ in range(2):
                for g_ in range(2):
                    i += 1
                    dma("pool", rggw_s[n, :, d_, g_, :, :], rg_gate_w[d_, g_, n].rearrange("(jc p) k -> p jc k", p=128),
                        [], [bgw] if i == 16 else [], dgw)
        Bc["rggw"] = bgw
        Bc["rgwo"] = cast_rows(rgwo_s, rg_w_out, 8)
    if nstages > 2:
        Bc["ffn01"] = cast_ffn(0, 1)
    if nstages > 3:
        Bc["ffn10"] = cast_ffn(1, 0)
    if nstages > 4:
        Bc["wqk"] = cast_pairs(wqk_s, wqk2, 6)
        Bc["wv"] = cast_rows(wv_s, wv, 8, step=8)
        Bc["awo"] = cast_rows(awo_s, attn_w_o, 8)
    if nstages > 5:
        Bc["ffn11"] = cast_ffn(1, 1)

    P.barrier(flush=[Bsm, Bcd] + [it[1] for it in wms.items])
    sb.reset(ov)

    def role_of(s, t):
        return nseq if t < 2 else s

    def src_tile(stage, s, t):
        if stage == 0:
            return xin[s, t * 128:(t + 1) * 128, :], []
        return resD[s, t * 128:(t + 1) * 128, :], [Bres[(s, t)]]

    def dst_tile(stage, s, t):
        if full and stage == 5:
            return outp[s, (t - 2) * 128:(t - 1) * 128, :], []
        return resD[s, t * 128:(t + 1) * 128, :], [Bres[(s, t)]]

    class TileCtx:
        def __init__(self, nslots, nxs=2):
            self.slots = Rot([(sb.alloc([128, D], F32), Buf("slot%d" % i), new_dsem()) for i in range(nslots)])
            self.xs = Rot([(sb.alloc([128, D], BF16), Buf("xs%d" % i)) for i in range(nxs)])
            self.junks = Rot([(sb.alloc([128, D], BF16), Buf("junk%d" % i)) for i in range(2)])
            self.tbank = Rot([0, 1])

            self.base_slots = self.slots
            self.extra_flush = []

        def flush(self):
            return [it[1] for it in self.base_slots.items] + list(self.extra_flush)

    def make_gcbc(l, k, r):
        t_, b_ = gcbc.next()
        dg = tc_cur[0]
        for half in range(2):
            bank = 6 + half
            for q4 in range(4):
                kc = half * 4 + q4
                dt_, db_ = dg.next()
                ts("dve", dt_, ident_f[:, :], gcT[:, l, k, r, kc:kc + 1], None, ALU.mult, None, [Bconst, Bmod], [db_])
                mm(BK[bank][:, q4 * 128:(q4 + 1) * 128], ones_f[:, :], dt_, True, True, [db_, Bconst], [BKb[bank]])
        cp("dve", t_[:, :], pair(6), [BKb[6], BKb[7]], [b_])
        return t_, b_

    tc_cur = [None]

    def prepA(tc, stage, s, t):
        sl, slb, sld = tc.slots.next()
        src, srcb = src_tile(stage, s, t)
        dma("pool", sl[:, :], src, srcb, [slb], sld)
        st, stb = stat.next()
        jk, jkb = tc.junks.next()
        act(jk[:, :], sl[:, :], AF.Square, [slb], [jkb, stb], accum=st[:, 0:1])
        rstd_from(st[:, 0:1], st[:, 1:2], stb)
        xs, xsb = tc.xs.next()
        ts("dve", xs[:, :], sl[:, :], st[:, 1:2], None, ALU.mult, None, [slb, stb], [xsb])
        return xs, xsb

    def prepB(tc, l, k, r, xs, xsb, hT_ap, hTb):
        b = tc.tbank.next()
        pb = bank_bf(b)
        for kc in range(8):
            tr(pb[:, kc * 128:(kc + 1) * 128], xs[:, kc * 128:(kc + 1) * 128], ident_b[:, :], [xsb, Bconst], [BKb[b]])
        for kc in range(8):
            ts("dve", hT_ap[:, kc, :], pb[:, kc * 128:(kc + 1) * 128], gmT[:, l, k, r, kc:kc + 1],
               modT[:, l, 3 * k * 8 + kc, r:r + 1], ALU.mult, ALU.add, [BKb[b], Bmod], [hTb])

    def post_update(tc, stage, s, t, yb, gc_t, gc_b):
        st, stb = stat.next()
        jk, jkb = tc.junks.next()
        act(jk[:, 0:512], BK[yb], AF.Square, [BKb[yb]], [jkb, stb], accum=st[:, 0:1])
        act(jk[:, 512:1024], BK[yb + 1], AF.Square, [BKb[yb + 1]], [jkb, stb], accum=st[:, 1:2])
        tt("dve", st[:, 2:3], st[:, 0:1], st[:, 1:2], ALU.add, [stb], [stb])
        rstd_from(st[:, 2:3], st[:, 3:4], stb)
        sl, slb, sld = tc.slots.next()
        src, srcb = src_tile(stage, s, t)
        dma("pool", sl[:, :], src, srcb, [slb], sld)
        stt("dve", pair(yb), pair(yb), st[:, 3:4], gc_t[:, :], ALU.mult, ALU.mult, [BKb[yb], BKb[yb + 1], stb, gc_b], [BKb[yb], BKb[yb + 1]])
        tt("dve", sl[:, :], sl[:, :], pair(yb), ALU.add, [slb, BKb[yb], BKb[yb + 1]], [slb])
        dst, dstb = dst_tile(stage, s, t)
        dma("pool", dst, sl[:, :], [slb], dstb, sld)

    def prep_group(tc, stage, l, k, tiles, hT_full, hTbufs):
        G = len(tiles)
        st, stb = stat.next()
        sls = []
        for i, (s_, t) in enumerate(tiles):
            sl, slb, sld = tc.slots.next()
            src, srcb = src_tile(stage, s_, t)
            dma("pool", sl[:, :], src, srcb, [slb], sld)
            sls.append((sl, slb))
        for i, (sl, slb) in enumerate(sls):
            jk, jkb = tc.junks.next()
            act(jk[:, :], sl[:, :], AF.Square, [slb], [jkb, stb], accum=st[:, i:i + 1])
        rstd_from(st[:, 0:G], st[:, 2:2 + G], stb)
        for i, (s_, t) in enumerate(tiles):
            sl, slb = sls[i]
            xs, xsb = tc.xs.next()
            ts("dve", xs[:, :], sl[:, :], st[:, 2 + i:3 + i], None, ALU.mult, None, [slb, stb], [xsb])
            prepB(tc, l, k, role_of(s_, t), xs, xsb, hT_full[:, :, t * 128:(t + 1) * 128], hTbufs[t])

    def post_group(tc, stage, items):
        G = len(items)
        st, stb = stat.next()
        st2, stb2 = stat.next()
        for i, (s_, t, yb, gc_t, gc_b) in enumerate(items):
            jk, jkb = tc.junks.next()
            act(jk[:, 0:512], BK[yb], AF.Square, [BKb[yb]], [jkb, stb], accum=st[:, 2 * i:2 * i + 1])
            act(jk[:, 512:1024], BK[yb + 1], AF.Square, [BKb[yb + 1]], [jkb, stb], accum=st[:, 2 * i + 1:2 * i + 2])
        stv = st[:, 0:2 * G].rearrange("p (g two) -> p g two", two=2)
        tt("dve", st2[:, 0:G], stv[:, :, 0], stv[:, :, 1], ALU.add, [stb], [stb2])
        rstd_from(st2[:, 0:G], st2[:, 2:2 + G], stb2)
        sls = []
        for i, (s_, t, yb, gc_t, gc_b) in enumerate(items):
            sl, slb, sld = tc.slots.next()
            src, srcb = src_tile(stage, s_, t)
            dma("pool", sl[:, :], src, srcb, [slb], sld)
            sls.append((sl, slb, sld))
        for i, (s_, t, yb, gc_t, gc_b) in enumerate(items):
            sl, slb, sld = sls[i]
            stt("dve", pair(yb), pair(yb), st2[:, 2 + i:3 + i], gc_t[:, :], ALU.mult, ALU.mult, [BKb[yb], BKb[yb + 1], stb2, gc_b], [BKb[yb], BKb[yb + 1]])
            tt("dve", sl[:, :], sl[:, :], pair(yb), ALU.add, [slb, BKb[yb], BKb[yb + 1]], [slb])
            dst, dstb = dst_tile(stage, s_, t)
            dma("pool", dst, sl[:, :], [slb], dstb, sld)

    def out_proj_mm(lhs_of_kc, lhs_bufs, wo, wo_bufs, yb):
        for kc in range(8):
            for h in range(2):
                mm(BK[yb + h], lhs_of_kc(kc), wo[:, kc, h * 512:(h + 1) * 512], kc == 0, kc == 7, lhs_bufs + wo_bufs, [BKb[yb + h]])

    class DiagRot:
        def __init__(self):
            self.r = Rot([(sb.alloc([128, 128], F32), Buf("dg%d" % i)) for i in range(2)])

        def next(self):
            t_, b_ = self.r.next()
            return t_[:, :], b_

    def ffn_stage(stage, l, j, cast_bufs):
        k = 0 if j == 0 else 2
        m0 = sb.mark()
        tc = TileCtx(4)
        tc_cur[0] = DiagRot()
        wout = sb.alloc([128, NFC, D], BF16)
        hT_t = [sb.alloc([128, 8, 1024], BF16) for _ in range(2)]
        hTb = [[Buf("hT%d_%d" % (a, i)) for i in range(8)] for a in range(2)]
        actT = sb.alloc([128, NFC, 1024], BF16)
        actb = [Buf("act%d" % i) for i in range(NFC)]
        sg = Rot([(sb.alloc([128, 512], F32), Buf("sg%d" % i)) for i in range(2)])
        gu = Rot([(2, 3), (4, 5)])
        ybank = Rot([6, 0])
        Bwin, Bwout = cast_bufs
        wob = []
        for (k0, k1) in ((0, 6), (6, 12), (12, 17), (17, 22)):
            b_ = Buf("wout")
            dma("sp", wout[:, k0:k1, :], wout_s[l][j][:, k0:k1, :], [Bwout], [b_], new_dsem())
            wob.append((k0, k1, b_))

        def wob_of(fc):
            for (k0, k1, b_) in wob:
                if k0 <= fc < k1:
                    return b_
        sbs = []
        x_only = full and stage == 5
        if not x_only:
            ctx_tiles = [(s, t) for s in range(nseq) for t in range(2)]
            for i in range(0, len(ctx_tiles), 8):
                sbs.append(ctx_tiles[i:i + 8])
        for s in range(nseq):
            sbs.append([(s, t) for t in range(2, 10)])
            sbs.append([(s, t) for t in range(10, 18)])
        nsb = len(sbs)

        def do_prepA(bi, i):
            s, t = sbs[bi][i]
            return prepA(tc, stage, s, t)

        def do_prepB(bi, i, xs, xsb):
            s, t = sbs[bi][i]
            prepB(tc, l, k, role_of(s, t), xs, xsb, hT_t[bi % 2][:, :, i * 128:(i + 1) * 128], hTb[bi % 2][i])

        for i in range(len(sbs[0])):
            xs, xsb = do_prepA(0, i)
            do_prepB(0, i, xs, xsb)
        for bi in range(nsb):
            tiles = sbs[bi]
            ntl = len(tiles)
            ncols = ntl * 128
            hT = hT_t[bi % 2]
            hb = hTb[bi % 2]
            role = role_of(*tiles[0])
            gc_t, gc_b = make_gcbc(l, k, role)
            halves = [(c0, min(512, ncols - c0)) for c0 in range(0, ncols, 512)]
            pend = None
            for g in range(11):
                wt_, wb_, wd_ = wslot.next()
                dma("sp", wt_, win_s[l][j][g], [Bwin], [wb_], wd_)
                for c in range(2):
                    fc = 2 * g + c
                    for (c0, cn) in halves:
                        gb, ub = gu.next()
                        hbs = hb[c0 // 128:(c0 + cn) // 128]
                        for kc in range(8):
                            mm(BK[gb][:, 0:cn], wt_[:, kc, 0, c * 128:(c + 1) * 128], hT[:, kc, c0:c0 + cn], kc == 0, kc == 7, [wb_] + hbs, [BKb[gb]])
                        for kc in range(8):
                            mm(BK[ub][:, 0:cn], wt_[:, kc, 1, c * 128:(c + 1) * 128], hT[:, kc, c0:c0 + cn], kc == 0, kc == 7, [wb_] + hbs, [BKb[ub]])
                        sg_t, sg_b = sg.next()
                        act(sg_t[:, 0:cn], BK[gb][:, 0:cn], AF.Silu, [BKb[gb]], [sg_b])
                        tt("dve", actT[:, fc, c0:c0 + cn], sg_t[:, 0:cn], BK[ub][:, 0:cn], ALU.mult, [sg_b, BKb[ub]], [actb[fc]])
                if bi + 1 < nsb:
                    nn = len(sbs[bi + 1])
                    if pend is not None:
                        do_prepB(bi + 1, pend[0], pend[1], pend[2])
                        pend = None
                    if g < nn:
                        xs, xsb = do_prepA(bi + 1, g)
                        pend = (g, xs, xsb)
            if pend is not None:
                do_prepB(bi + 1, pend[0], pend[1], pend[2])
            for i, (s, t) in enumerate(tiles):
                yb = ybank.next()
                for fc in range(NFC):
                    for h in range(2):
                        mm(BK[yb + h], actT[:, fc, i * 128:(i + 1) * 128], wout[:, fc, h * 512:(h + 1) * 512],
                           fc == 0, fc == NFC - 1, [actb[fc], wob_of(fc)], [BKb[yb + h]])
                post_update(tc, stage, s, t, yb, gc_t, gc_b)
        P.barrier(flush=tc.flush())
        sb.reset(m0)

    def out_proj_tile(tc, stage, s, t, lhs_of_kc, lhs_bufs, wo, wo_bufs, gc_t, gc_b, yb):
        for kc in range(8):
            for h in range(2):
                mm(BK[yb + h], lhs_of_kc(kc), wo[:, kc, h * 512:(h + 1) * 512], kc == 0, kc == 7, lhs_bufs + wo_bufs, [BKb[yb + h]])
        post_update(tc, stage, s, t, yb, gc_t, gc_b)

    def load_wo(scr, castbuf):
        wo = wslots_t[:, 0:2].rearrange("p a k u c -> p (a k u c)").rearrange("p (k c) -> p k c", k=8)
        it0, it1 = wslot.items[0], wslot.items[1]
        o = P.op("sp", lambda e: e.dma_start(out=wo, in_=scr), R=[castbuf], W=[it0[1], it1[1]], dsem=it0[2])
        wslot.i = 2
        return wo, it0[1], it1[1]

    def rglru_stage(stage, l):
        m0 = sb.mark()
        tc = TileCtx(2, 1)
        tc_cur[0] = DiagRot()
        hT = sb.alloc([128, 8, NTOK], BF16)
        hTb = [Buf("hTf%d" % i) for i in range(NT)]
        prodT = sb.alloc([128, 8, NTOK], BF16)
        prb = [Buf("prod%d" % i) for i in range(8)]
        gw_t = sb.alloc([128, 1, 8, 256], BF16)
        gws = Rot([(gw_t[:, i], Buf("gw%d" % i), new_dsem()) for i in range(1)])
        XW = 2310
        xr = sb.alloc([128, 2, XW], F32)
        xrb = [Buf("xr0"), Buf("xr1")]
        rx = sb.alloc([128, 2, XW], F32)
        rxb_t = sb.alloc([128, 2, XW], BF16)
        rxB = [Buf("rx0"), Buf("rx1")]
        AB = Rot([(sb.alloc([128, 2, 512], F32), sb.alloc([128, 2, 512], F32), Buf("ab%d" % i)) for i in range(2)])
        q_t = sb.alloc([128, 2, 512], F32)
        q_b = Buf("q")
        SB_ = Rot([(sb.alloc([128, 2, 512], F32), Buf("sbk%d" % i)) for i in range(2)])
        carry = sb.alloc([128, 2], F32)
        Bcarry = Buf("carry")
        alias_slots = [(rx[:, 0, 0:1024], rxB[0], new_dsem()), (rx[:, 1, 0:1024], rxB[1], new_dsem())]
        tc.extra_flush = [rxB[0], rxB[1]]
        P.op("pool", lambda e: e.memset(xr[:, :, :], 0.0), W=xrb)
        ranges = [(0, 256, 2, 0)] + [(256 + 512 * i, 512, 261 + 512 * i, 259 + 512 * i) for i in range(4)]
        pb = Rot([0, 1, 6, 7])
        gb4 = Rot([(2, 3), (4, 5)])
        def prep_tile(s_, t):
            xs, xsb = prepA(tc, stage, s_, t)
            prepB(tc, l, 1, role_of(s_, t), xs, xsb, hT[:, :, t * 128:(t + 1) * 128], hTb[t])

        for s in range(nseq):
            if s == 0:
                for t in range(0, NT, 2):
                    prep_group(tc, stage, l, 1, [(s, t), (s, t + 1)], hT, hTb)
            gc_c = make_gcbc(l, 1, nseq)
            gc_x = make_gcbc(l, 1, s)
            for n in range(4):
                wt_, wb_, wd_ = wslot.next()
                dma("sp", wt_, rgin_s[n], [Bc["rgin"]], [wb_], wd_)
                gt_, gbuf_, gd_ = gws.next()
                dma("sp", gt_, rggw_s[n].rearrange("p d g j k -> p (d g j) k"), [Bc["rggw"]], [gbuf_], gd_)
                for c in range(2):
                    gch = 2 * n + c
                    for (h0, cn, dc0, u0) in ranges:
                        hbs = hTb[h0 // 128:(h0 + cn) // 128]
                        b = pb.next()
                        for kc in range(8):
                            mm(BK[b][:, 0:cn], wt_[:, kc, 0, c * 128:(c + 1) * 128], hT[:, kc, h0:h0 + cn], kc == 0, kc == 7, [wb_] + hbs, [BKb[b]])
                        act(prodT[:, gch, h0:h0 + cn], BK[b][:, 0:cn], AF.Gelu, [BKb[b]], [prb[gch]])
                        b = pb.next()
                        for kc in range(8):
                            mm(BK[b][:, 0:cn], wt_[:, kc, 1, c * 128:(c + 1) * 128], hT[:, kc, h0:h0 + cn], kc == 0, kc == 7, [wb_] + hbs, [BKb[b]])
                        act(xr[:, c, dc0:dc0 + cn], BK[b][:, 0:cn], AF.Identity, [BKb[b]], [xrb[c]])
                    NU = 2307
                    cw = [smallT[:, O_CW + kk * 8 + gch:O_CW + kk * 8 + gch + 1] for kk in range(4)]
                    cb = smallT[:, O_CB + gch:O_CB + gch + 1]
                    ts("pool", rx[:, c, 0:NU], xr[:, c, 0:NU], cw[0], cb, ALU.mult, ALU.add, [xrb[c], Bmod], [rxB[c]])
                    for kk in range(1, 4):
                        stt("dve", rx[:, c, 0:NU], xr[:, c, kk:kk + NU], cw[kk], rx[:, c, 0:NU], ALU.mult, ALU.add, [xrb[c], rxB[c], Bmod], [rxB[c]])
                    cp("pool", rxb_t[:, c, 0:NU], rx[:, c, 0:NU], [rxB[c]], [rxB[c]])
                for d_ in range(2):
                    order = ranges if d_ == 0 else [ranges[0]] + ranges[:0:-1]
                    for ri, (h0, cn, dc0, u0) in enumerate(order):
                        a_t, i_t, ab_b = AB.next()
                        for k2 in range(2):
                            gch = 2 * n + k2
                            rb, ib = ((2, 3), (4, 5))[k2]
                            for g_ in range(2):
                                bb = rb if g_ == 0 else ib
                                for jc in range(2):
                                    mm(BK[bb][:, 0:cn], gt_[:, (d_ * 2 + g_) * 2 + jc, k2 * 128:(k2 + 1) * 128], rxb_t[:, jc, u0:u0 + cn],
                                       jc == 0, jc == 1, [gbuf_, rxB[jc]], [BKb[bb]])
                            hb_ = [hgb[:, (d_ * 2 + g_) * 8 + gch:(d_ * 2 + g_) * 8 + gch + 1] for g_ in range(2)]
                            act(a_t[:, k2, 0:cn], BK[rb][:, 0:cn], AF.Tanh, [BKb[rb], Bmod], [ab_b], bias=hb_[0], scale=0.5)
                            act(i_t[:, k2, 0:cn], BK[ib][:, 0:cn], AF.Tanh, [BKb[ib], Bmod], [ab_b], bias=hb_[1], scale=0.5)
                            hc_ = hcoef[:, d_ * 8 + gch:d_ * 8 + gch + 1]
                            act(a_t[:, k2, 0:cn], a_t[:, k2, 0:cn], AF.Exp, [ab_b, Bmod], [ab_b], scale=hc_, bias=hc_)
                        act(q_t[:, :, 0:cn], a_t[:, :, 0:cn], AF.Square, [ab_b], [q_b])
                        act(q_t[:, :, 0:cn], q_t[:, :, 0:cn], AF.Sqrt, [q_b, Bmod], [q_b], scale=-0.25, bias=quartT[:, 0:1])
                        stt("dve", i_t[:, :, 0:cn], i_t[:, :, 0:cn], 1.0, rx[:, :, u0:u0 + cn], ALU.add, ALU.mult, [ab_b, rxB[0], rxB[1]], [ab_b])
                        tt("pool", i_t[:, :, 0:cn], i_t[:, :, 0:cn], q_t[:, :, 0:cn], ALU.mult, [ab_b, q_b], [ab_b])
                        if d_ == 0:
                            for k2 in range(2):
                                if ri == 0:
                                    init = 0.0
                                else:
                                    pu = 255 if ri == 1 else u0 - 1
                                    init = xr[:, k2, pu:pu + 1]
                                P.op("dve", lambda e, o_=xr[:, k2, u0:u0 + cn], a_=a_t[:, k2, 0:cn], b_=i_t[:, k2, 0:cn], in_=init:
                                     e.tensor_tensor_scan(out=o_, data0=a_, data1=b_, initial=in_, op0=ALU.mult, op1=ALU.add),
                                     R=[ab_b, xrb[k2]], W=[xrb[k2]])
                        else:
                            s_t, s_b = SB_.next()
                            for k2 in range(2):
                                init = 0.0 if ri == 0 else carry[:, k2:k2 + 1]
                                P.op("dve", lambda e, o_=s_t[:, k2, 0:cn][:, ::-1], a_=a_t[:, k2, 0:cn][:, ::-1], b_=i_t[:, k2, 0:cn][:, ::-1], in_=init:
                                     e.tensor_tensor_scan(out=o_, data0=a_, data1=b_, initial=in_, op0=ALU.mult, op1=ALU.add),
                                     R=[ab_b, Bcarry], W=[s_b])
                                cp("dve", carry[:, k2:k2 + 1], s_t[:, k2, 0:1], [s_b], [Bcarry])
                            tt("pool", s_t[:, :, 0:cn], s_t[:, :, 0:cn], xr[:, :, u0:u0 + cn], ALU.add, [s_b, xrb[0], xrb[1]], [s_b])
                            tt("dve", prodT[:, 2 * n:2 * n + 2, h0:h0 + cn], prodT[:, 2 * n:2 * n + 2, h0:h0 + cn], s_t[:, :, 0:cn], ALU.mult,
                               [prb[2 * n], prb[2 * n + 1], s_b], [prb[2 * n], prb[2 * n + 1]])
                for c in range(2):
                    for (p0, pn) in ((0, 2), (258, 3), (2309, 1)):
                        P.op("pool", lambda e, a_=xr[:, c, p0:p0 + pn]: e.memset(a_, 0.0), W=[xrb[c]])
            if s == 0:
                print("rglru stage sbuf end", sb.off, flush=True)
            wo, wb0, wb1 = load_wo(rgwo_s, Bc["rgwo"])
            yb = Rot([6, 2, 4])
            tc.slots = Rot(list(tc.base_slots.items) + alias_slots)
            for t in range(0, NT, 2):
                g_ = gc_c if t < 2 else gc_x
                items = []
                for tt_ in (t, t + 1):
                    ybk = yb.next()
                    out_proj_mm(lambda kc, tt_=tt_: prodT[:, kc, tt_ * 128:(tt_ + 1) * 128], prb, wo, [wb0, wb1], ybk)
                    items.append((s, tt_, ybk, g_[0], g_[1]))
                post_group(tc, stage, items)
                if s + 1 < nseq:
                    prep_group(tc, stage, l, 1, [(s + 1, t), (s + 1, t + 1)], hT, hTb)
            tc.slots = tc.base_slots
        P.barrier(flush=tc.flush())
        sb.reset(m0)

    def attn_stage(stage, l):
        m0 = sb.mark()
        tc = TileCtx(2)
        tc_cur[0] = DiagRot()
        hT = sb.alloc([128, 8, NTOK], BF16)
        hTb = [Buf("hTf%d" % i) for i in range(NT)]
        QT = sb.alloc([128, 8, SEQ], BF16)
        Qb = [Buf("q%d" % i) for i in range(8)]
        alias_slots = [(QT[:, j, :].bitcast(F32), Qb[j], new_dsem()) for j in range(2)]
        tc.extra_flush = list(Qb)
        Kd = sb.alloc([128, 4, 2, NTOK], BF16)
        Kb = [Buf("k%d" % i) for i in range(4)]
        Va = sb.alloc([128, NT, 4, 66], BF16)
        Vb = [Buf("v%d" % i) for i in range(NT)]
        cosT = sb.alloc([128, SEQ], F32)
        sinT = sb.alloc([128, SEQ], F32)
        Brope = Buf("rope")
        PT = Rot([(sb.alloc([128, 5, 4, 128], BF16), Buf("pt%d" % i)) for i in range(2)])
        rt = Rot([(sb.alloc([128, 512], F32), Buf("rt%d" % i)) for i in range(2)])
        attn = Rot([(sb.alloc([128, D], BF16), Buf("at%d" % i)) for i in range(2)])
        den = Rot([(sb.alloc([128, 8], F32), Buf("den%d" % i)) for i in range(2)])
        dma("sp", cosT[:, :], ropec, [], [Brope], new_dsem())
        dma("sp", sinT[:, :], ropes, [], [Brope], new_dsem())
        P.op("pool", lambda e: e.memset(Va[:, :, :, 64:66], 1.0), W=Vb)
        P.op("pool", lambda e: e.memset(Kd[:, :, :, :], 0.0), W=Kb)
        pb2 = Rot([(0, 1), (2, 3)])
        pb1 = Rot([0, 1, 2, 3])
        def prep_tile(s_, t):
            xs, xsb = prepA(tc, stage, s_, t)
            prepB(tc, l, 1, role_of(s_, t), xs, xsb, hT[:, :, t * 128:(t + 1) * 128], hTb[t])

        for s in range(nseq):
            if s == 0:
                for t in range(0, NT, 2):
                    prep_group(tc, stage, l, 1, [(s, t), (s, t + 1)], hT, hTb)
            gc_x = make_gcbc(l, 1, s)
            for g in range(6):
                wt_, wb_, wd_ = wslot.next()
                dma("sp", wt_, wqk_s[g], [Bc["wqk"]], [wb_], wd_)
                for c in range(2):
                    if g < 4:
                        dst_of = lambda x0, cn, j=2 * g + c: [(0, 128, QT[:, j, x0:x0 + cn])]
                        dbuf = Qb[2 * g + c]
                    else:
                        kv = 2 * (g - 4) + c
                        dst_of = lambda x0, cn, kv=kv: [(0, 64, Kd[0:64, kv, 0, 256 + x0:256 + x0 + cn]),
                                                        (64, 128, Kd[64:128, kv, 1, 256 + x0:256 + x0 + cn])]
                        dbuf = Kb[kv]
                        b = pb1.next()
                        for kc in range(8):
                            mm(BK[b][:, 0:256], wt_[:, kc, 0, c * 128:(c + 1) * 128], hT[:, kc, 0:256], kc == 0, kc == 7, [wb_] + hTb[0:2], [BKb[b]])
                        cp("dve", Kd[0:64, kv, 0, 0:256], BK[b][0:64, 0:256], [BKb[b]], [dbuf])
                        cp("dve", Kd[64:128, kv, 1, 0:256], BK[b][64:128, 0:256], [BKb[b]], [dbuf])
                    for xi in range(4):
                        x0 = xi * 512
                        h0 = 256 + x0
                        hbs = hTb[h0 // 128:(h0 + 512) // 128]
                        bp, bs_ = pb2.next()
                        for kc in range(8):
                            mm(BK[bp], wt_[:, kc, 0, c * 128:(c + 1) * 128], hT[:, kc, h0:h0 + 512], kc == 0, kc == 7, [wb_] + hbs, [BKb[bp]])
                        for kc in range(8):
                            mm(BK[bs_], wt_[:, kc, 1, c * 128:(c + 1) * 128], hT[:, kc, h0:h0 + 512], kc == 0, kc == 7, [wb_] + hbs, [BKb[bs_]])
                        t1, t1b = rt.next()
                        t2, t2b = rt.next()
                        tt("dve", t1[:, :], BK[bp], cosT[:, x0:x0 + 512], ALU.mult, [BKb[bp], Brope], [t1b])
                        tt("dve", t2[:, :], BK[bs_], sinT[:, x0:x0 + 512], ALU.mult, [BKb[bs_], Brope], [t2b])
                        for (p0, p1, dap) in dst_of(x0, 512):
                            tt("pool", dap, t1[p0:p1, :], t2[p0:p1, :], ALU.add, [t1b, t2b], [dbuf])
            wv_t, Bwv, wvd_ = wslot.next()
            dma("sp", wv_t[:, :, 0, :], wv_s, [Bc["wv"]], [Bwv], wvd_)
            for t in range(NT):
                b = pb1.next()
                for kc in range(8):
                    mm(BK[b][:, 0:256], hT[:, kc, t * 128:(t + 1) * 128], wv_t[:, kc, 0, :], kc == 0, kc == 7, [hTb[t], Bwv], [BKb[b]])
                act(Va[:, t, :, 0:64], BK[b][:, 0:256].rearrange("p (k d) -> p k d", k=4), AF.Copy, [BKb[b]], [Vb[t]])
            wo, wb0, wb1 = load_wo(awo_s, Bc["awo"])
            spair = Rot([0, 2])
            obank = Rot([4, 5])

            def chunks_of(qb):
                tq = 2 + qb
                ch = [(0, None), (1, None)]
                if qb > 0:
                    ch.append((tq - 1, mask_prev))
                ch.append((tq, None))
                if qb < 15:
                    ch.append((tq + 1, mask_next))
                return ch

            def emit_scores(qb, kv):
                chunks = chunks_of(qb)
                pt_t, pt_b = PT.next()
                ci = 0
                while ci < len(chunks):
                    n2 = min(2 if os.environ.get("ATT_EXP2", "1") == "1" else 1, len(chunks) - ci)
                    b0 = spair.next()
                    for cc in range(n2):
                        kt, msk = chunks[ci + cc]
                        first = True
                        if msk is not None:
                            mm(BK[b0 + cc][:, 0:512], ident_b[:, :], msk[:, :, :].rearrange("p g q -> p (g q)"), True, False, [Bconst], [BKb[b0 + cc]])
                            first = False
                        for gq in range(4):
                            h = 4 * kv + gq
                            mm(BK[b0 + cc][:, gq * 128:(gq + 1) * 128], Kd[:, kv, h % 2, kt * 128:(kt + 1) * 128],
                               QT[:, h // 2, qb * 128:(qb + 1) * 128], first, gq == 3, [Kb[kv], Qb[h // 2]], [BKb[b0 + cc]])
                            first = False
                    act(pt_t[:, ci:ci + n2].rearrange("p c g q -> p (c g q)"), ps[:, b0 * 512:(b0 + n2) * 512], AF.Exp,
                        [BKb[b0 + cc_] for cc_ in range(n2)], [pt_b], scale=0.125)
                    ci += n2
                return (qb, kv, chunks, pt_t, pt_b)

            cur_at = [None]
            tbk = Rot([6, 7])

            def emit_pv(item):
                qb, kv, chunks, pt_t, pt_b = item
                nch = len(chunks)
                tq = 2 + qb
                if kv == 0:
                    cur_at[0] = attn.next()
                at_t, at_b = cur_at[0]
                ob = obank.next()
                for gq in range(4):
                    for ci, (kt, msk) in enumerate(chunks):
                        mm(BK[ob][:, gq * 65:(gq + 1) * 65], pt_t[:, ci, gq, :], Va[:, kt, kv, 0:65], ci == 0, ci == nch - 1, [pt_b, Vb[kt]], [BKb[ob]])
                dn_t, dn_b = den.next()
                ov_ = BK[ob][:, 0:260].rearrange("p (g d) -> p g d", d=65)
                tt("dve", dn_t[:, 0:4], ov_[:, :, 64], esink[:, 4 * kv:4 * kv + 4], ALU.add, [BKb[ob], Bmod], [dn_b])
                P.op("dve", lambda e, a_=dn_t[:, 4:8], b_=dn_t[:, 0:4]: e.reciprocal(out=a_, in_=b_), R=[dn_b], W=[dn_b])
                if os.environ.get("ATT_BCAST", "1") == "1":
                    tt("dve", at_t[:, kv * 256:(kv + 1) * 256].rearrange("p (g d) -> p g d", d=64), ov_[:, :, 0:64],
                       dn_t[:, 4:8].unsqueeze(2).to_broadcast([128, 4, 64]), ALU.mult, [BKb[ob], dn_b], [at_b])
                else:
                    for gq in range(4):
                        h = 4 * kv + gq
                        ts("dve", at_t[:, h * 64:(h + 1) * 64], ov_[:, gq, 0:64], dn_t[:, 4 + gq:5 + gq], None, ALU.mult, None, [BKb[ob], dn_b], [at_b])
                if kv == 3:
                    ybk = tbk.next()
                    pbt = bank_bf(ybk)
                    for kc in range(8):
                        tr(pbt[:, kc * 128:(kc + 1) * 128], at_t[:, kc * 128:(kc + 1) * 128], ident_b[:, :], [at_b, Bconst], [BKb[ybk]])
                    cp("dve", hT[:, :, tq * 128:(tq + 1) * 128], pbt[:, 0:1024].rearrange("p (k q) -> p k q", k=8), [BKb[ybk]], [hTb[tq]])

            prev = None
            for qb in range(16):
                for kv in range(4):
                    cur = emit_scores(qb, kv)
                    if os.environ.get("ATT_PIPE", "1") != "1":
                        emit_pv(cur)
                        continue
                    if prev is not None:
                        emit_pv(prev)
                    prev = cur
            if prev is not None:
                emit_pv(prev)
            ybr = Rot([6, 2, 4])
            tc.slots = Rot(list(tc.base_slots.items) + alias_slots)
            if s + 1 < nseq:
                prep_group(tc, stage, l, 1, [(s + 1, 0), (s + 1, 1)], hT, hTb)
            for tq in range(2, NT, 2):
                items = []
                for tt_ in (tq, tq + 1):
                    ybk = ybr.next()
                    out_proj_mm(lambda kc, tt_=tt_: hT[:, kc, tt_ * 128:(tt_ + 1) * 128], [hTb[tt_]], wo, [wb0, wb1], ybk)
                    items.append((s, tt_, ybk, gc_x[0], gc_x[1]))
                post_group(tc, stage, items)
                if s + 1 < nseq:
                    prep_group(tc, stage, l, 1, [(s + 1, tq), (s + 1, tq + 1)], hT, hTb)
            tc.slots = tc.base_slots
        print("attn stage sbuf end", sb.off, flush=True)
        P.barrier(flush=tc.flush())
        sb.reset(m0)

    if nstages > 0:
        ffn_stage(0, 0, 0, Bc["ffn00"])
    if nstages > 1:
        rglru_stage(1, 0)
    if nstages > 2:
        ffn_stage(2, 0, 1, Bc["ffn01"])
    if nstages > 3:
        ffn_stage(3, 1, 0, Bc["ffn10"])
    if nstages > 4:
        attn_stage(4, 1)
    if nstages > 5:
        ffn_stage(5, 1, 1, Bc["ffn11"])

    if not full:
        dcp = new_dsem()
        for s in range(nseq):
            dma("sp", outp[s], resD[s], [Bres[(s, t)] for t in range(NT)], [], dcp)
    def final(e):
        for d in all_ds:
            if d.count:
                e.wait_ge(d.h, d.count)
        return e.nop()
    P.op("sp", final)

    run = P.emit(esems)
    with nc.Block() as block:
        @block.tensor
        def _(e):
            run("pe", e)

        @block.scalar
        def _(e):
            run("act", e)

        @block.vector
        def _(e):
            run("dve", e)

        @block.gpsimd
        def _(e):
            run("pool", e)

        @block.sync
        def _(e):
            run("sp", e)
    print("ops:", {k: len(v) for k, v in P.q.items()}, "marked:", {k: sum(1 for o in v if o.marked) for k, v in P.q.items()},
          "waits:", {k: sum(len(o.waits) for o in v) for k, v in P.q.items()}, "sbuf peak", sb.peak, "dsems", nds[0], flush=True)
    return nc


def rope_tables():
    inv = 1.0 / (10000.0 ** (np.arange(0, 32, 2, dtype=np.float32) / 32.0))
    pos = np.arange(SEQ)
    row = (pos // 64).astype(np.float32)
    col = (pos % 64).astype(np.float32)
    cosT = np.zeros((128, SEQ), np.float32)
    sinT = np.zeros((128, SEQ), np.float32)
    for p in range(128):
        d = p % 64
        base = row if d < 32 else col
        ang = (base * inv[d % 16]).astype(np.float32)
        cosT[p] = np.cos(ang)
        sg = -1.0 if (d % 32) < 16 else 1.0
        sinT[p] = sg * np.sin(ang)
    return cosT, sinT


def host_layout(inputs, nseq, core):
    f = lambda a: np.ascontiguousarray(np.asarray(a, dtype=np.float32))
    x = f(inputs["x"])
    ctx = f(inputs["ctx"])
    c = f(inputs["c"])
    b0 = core * nseq
    xin = np.concatenate([ctx[b0:b0 + nseq], x[b0:b0 + nseq]], axis=1)
    cond = np.concatenate([c[b0:b0 + nseq], f(inputs["c_ctx"])[None]], axis=0).reshape(-1, 128)
    small = np.concatenate([
        f(inputs["b_mod"]).reshape(-1, 128), f(inputs["norm_g"]).reshape(-1, 128),
        f(inputs["rg_conv_w"]).reshape(-1, 128), f(inputs["rg_conv_b"]).reshape(-1, 128),
        f(inputs["rg_gate_b"]).reshape(-1, 128), f(inputs["rg_lambda"]).reshape(-1, 128)], axis=0)
    assert small.shape[0] == NV
    wqkv = f(inputs["attn_w_qkv"])[0]
    wq = wqkv[:, :1024]
    wk = wqkv[:, 1024:1280]
    kdup = np.concatenate([np.concatenate([wk[:, kv * 64:(kv + 1) * 64]] * 2, axis=1) for kv in range(4)], axis=1)
    plain = np.concatenate([wq, kdup], axis=1)
    dd = np.arange(64)
    part = np.where((dd % 32) < 16, dd + 16, dd - 16)
    perm = (np.arange(1536) // 64) * 64 + part[np.arange(1536) % 64]
    sw = plain[:, perm]
    wqk2 = np.ascontiguousarray(np.stack([plain, sw], axis=1))
    cosT, sinT = rope_tables()
    return {
        "xin": np.ascontiguousarray(xin), "cond": np.ascontiguousarray(cond), "small": small,
        "sink": f(inputs["attn_sink"]).reshape(1, 16), "ropec": cosT, "ropes": sinT,
        "w_mod": f(inputs["w_mod"]), "ffn_w_in": f(inputs["ffn_w_in"]), "ffn_w_out": f(inputs["ffn_w_out"]),
        "rg_w_in": f(inputs["rg_w_in"])[0], "rg_gate_w": f(inputs["rg_gate_w"])[0], "rg_w_out": f(inputs["rg_w_out"])[0],
        "wqk2": wqk2, "wv": np.ascontiguousarray(wqkv[:, 1280:1536]), "attn_w_o": f(inputs["attn_w_o"])[0],
    }


def kernel(**inputs):
    nseq = 32 // NCORES
    nc = build_program(nseq, 6)
    in_maps = [host_layout(inputs, nseq, core) for core in range(NCORES)]
    res = run_bass_kernel_spmd(nc, in_maps, core_ids=list(range(NCORES)))
    out = np.concatenate([np.asarray(r["out"]) for r in res.results], axis=0)
    return out.astype(np.float32)
```

```python
import os
import numpy as np
import concourse.bass as bass
import concourse.mybir as mybir
from concourse.bass_utils import run_bass_kernel_spmd

F32 = mybir.dt.float32
BF16 = mybir.dt.bfloat16
AF = mybir.ActivationFunctionType
ALU = mybir.AluOpType

D = 1024
SEQ = 2048
CTX = 256
NTOK = SEQ + CTX
NT = NTOK // 128
DFF = 2816
NFC = DFF // 128
NCORES = 8
EPS = 1e-6
NV = 328
O_BMOD, O_G, O_CW, O_CB, O_GB, O_LAM = 0, 144, 240, 272, 280, 312


class Buf:
    __slots__ = ("name", "lw", "rd")

    def __init__(self, name=""):
        self.name = name
        self.lw = None
        self.rd = []


class DSem:
    __slots__ = ("h", "count", "key")

    def __init__(self, h, key):
        self.h = h
        self.count = 0
        self.key = key


class Op:
    __slots__ = ("eng", "idx", "fn", "waits", "marked", "clock", "dsem", "dval", "semval")

    def __init__(self, eng, idx, fn):
        self.eng = eng
        self.idx = idx
        self.fn = fn
        self.waits = []
        self.marked = False
        self.clock = None
        self.dsem = None
        self.dval = 0
        self.semval = 0


class Prog:
    ENG = ("pe", "act", "dve", "pool", "sp")

    def __init__(self, nc):
        self.nc = nc
        self.q = {k: [] for k in self.ENG}
        self.clock = {k: {} for k in self.ENG}

    @staticmethod
    def _kv(op):
        if op.dsem is not None:
            return op.dsem.key, op.dval
        return op.eng, op.idx + 1

    def op(self, eng, fn, R=(), W=(), dsem=None, extra=()):
        q = self.q[eng]
        o = Op(eng, len(q), fn)
        is_dma = dsem is not None
        if is_dma:
            dsem.count += 16
            o.dsem = dsem
            o.dval = dsem.count
        deps = []
        for b in R:
            if b.lw is not None:
                deps.append((b.lw, 0))
        for b in W:
            if b.lw is not None:
                deps.append((b.lw, 1))
            for r in b.rd:
                deps.append((r, 2))
        for d in extra:
            deps.append((d, 0))
        clk = self.clock[eng]
        best = {}
        for (d, raw) in deps:
            if d is o:
                continue
            d_dma = d.dsem is not None
            if (not d_dma) and d.eng == eng and not is_dma:
                if eng == "pe" or (eng != "pool" and raw == 2):
                    continue
            k, v = self._kv(d)
            if clk.get(k, 0) >= v:
                continue
            cur = best.get(k)
            if cur is None or cur[0] < v:
                best[k] = (v, d)
        for k, (v, d) in best.items():
            if clk.get(k, 0) >= v:
                continue
            o.waits.append(d)
            d.marked = True
            for kk, vv in d.clock.items():
                if clk.get(kk, 0) < vv:
                    clk[kk] = vv
        c = dict(clk)
        k, v = self._kv(o)
        c[k] = v
        o.clock = c
        for b in R:
            b.rd.append(o)
        for b in W:
            b.lw = o
            b.rd = []
        q.append(o)
        return o

    def barrier(self, flush=()):
        lasts = []
        for e in self.ENG:
            q = self.q[e]
            for o in reversed(q):
                if o.dsem is None:
                    lasts.append(o)
                    break
        fl = list(flush)
        for e in self.ENG:
            self.op(e, lambda en: en.nop(), W=fl if e == "sp" else (), extra=[o for o in lasts if o.eng != e])
        spl = self.q["sp"][-1]
        for e in self.ENG:
            if e != "sp":
                self.op(e, lambda en: en.nop(), extra=[spl])

    def emit(self, sems):
        for e, q in self.q.items():
            n = 0
            for o in q:
                if o.dsem is None and o.marked:
                    n += 1
                    o.semval = n

        def run(ename, engine):
            for o in self.q[ename]:
                for d in o.waits:
                    if d.dsem is not None:
                        engine.wait_ge(d.dsem.h, d.dval)
                    else:
                        engine.wait_ge(sems[d.eng], d.semval)
                ins = o.fn(engine)
                if o.dsem is not None:
                    ins.then_inc(o.dsem.h, 16)
                elif o.marked:
                    ins.then_inc(sems[ename], 1)
        return run


class Rot:
    def __init__(self, items):
        self.items = items
        self.i = 0

    def next(self):
        it = self.items[self.i % len(self.items)]
        self.i += 1
        return it


class SBAlloc:
    LO = 16640
    HI = 229376

    def __init__(self, nc):
        self.nc = nc
        self.off = self.LO
        self.n = 0
        self.peak = 0

    def alloc(self, shape, dt):
        esz = 4 if dt == F32 else 2
        nb = esz
        for s in shape[1:]:
            nb *= s
        off = (self.off + 63) // 64 * 64
        assert off + nb <= self.HI, "SBUF overflow: need %d at %d" % (nb, off)
        self.off = off + nb
        self.peak = max(self.peak, self.off)
        self.n += 1
        return self.nc.alloc_sbuf_tensor_at("t%d" % self.n, list(shape), dt, offset=off)

    def mark(self):
        return self.off

    def reset(self, m):
        self.off = m


def build_program(nseq, nstages=6):
    nc = bass.Bass("TRN2", target_bir_lowering=False)
    R = nseq + 1
    full = nstages >= 6

    def din(name, shape, dt=F32):
        return nc.dram_tensor(name, list(shape), dt, kind="ExternalInput").ap()

    def dscr(name, shape, dt=BF16):
        return nc.dram_tensor(name, list(shape), dt, kind="Internal").ap()

    xin = din("xin", [nseq, NTOK, D])
    cond = din("cond", [R * 8, 128])
    small = din("small", [NV, 128])
    sink = din("sink", [1, 16])
    ropec = din("ropec", [128, SEQ])
    ropes = din("ropes", [128, SEQ])
    w_mod = din("w_mod", [2, D, 9 * D])
    ffn_w_in = din("ffn_w_in", [2, 2, D, 2 * DFF])
    ffn_w_out = din("ffn_w_out", [2, 2, DFF, D])
    rg_w_in = din("rg_w_in", [D, 2 * D])
    rg_gate_w = din("rg_gate_w", [2, 2, 4, 256, 256])
    rg_w_out = din("rg_w_out", [D, D])
    wqk2 = din("wqk2", [D, 2, 1536])
    wv = din("wv", [D, 256])
    attn_w_o = din("attn_w_o", [D, D])
    if full:
        outp = nc.dram_tensor("out", [nseq, SEQ, D], F32, kind="ExternalOutput").ap()
    else:
        outp = nc.dram_tensor("dbg", [nseq, NTOK, D], F32, kind="ExternalOutput").ap()
    resD = dscr("resD", [nseq, NTOK, D], F32)
    win_s = [[dscr("win_s%d%d" % (l, j), [11, 128, 8, 2, 256]) for j in range(2)] for l in range(2)]
    wout_s = [[dscr("wout_s%d%d" % (l, j), [128, NFC, D]) for j in range(2)] for l in range(2)]
    rgin_s = dscr("rgin_s", [4, 128, 8, 2, 256])
    rggw_s = dscr("rggw_s", [4, 128, 2, 2, 2, 256])
    rgwo_s = dscr("rgwo_s", [128, 8, D])
    wqk_s = dscr("wqk_s", [6, 128, 8, 2, 256])
    wv_s = dscr("wv_s", [128, 8, 256])
    awo_s = dscr("awo_s", [128, 8, D])

    P = Prog(nc)
    sb = SBAlloc(nc)
    sem_handles = []

    def new_sem(name):
        h = nc.alloc_semaphore(name)
        sem_handles.append(h)
        return h

    nds = [0]

    all_ds = []

    def new_dsem():
        nds[0] += 1
        d = DSem(new_sem("d%d" % nds[0]), "d%d" % nds[0])
        all_ds.append(d)
        return d

    esems = {k: new_sem("s_" + k) for k in Prog.ENG}

    ps = nc.alloc_psum_tensor("ps", [128, 4096], F32)
    BK = [ps[:, b * 512:(b + 1) * 512] for b in range(8)]
    BKb = [Buf("bank%d" % b) for b in range(8)]

    def bank_bf(b):
        return BK[b].bitcast(BF16)

    def pair(b):
        return ps[:, b * 512:(b + 2) * 512]

    ident_b = sb.alloc([128, 128], BF16)
    ident_f = sb.alloc([128, 128], F32)
    ones_f = sb.alloc([128, 128], F32)
    mask_prev = sb.alloc([128, 4, 128], BF16)
    mask_next = sb.alloc([128, 4, 128], BF16)
    smallT = sb.alloc([128, NV], F32)
    sT = sb.alloc([128, R * 8], F32)
    modT = sb.alloc([128, 2, 72, R], F32)
    gmT = sb.alloc([128, 2, 3, R, 8], F32)
    gcT = sb.alloc([128, 2, 3, R, 8], F32)
    coef = sb.alloc([128, 16], F32)
    esink = sb.alloc([128, 16], F32)
    hgb = sb.alloc([128, 32], F32)
    hcoef = sb.alloc([128, 16], F32)
    quartT = sb.alloc([128, 1], F32)
    epsT = sb.alloc([128, 1], F32)
    oneT = sb.alloc([128, 1], F32)
    wslots_t = sb.alloc([128, 3, 8, 2, 256], BF16)
    wslot = Rot([(wslots_t[:, i], Buf("ws%d" % i), new_dsem()) for i in range(3)])
    ws_bufs = [it[1] for it in wslot.items]
    gcbc = Rot([(sb.alloc([128, D], F32), Buf("gcbc%d" % i)) for i in range(2)])
    stat_t = sb.alloc([128, 16, 4], F32)
    stat = Rot([(stat_t[:, i, :], Buf("st%d" % i)) for i in range(16)])
    Bconst = Buf("const")
    Bmod = Buf("mod")
    persist_mark = sb.mark()

    Bres = {(s, t): Buf("res%d_%d" % (s, t)) for s in range(nseq) for t in range(NT)}

    def mm(out, lhsT, rhs, start, stop, R_, W_):
        return P.op("pe", lambda e: e.matmul(out, lhsT=lhsT, rhs=rhs, start=start, stop=stop), R=R_, W=W_)

    def tr(out, in_, ident, R_, W_):
        return P.op("pe", lambda e: e.transpose(out=out, in_=in_, identity=ident), R=R_, W=W_)

    def act(out, in_, func, R_, W_, bias=None, scale=None, accum=None):
        kw = {}
        if bias is not None:
            kw["bias"] = bias
        if scale is not None:
            kw["scale"] = scale
        if accum is not None:
            kw["accum_out"] = accum
        return P.op("act", lambda e: e.activation(out=out, in_=in_, func=func, **kw), R=R_, W=W_)

    def ts(eng, out, in0, s1, s2, op0, op1, R_, W_):
        if s2 is None:
            return P.op(eng, lambda e: e.tensor_scalar(out=out, in0=in0, scalar1=s1, scalar2=None, op0=op0), R=R_, W=W_)
        return P.op(eng, lambda e: e.tensor_scalar(out=out, in0=in0, scalar1=s1, scalar2=s2, op0=op0, op1=op1), R=R_, W=W_)

    def tt(eng, out, in0, in1, op, R_, W_):
        return P.op(eng, lambda e: e.tensor_tensor(out=out, in0=in0, in1=in1, op=op), R=R_, W=W_)

    def stt(eng, out, in0, s, in1, op0, op1, R_, W_):
        return P.op(eng, lambda e: e.scalar_tensor_tensor(out=out, in0=in0, scalar=s, in1=in1, op0=op0, op1=op1), R=R_, W=W_)

    def cp(eng, out, in_, R_, W_):
        return P.op(eng, lambda e: e.tensor_copy(out=out, in_=in_), R=R_, W=W_)

    def dma(q, out, in_, R_, W_, dsem):
        return P.op(q, lambda e: e.dma_start(out=out, in_=in_), R=R_, W=W_, dsem=dsem)

    def rstd_from(ssq_ap, out_ap, stbuf):
        ts("dve", out_ap, ssq_ap, 1.0 / D, EPS, ALU.mult, ALU.add, [stbuf], [stbuf])
        act(out_ap, out_ap, AF.Sqrt, [stbuf], [stbuf])
        P.op("dve", lambda e: e.reciprocal(out=out_ap, in_=out_ap), R=[stbuf], W=[stbuf])

    ov = sb.mark()
    P.op("pool", lambda e: e.memset(ident_f[:], 1.0), W=[Bconst])
    P.op("pool", lambda e: e.affine_select(out=ident_f[:], in_=ident_f[:], pattern=[[-1, 128]], compare_op=ALU.is_equal,
                                           fill=0.0, base=0, channel_multiplier=1), R=[Bconst], W=[Bconst])
    P.op("pool", lambda e: e.memset(ones_f[:], 1.0), W=[Bconst])
    P.op("pool", lambda e: e.memset(epsT[:], EPS), W=[Bconst])
    P.op("pool", lambda e: e.memset(oneT[:], 1.0), W=[Bconst])
    cp("dve", ident_b[:], ident_f[:], [Bconst], [Bconst])
    P.op("pool", lambda e: e.memset(mask_prev[:], -30000.0), W=[Bconst])
    P.op("pool", lambda e: e.memset(mask_next[:], -30000.0), W=[Bconst])
    P.op("pool", lambda e: e.affine_select(out=mask_prev[:], in_=mask_prev[:], pattern=[[0, 4], [1, 128]], compare_op=ALU.is_gt,
                                           fill=0.0, base=0, channel_multiplier=-1), R=[Bconst], W=[Bconst])
    P.op("pool", lambda e: e.affine_select(out=mask_next[:], in_=mask_next[:], pattern=[[0, 4], [-1, 128]], compare_op=ALU.is_gt,
                                           fill=0.0, base=0, channel_multiplier=1), R=[Bconst], W=[Bconst])
    sm_in = sb.alloc([128, 3, 128], F32)
    Bsm = Buf("sm")
    ds_sm = new_dsem()
    dma("sp", sm_in[:, 0, :], small[0:128, :], [], [], ds_sm)
    dma("sp", sm_in[:, 1, :], small[128:256, :], [], [], ds_sm)
    dma("sp", sm_in[0:72, 2, :], small[256:328, :], [], [Bsm], ds_sm)
    for i, n in enumerate((128, 128, 72)):
        tr(BK[0][:, i * 128:i * 128 + n], sm_in[0:n, i, :], ident_f[0:n, 0:n], [Bsm, Bconst], [BKb[0]])
    cp("dve", smallT[:, :], BK[0][:, 0:NV], [BKb[0]], [Bmod])
    cd_in = sb.alloc([128, 128], F32)
    ds_cd = new_dsem()
    Bcd = Buf("cd")
    dma("sp", cd_in[0:R * 8, :], cond, [], [Bcd], ds_cd)
    act(cd_in[0:R * 8, :], cd_in[0:R * 8, :], AF.Silu, [Bcd], [Bcd])
    tr(BK[1][:, 0:R * 8], cd_in[0:R * 8, :], ident_f[0:R * 8, 0:R * 8], [Bcd, Bconst], [BKb[1]])
    cp("dve", sT[:, :], BK[1][:, 0:R * 8], [BKb[1]], [Bmod])
    wm_t = sb.alloc([128, 2, 8, 512], F32)
    wms = Rot([(wm_t[:, i], Buf("wm%d" % i), new_dsem()) for i in range(2)])
    sT3 = sT[:, :].rearrange("p (r k) -> p k r", k=8)
    for l in range(2):
        bank = 2 + l
        for g in range(18):
            wt_, wb_, wd_ = wms.next()
            dma("sp", wt_, w_mod[l, :, g * 512:(g + 1) * 512].rearrange("(kc p) c -> p kc c", p=128), [], [wb_], wd_)
            for c4 in range(4):
                j = g * 4 + c4
                for kc in range(8):
                    mm(BK[bank][:, j * R:(j + 1) * R], wt_[:, kc, c4 * 128:(c4 + 1) * 128], sT3[:, kc, :],
                       kc == 0, kc == 7, [wb_, Bmod], [BKb[bank]])
        for r in range(R):
            tt("dve", modT[:, l, :, r], BK[bank][:, 0:72 * R].rearrange("p (j r) -> p j r", r=R)[:, :, r],
               smallT[:, O_BMOD + l * 72:O_BMOD + (l + 1) * 72], ALU.add, [BKb[bank], Bmod], [Bmod])
    for l in range(2):
        for k in range(3):
            wt_k = 1.0 if k == 1 else 0.5
            gpre = smallT[:, O_G + (l * 6 + k) * 8:O_G + (l * 6 + k) * 8 + 8]
            gpost = smallT[:, O_G + (l * 6 + 3 + k) * 8:O_G + (l * 6 + 3 + k) * 8 + 8]
            for r in range(R):
                stt("dve", gmT[:, l, k, r, :], modT[:, l, (3 * k + 1) * 8:(3 * k + 2) * 8, r], 1.0, gpre, ALU.add, ALU.mult, [Bmod], [Bmod])
                stt("dve", gcT[:, l, k, r, :], modT[:, l, (3 * k + 2) * 8:(3 * k + 3) * 8, r], wt_k, gpost, ALU.mult, ALU.mult, [Bmod], [Bmod])
    act(coef[:, :], smallT[:, O_LAM:O_LAM + 16], AF.Exp, [Bmod], [Bmod], scale=-1.0)
    act(coef[:, :], coef[:, :], AF.Ln, [Bmod], [Bmod], bias=oneT[:, 0:1])
    ts("dve", coef[:, :], coef[:, :], -8.0, None, ALU.mult, None, [Bmod], [Bmod])
    ts("dve", hcoef[:, :], coef[:, :], 0.5, None, ALU.mult, None, [Bmod], [Bmod])
    ts("dve", hgb[:, :], smallT[:, O_GB:O_GB + 32], 0.5, None, ALU.mult, None, [Bmod], [Bmod])
    P.op("pool", lambda e: e.memset(quartT[:], 0.25), W=[Bmod])
    ds_sk = new_dsem()
    dma("sp", esink[:, :], sink.partition_broadcast(128), [], [Bmod], ds_sk)
    act(esink[:, :], esink[:, :], AF.Exp, [Bmod], [Bmod])

    def cast_pairs(dst, src3, ngroups):
        b = Buf("cast")
        d = new_dsem()
        n = ngroups * 2
        i = 0
        for g in range(ngroups):
            for u in range(2):
                i += 1
                dma("pool", dst[g, :, :, u, :], src3[:, u, g * 256:(g + 1) * 256].rearrange("(kc p) c -> p kc c", p=128),
                    [], [b] if i == n else [], d)
        return b

    def cast_rows(dst, src, kcn, step=4):
        b = Buf("cast")
        d = new_dsem()
        v = src.rearrange("(kc p) c -> p kc c", p=128)
        k0 = 0
        while k0 < kcn:
            k1 = min(kcn, k0 + step)
            dma("pool", dst[:, k0:k1, :], v[:, k0:k1, :], [], [b] if k1 == kcn else [], d)
            k0 = k1
        return b

    def cast_ffn(l, j):
        a = cast_pairs(win_s[l][j], ffn_w_in[l, j].rearrange("k (u f) -> k u f", u=2), 11)
        b = cast_rows(wout_s[l][j], ffn_w_out[l, j], NFC)
        return a, b

    Bc = {}
    Bc["ffn00"] = cast_ffn(0, 0)
    if nstages > 1:
        Bc["rgin"] = cast_pairs(rgin_s, rg_w_in.rearrange("k (u f) -> k u f", u=2), 4)
        bgw = Buf("cast")
        dgw = new_dsem()
        i = 0
        for n in range(4):
            for d_ in range(2):
                for g_ in range(2):
                    i += 1
                    dma("pool", rggw_s[n, :, d_, g_, :, :], rg_gate_w[d_, g_, n].rearrange("(jc p) k -> p jc k", p=128),
                        [], [bgw] if i == 16 else [], dgw)
        Bc["rggw"] = bgw
        Bc["rgwo"] = cast_rows(rgwo_s, rg_w_out, 8)
    if nstages > 2:
        Bc["ffn01"] = cast_ffn(0, 1)
    if nstages > 3:
        Bc["ffn10"] = cast_ffn(1, 0)
    if nstages > 4:
        Bc["wqk"] = cast_pairs(wqk_s, wqk2, 6)
        Bc["wv"] = cast_rows(wv_s, wv, 8, step=8)
        Bc["awo"] = cast_rows(awo_s, attn_w_o, 8)
    if nstages > 5:
        Bc["ffn11"] = cast_ffn(1, 1)

    P.barrier(flush=[Bsm, Bcd] + [it[1] for it in wms.items])
    sb.reset(ov)

    def role_of(s, t):
        return nseq if t < 2 else s

    def src_tile(stage, s, t):
        if stage == 0:
            return xin[s, t * 128:(t + 1) * 128, :], []
        return resD[s, t * 128:(t + 1) * 128, :], [Bres[(s, t)]]

    def dst_tile(stage, s, t):
        if full and stage == 5:
            return outp[s, (t - 2) * 128:(t - 1) * 128, :], []
        return resD[s, t * 128:(t + 1) * 128, :], [Bres[(s, t)]]

    class TileCtx:
        def __init__(self, nslots, nxs=2):
            self.slots = Rot([(sb.alloc([128, D], F32), Buf("slot%d" % i), new_dsem()) for i in range(nslots)])
            self.xs = Rot([(sb.alloc([128, D], BF16), Buf("xs%d" % i)) for i in range(nxs)])
            self.junks = Rot([(sb.alloc([128, D], BF16), Buf("junk%d" % i)) for i in range(2)])
            self.tbank = Rot([0, 1])

            self.base_slots = self.slots
            self.extra_flush = []

        def flush(self):
            return [it[1] for it in self.base_slots.items] + list(self.extra_flush)

    def make_gcbc(l, k, r):
        t_, b_ = gcbc.next()
        dg = tc_cur[0]
        for half in range(2):
            bank = 6 + half
            for q4 in range(4):
                kc = half * 4 + q4
                dt_, db_ = dg.next()
                ts("dve", dt_, ident_f[:, :], gcT[:, l, k, r, kc:kc + 1], None, ALU.mult, None, [Bconst, Bmod], [db_])
                mm(BK[bank][:, q4 * 128:(q4 + 1) * 128], ones_f[:, :], dt_, True, True, [db_, Bconst], [BKb[bank]])
        cp("dve", t_[:, :], pair(6), [BKb[6], BKb[7]], [b_])
        return t_, b_

    tc_cur = [None]

    def prepA(tc, stage, s, t):
        sl, slb, sld = tc.slots.next()
        src, srcb = src_tile(stage, s, t)
        dma("pool", sl[:, :], src, srcb, [slb], sld)
        st, stb = stat.next()
        jk, jkb = tc.junks.next()
        act(jk[:, :], sl[:, :], AF.Square, [slb], [jkb, stb], accum=st[:, 0:1])
        rstd_from(st[:, 0:1], st[:, 1:2], stb)
        xs, xsb = tc.xs.next()
        ts("dve", xs[:, :], sl[:, :], st[:, 1:2], None, ALU.mult, None, [slb, stb], [xsb])
        return xs, xsb

    def prepB(tc, l, k, r, xs, xsb, hT_ap, hTb):
        b = tc.tbank.next()
        pb = bank_bf(b)
        for kc in range(8):
            tr(pb[:, kc * 128:(kc + 1) * 128], xs[:, kc * 128:(kc + 1) * 128], ident_b[:, :], [xsb, Bconst], [BKb[b]])
        for kc in range(8):
            ts("dve", hT_ap[:, kc, :], pb[:, kc * 128:(kc + 1) * 128], gmT[:, l, k, r, kc:kc + 1],
               modT[:, l, 3 * k * 8 + kc, r:r + 1], ALU.mult, ALU.add, [BKb[b], Bmod], [hTb])

    def post_update(tc, stage, s, t, yb, gc_t, gc_b):
        st, stb = stat.next()
        jk, jkb = tc.junks.next()
        act(jk[:, 0:512], BK[yb], AF.Square, [BKb[yb]], [jkb, stb], accum=st[:, 0:1])
        act(jk[:, 512:1024], BK[yb + 1], AF.Square, [BKb[yb + 1]], [jkb, stb], accum=st[:, 1:2])
        tt("dve", st[:, 2:3], st[:, 0:1], st[:, 1:2], ALU.add, [stb], [stb])
        rstd_from(st[:, 2:3], st[:, 3:4], stb)
        sl, slb, sld = tc.slots.next()
        src, srcb = src_tile(stage, s, t)
        dma("pool", sl[:, :], src, srcb, [slb], sld)
        stt("dve", pair(yb), pair(yb), st[:, 3:4], gc_t[:, :], ALU.mult, ALU.mult, [BKb[yb], BKb[yb + 1], stb, gc_b], [BKb[yb], BKb[yb + 1]])
        tt("dve", sl[:, :], sl[:, :], pair(yb), ALU.add, [slb, BKb[yb], BKb[yb + 1]], [slb])
        dst, dstb = dst_tile(stage, s, t)
        dma("pool", dst, sl[:, :], [slb], dstb, sld)

    def prep_group(tc, stage, l, k, tiles, hT_full, hTbufs):
        G = len(tiles)
        st, stb = stat.next()
        sls = []
        for i, (s_, t) in enumerate(tiles):
            sl, slb, sld = tc.slots.next()
            src, srcb = src_tile(stage, s_, t)
            dma("pool", sl[:, :], src, srcb, [slb], sld)
            sls.append((sl, slb))
        for i, (sl, slb) in enumerate(sls):
            jk, jkb = tc.junks.next()
            act(jk[:, :], sl[:, :], AF.Square, [slb], [jkb, stb], accum=st[:, i:i + 1])
        rstd_from(st[:, 0:G], st[:, 2:2 + G], stb)
        for i, (s_, t) in enumerate(tiles):
            sl, slb = sls[i]
            xs, xsb = tc.xs.next()
            ts("dve", xs[:, :], sl[:, :], st[:, 2 + i:3 + i], None, ALU.mult, None, [slb, stb], [xsb])
            prepB(tc, l, k, role_of(s_, t), xs, xsb, hT_full[:, :, t * 128:(t + 1) * 128], hTbufs[t])

    def post_group(tc, stage, items):
        G = len(items)
        st, stb = stat.next()
        st2, stb2 = stat.next()
        for i, (s_, t, yb, gc_t, gc_b) in enumerate(items):
            jk, jkb = tc.junks.next()
            act(jk[:, 0:512], BK[yb], AF.Square, [BKb[yb]], [jkb, stb], accum=st[:, 2 * i:2 * i + 1])
            act(jk[:, 512:1024], BK[yb + 1], AF.Square, [BKb[yb + 1]], [jkb, stb], accum=st[:, 2 * i + 1:2 * i + 2])
        stv = st[:, 0:2 * G].rearrange("p (g two) -> p g two", two=2)
        tt("dve", st2[:, 0:G], stv[:, :, 0], stv[:, :, 1], ALU.add, [stb], [stb2])
        rstd_from(st2[:, 0:G], st2[:, 2:2 + G], stb2)
        sls = []
        for i, (s_, t, yb, gc_t, gc_b) in enumerate(items):
            sl, slb, sld = tc.slots.next()
            src, srcb = src_tile(stage, s_, t)
            dma("pool", sl[:, :], src, srcb, [slb], sld)
            sls.append((sl, slb, sld))
        for i, (s_, t, yb, gc_t, gc_b) in enumerate(items):
            sl, slb, sld = sls[i]
            stt("dve", pair(yb), pair(yb), st2[:, 2 + i:3 + i], gc_t[:, :], ALU.mult, ALU.mult, [BKb[yb], BKb[yb + 1], stb2, gc_b], [BKb[yb], BKb[yb + 1]])
            tt("dve", sl[:, :], sl[:, :], pair(yb), ALU.add, [slb, BKb[yb], BKb[yb + 1]], [slb])
            dst, dstb = dst_tile(stage, s_, t)
            dma("pool", dst, sl[:, :], [slb], dstb, sld)

    def out_proj_mm(lhs_of_kc, lhs_bufs, wo, wo_bufs, yb):
        for kc in range(8):
            for h in range(2):
                mm(BK[yb + h], lhs_of_kc(kc), wo[:, kc, h * 512:(h + 1) * 512], kc == 0, kc == 7, lhs_bufs + wo_bufs, [BKb[yb + h]])

    class DiagRot:
        def __init__(self):
            self.r = Rot([(sb.alloc([128, 128], F32), Buf("dg%d" % i)) for i in range(2)])

        def next(self):
            t_, b_ = self.r.next()
            return t_[:, :], b_

    def ffn_stage(stage, l, j, cast_bufs):
        k = 0 if j == 0 else 2
        m0 = sb.mark()
        tc = TileCtx(4)
        tc_cur[0] = DiagRot()
        wout = sb.alloc([128, NFC, D], BF16)
        hT_t = [sb.alloc([128, 8, 1024], BF16) for _ in range(2)]
        hTb = [[Buf("hT%d_%d" % (a, i)) for i in range(8)] for a in range(2)]
        actT = sb.alloc([128, NFC, 1024], BF16)
        actb = [Buf("act%d" % i) for i in range(NFC)]
        sg = Rot([(sb.alloc([128, 512], F32), Buf("sg%d" % i)) for i in range(2)])
        gu = Rot([(2, 3), (4, 5)])
        ybank = Rot([6, 0])
        Bwin, Bwout = cast_bufs
        wob = []
        for (k0, k1) in ((0, 6), (6, 12), (12, 17), (17, 22)):
            b_ = Buf("wout")
            dma("sp", wout[:, k0:k1, :], wout_s[l][j][:, k0:k1, :], [Bwout], [b_], new_dsem())
            wob.append((k0, k1, b_))

        def wob_of(fc):
            for (k0, k1, b_) in wob:
                if k0 <= fc < k1:
                    return b_
        sbs = []
        x_only = full and stage == 5
        if not x_only:
            ctx_tiles = [(s, t) for s in range(nseq) for t in range(2)]
            for i in range(0, len(ctx_tiles), 8):
                sbs.append(ctx_tiles[i:i + 8])
        for s in range(nseq):
            sbs.append([(s, t) for t in range(2, 10)])
            sbs.append([(s, t) for t in range(10, 18)])
        nsb = len(sbs)

        def do_prepA(bi, i):
            s, t = sbs[bi][i]
            return prepA(tc, stage, s, t)

        def do_prepB(bi, i, xs, xsb):
            s, t = sbs[bi][i]
            prepB(tc, l, k, role_of(s, t), xs, xsb, hT_t[bi % 2][:, :, i * 128:(i + 1) * 128], hTb[bi % 2][i])

        for i in range(len(sbs[0])):
            xs, xsb = do_prepA(0, i)
            do_prepB(0, i, xs, xsb)
        for bi in range(nsb):
            tiles = sbs[bi]
            ntl = len(tiles)
            ncols = ntl * 128
            hT = hT_t[bi % 2]
            hb = hTb[bi % 2]
            role = role_of(*tiles[0])
            gc_t, gc_b = make_gcbc(l, k, role)
            halves = [(c0, min(512, ncols - c0)) for c0 in range(0, ncols, 512)]
            pend = None
            for g in range(11):
                wt_, wb_, wd_ = wslot.next()
                dma("sp", wt_, win_s[l][j][g], [Bwin], [wb_], wd_)
                for c in range(2):
                    fc = 2 * g + c
                    for (c0, cn) in halves:
                        gb, ub = gu.next()
                        hbs = hb[c0 // 128:(c0 + cn) // 128]
                        for kc in range(8):
                            mm(BK[gb][:, 0:cn], wt_[:, kc, 0, c * 128:(c + 1) * 128], hT[:, kc, c0:c0 + cn], kc == 0, kc == 7, [wb_] + hbs, [BKb[gb]])
                        for kc in range(8):
                            mm(BK[ub][:, 0:cn], wt_[:, kc, 1, c * 128:(c + 1) * 128], hT[:, kc, c0:c0 + cn], kc == 0, kc == 7, [wb_] + hbs, [BKb[ub]])
                        sg_t, sg_b = sg.next()
                        act(sg_t[:, 0:cn], BK[gb][:, 0:cn], AF.Silu, [BKb[gb]], [sg_b])
                        tt("dve", actT[:, fc, c0:c0 + cn], sg_t[:, 0:cn], BK[ub][:, 0:cn], ALU.mult, [sg_b, BKb[ub]], [actb[fc]])
                if bi + 1 < nsb:
                    nn = len(sbs[bi + 1])
                    if pend is not None:
                        do_prepB(bi + 1, pend[0], pend[1], pend[2])
                        pend = None
                    if g < nn:
                        xs, xsb = do_prepA(bi + 1, g)
                        pend = (g, xs, xsb)
            if pend is not None:
                do_prepB(bi + 1, pend[0], pend[1], pend[2])
            for i, (s, t) in enumerate(tiles):
                yb = ybank.next()
                for fc in range(NFC):
                    for h in range(2):
                        mm(BK[yb + h], actT[:, fc, i * 128:(i + 1) * 128], wout[:, fc, h * 512:(h + 1) * 512],
                           fc == 0, fc == NFC - 1, [actb[fc], wob_of(fc)], [BKb[yb + h]])
                post_update(tc, stage, s, t, yb, gc_t, gc_b)
        P.barrier(flush=tc.flush())
        sb.reset(m0)

    def out_proj_tile(tc, stage, s, t, lhs_of_kc, lhs_bufs, wo, wo_bufs, gc_t, gc_b, yb):
        for kc in range(8):
            for h in range(2):
                mm(BK[yb + h], lhs_of_kc(kc), wo[:, kc, h * 512:(h + 1) * 512], kc == 0, kc == 7, lhs_bufs + wo_bufs, [BKb[yb + h]])
        post_update(tc, stage, s, t, yb, gc_t, gc_b)

    def load_wo(scr, castbuf):
        wo = wslots_t[:, 0:2].rearrange("p a k u c -> p (a k u c)").rearrange("p (k c) -> p k c", k=8)
        it0, it1 = wslot.items[0], wslot.items[1]
        o = P.op("sp", lambda e: e.dma_start(out=wo, in_=scr), R=[castbuf], W=[it0[1], it1[1]], dsem=it0[2])
        wslot.i = 2
        return wo, it0[1], it1[1]

    def rglru_stage(stage, l):
        m0 = sb.mark()
        tc = TileCtx(2, 1)
        tc_cur[0] = DiagRot()
        hT = sb.alloc([128, 8, NTOK], BF16)
        hTb = [Buf("hTf%d" % i) for i in range(NT)]
        prodT = sb.alloc([128, 8, NTOK], BF16)
        prb = [Buf("prod%d" % i) for i in range(8)]
        gw_t = sb.alloc([128, 1, 8, 256], BF16)
        gws = Rot([(gw_t[:, i], Buf("gw%d" % i), new_dsem()) for i in range(1)])
        XW = 2310
        xr = sb.alloc([128, 2, XW], F32)
        xrb = [Buf("xr0"), Buf("xr1")]
        rx = sb.alloc([128, 2, XW], F32)
        rxb_t = sb.alloc([128, 2, XW], BF16)
        rxB = [Buf("rx0"), Buf("rx1")]
        AB = Rot([(sb.alloc([128, 2, 512], F32), sb.alloc([128, 2, 512], F32), Buf("ab%d" % i)) for i in range(2)])
        q_t = sb.alloc([128, 2, 512], F32)
        q_b = Buf("q")
        SB_ = Rot([(sb.alloc([128, 2, 512], F32), Buf("sbk%d" % i)) for i in range(2)])
        carry = sb.alloc([128, 2], F32)
        Bcarry = Buf("carry")
        alias_slots = [(rx[:, 0, 0:1024], rxB[0], new_dsem()), (rx[:, 1, 0:1024], rxB[1], new_dsem())]
        tc.extra_flush = [rxB[0], rxB[1]]
        P.op("pool", lambda e: e.memset(xr[:, :, :], 0.0), W=xrb)
        ranges = [(0, 256, 2, 0)] + [(256 + 512 * i, 512, 261 + 512 * i, 259 + 512 * i) for i in range(4)]
        pb = Rot([0, 1, 6, 7])
        gb4 = Rot([(2, 3), (4, 5)])
        def prep_tile(s_, t):
            xs, xsb = prepA(tc, stage, s_, t)
            prepB(tc, l, 1, role_of(s_, t), xs, xsb, hT[:, :, t * 128:(t + 1) * 128], hTb[t])

        for s in range(nseq):
            if s == 0:
                for t in range(0, NT, 2):
                    prep_group(tc, stage, l, 1, [(s, t), (s, t + 1)], hT, hTb)
            gc_c = make_gcbc(l, 1, nseq)
            gc_x = make_gcbc(l, 1, s)
            for n in range(4):
                wt_, wb_, wd_ = wslot.next()
                dma("sp", wt_, rgin_s[n], [Bc["rgin"]], [wb_], wd_)
                gt_, gbuf_, gd_ = gws.next()
                dma("sp", gt_, rggw_s[n].rearrange("p d g j k -> p (d g j) k"), [Bc["rggw"]], [gbuf_], gd_)
                for c in range(2):
                    gch = 2 * n + c
                    for (h0, cn, dc0, u0) in ranges:
                        hbs = hTb[h0 // 128:(h0 + cn) // 128]
                        b = pb.next()
                        for kc in range(8):
                            mm(BK[b][:, 0:cn], wt_[:, kc, 0, c * 128:(c + 1) * 128], hT[:, kc, h0:h0 + cn], kc == 0, kc == 7, [wb_] + hbs, [BKb[b]])
                        act(prodT[:, gch, h0:h0 + cn], BK[b][:, 0:cn], AF.Gelu, [BKb[b]], [prb[gch]])
                        b = pb.next()
                        for kc in range(8):
                            mm(BK[b][:, 0:cn], wt_[:, kc, 1, c * 128:(c + 1) * 128], hT[:, kc, h0:h0 + cn], kc == 0, kc == 7, [wb_] + hbs, [BKb[b]])
                        act(xr[:, c, dc0:dc0 + cn], BK[b][:, 0:cn], AF.Identity, [BKb[b]], [xrb[c]])
                    NU = 2307
                    cw = [smallT[:, O_CW + kk * 8 + gch:O_CW + kk * 8 + gch + 1] for kk in range(4)]
                    cb = smallT[:, O_CB + gch:O_CB + gch + 1]
                    ts("pool", rx[:, c, 0:NU], xr[:, c, 0:NU], cw[0], cb, ALU.mult, ALU.add, [xrb[c], Bmod], [rxB[c]])
                    for kk in range(1, 4):
                        stt("dve", rx[:, c, 0:NU], xr[:, c, kk:kk + NU], cw[kk], rx[:, c, 0:NU], ALU.mult, ALU.add, [xrb[c], rxB[c], Bmod], [rxB[c]])
                    cp("pool", rxb_t[:, c, 0:NU], rx[:, c, 0:NU], [rxB[c]], [rxB[c]])
                for d_ in range(2):
                    order = ranges if d_ == 0 else [ranges[0]] + ranges[:0:-1]
                    for ri, (h0, cn, dc0, u0) in enumerate(order):
                        a_t, i_t, ab_b = AB.next()
                        for k2 in range(2):
                            gch = 2 * n + k2
                            rb, ib = ((2, 3), (4, 5))[k2]
                            for g_ in range(2):
                                bb = rb if g_ == 0 else ib
                                for jc in range(2):
                                    mm(BK[bb][:, 0:cn], gt_[:, (d_ * 2 + g_) * 2 + jc, k2 * 128:(k2 + 1) * 128], rxb_t[:, jc, u0:u0 + cn],
                                       jc == 0, jc == 1, [gbuf_, rxB[jc]], [BKb[bb]])
                            hb_ = [hgb[:, (d_ * 2 + g_) * 8 + gch:(d_ * 2 + g_) * 8 + gch + 1] for g_ in range(2)]
                            act(a_t[:, k2, 0:cn], BK[rb][:, 0:cn], AF.Tanh, [BKb[rb], Bmod], [ab_b], bias=hb_[0], scale=0.5)
                            act(i_t[:, k2, 0:cn], BK[ib][:, 0:cn], AF.Tanh, [BKb[ib], Bmod], [ab_b], bias=hb_[1], scale=0.5)
                            hc_ = hcoef[:, d_ * 8 + gch:d_ * 8 + gch + 1]
                            act(a_t[:, k2, 0:cn], a_t[:, k2, 0:cn], AF.Exp, [ab_b, Bmod], [ab_b], scale=hc_, bias=hc_)
                        act(q_t[:, :, 0:cn], a_t[:, :, 0:cn], AF.Square, [ab_b], [q_b])
                        act(q_t[:, :, 0:cn], q_t[:, :, 0:cn], AF.Sqrt, [q_b, Bmod], [q_b], scale=-0.25, bias=quartT[:, 0:1])
                        stt("dve", i_t[:, :, 0:cn], i_t[:, :, 0:cn], 1.0, rx[:, :, u0:u0 + cn], ALU.add, ALU.mult, [ab_b, rxB[0], rxB[1]], [ab_b])
                        tt("pool", i_t[:, :, 0:cn], i_t[:, :, 0:cn], q_t[:, :, 0:cn], ALU.mult, [ab_b, q_b], [ab_b])
                        if d_ == 0:
                            for k2 in range(2):
                                if ri == 0:
                                    init = 0.0
                                else:
                                    pu = 255 if ri == 1 else u0 - 1
                                    init = xr[:, k2, pu:pu + 1]
                                P.op("dve", lambda e, o_=xr[:, k2, u0:u0 + cn], a_=a_t[:, k2, 0:cn], b_=i_t[:, k2, 0:cn], in_=init:
                                     e.tensor_tensor_scan(out=o_, data0=a_, data1=b_, initial=in_, op0=ALU.mult, op1=ALU.add),
                                     R=[ab_b, xrb[k2]], W=[xrb[k2]])
                        else:
                            s_t, s_b = SB_.next()
                            for k2 in range(2):
                                init = 0.0 if ri == 0 else carry[:, k2:k2 + 1]
                                P.op("dve", lambda e, o_=s_t[:, k2, 0:cn][:, ::-1], a_=a_t[:, k2, 0:cn][:, ::-1], b_=i_t[:, k2, 0:cn][:, ::-1], in_=init:
                                     e.tensor_tensor_scan(out=o_, data0=a_, data1=b_, initial=in_, op0=ALU.mult, op1=ALU.add),
                                     R=[ab_b, Bcarry], W=[s_b])
                                cp("dve", carry[:, k2:k2 + 1], s_t[:, k2, 0:1], [s_b], [Bcarry])
                            tt("pool", s_t[:, :, 0:cn], s_t[:, :, 0:cn], xr[:, :, u0:u0 + cn], ALU.add, [s_b, xrb[0], xrb[1]], [s_b])
                            tt("dve", prodT[:, 2 * n:2 * n + 2, h0:h0 + cn], prodT[:, 2 * n:2 * n + 2, h0:h0 + cn], s_t[:, :, 0:cn], ALU.mult,
                               [prb[2 * n], prb[2 * n + 1], s_b], [prb[2 * n], prb[2 * n + 1]])
                for c in range(2):
                    for (p0, pn) in ((0, 2), (258, 3), (2309, 1)):
                        P.op("pool", lambda e, a_=xr[:, c, p0:p0 + pn]: e.memset(a_, 0.0), W=[xrb[c]])
            if s == 0:
                print("rglru stage sbuf end", sb.off, flush=True)
            wo, wb0, wb1 = load_wo(rgwo_s, Bc["rgwo"])
            yb = Rot([6, 2, 4])
            tc.slots = Rot(list(tc.base_slots.items) + alias_slots)
            for t in range(0, NT, 2):
                g_ = gc_c if t < 2 else gc_x
                items = []
                for tt_ in (t, t + 1):
                    ybk = yb.next()
                    out_proj_mm(lambda kc, tt_=tt_: prodT[:, kc, tt_ * 128:(tt_ + 1) * 128], prb, wo, [wb0, wb1], ybk)
                    items.append((s, tt_, ybk, g_[0], g_[1]))
                post_group(tc, stage, items)
                if s + 1 < nseq:
                    prep_group(tc, stage, l, 1, [(s + 1, t), (s + 1, t + 1)], hT, hTb)
            tc.slots = tc.base_slots
        P.barrier(flush=tc.flush())
        sb.reset(m0)

    def attn_stage(stage, l):
        m0 = sb.mark()
        tc = TileCtx(2)
        tc_cur[0] = DiagRot()
        hT = sb.alloc([128, 8, NTOK], BF16)
        hTb = [Buf("hTf%d" % i) for i in range(NT)]
        QT = sb.alloc([128, 8, SEQ], BF16)
        Qb = [Buf("q%d" % i) for i in range(8)]
        alias_slots = [(QT[:, j, :].bitcast(F32), Qb[j], new_dsem()) for j in range(2)]
        tc.extra_flush = list(Qb)
        Kd = sb.alloc([128, 4, 2, NTOK], BF16)
        Kb = [Buf("k%d" % i) for i in range(4)]
        Va = sb.alloc([128, NT, 4, 66], BF16)
        Vb = [Buf("v%d" % i) for i in range(NT)]
        cosT = sb.alloc([128, SEQ], F32)
        sinT = sb.alloc([128, SEQ], F32)
        Brope = Buf("rope")
        PT = Rot([(sb.alloc([128, 5, 4, 128], BF16), Buf("pt%d" % i)) for i in range(2)])
        rt = Rot([(sb.alloc([128, 512], F32), Buf("rt%d" % i)) for i in range(2)])
        attn = Rot([(sb.alloc([128, D], BF16), Buf("at%d" % i)) for i in range(2)])
        den = Rot([(sb.alloc([128, 8], F32), Buf("den%d" % i)) for i in range(2)])
        dma("sp", cosT[:, :], ropec, [], [Brope], new_dsem())
        dma("sp", sinT[:, :], ropes, [], [Brope], new_dsem())
        P.op("pool", lambda e: e.memset(Va[:, :, :, 64:66], 1.0), W=Vb)
        P.op("pool", lambda e: e.memset(Kd[:, :, :, :], 0.0), W=Kb)
        pb2 = Rot([(0, 1), (2, 3)])
        pb1 = Rot([0, 1, 2, 3])
        def prep_tile(s_, t):
            xs, xsb = prepA(tc, stage, s_, t)
            prepB(tc, l, 1, role_of(s_, t), xs, xsb, hT[:, :, t * 128:(t + 1) * 128], hTb[t])

        for s in range(nseq):
            for t in range(0, NT, 2):
                prep_group(tc, stage, l, 1, [(s, t), (s, t + 1)], hT, hTb)
            gc_x = make_gcbc(l, 1, s)
            for g in range(6):
                wt_, wb_, wd_ = wslot.next()
                dma("sp", wt_, wqk_s[g], [Bc["wqk"]], [wb_], wd_)
                for c in range(2):
                    if g < 4:
                        dst_of = lambda x0, cn, j=2 * g + c: [(0, 128, QT[:, j, x0:x0 + cn])]
                        dbuf = Qb[2 * g + c]
                    else:
                        kv = 2 * (g - 4) + c
                        dst_of = lambda x0, cn, kv=kv: [(0, 64, Kd[0:64, kv, 0, 256 + x0:256 + x0 + cn]),
                                                        (64, 128, Kd[64:128, kv, 1, 256 + x0:256 + x0 + cn])]
                        dbuf = Kb[kv]
                        b = pb1.next()
                        for kc in range(8):
                            mm(BK[b][:, 0:256], wt_[:, kc, 0, c * 128:(c + 1) * 128], hT[:, kc, 0:256], kc == 0, kc == 7, [wb_] + hTb[0:2], [BKb[b]])
                        cp("dve", Kd[0:64, kv, 0, 0:256], BK[b][0:64, 0:256], [BKb[b]], [dbuf])
                        cp("dve", Kd[64:128, kv, 1, 0:256], BK[b][64:128, 0:256], [BKb[b]], [dbuf])
                    for xi in range(4):
                        x0 = xi * 512
                        h0 = 256 + x0
                        hbs = hTb[h0 // 128:(h0 + 512) // 128]
                        bp, bs_ = pb2.next()
                        for kc in range(8):
                            mm(BK[bp], wt_[:, kc, 0, c * 128:(c + 1) * 128], hT[:, kc, h0:h0 + 512], kc == 0, kc == 7, [wb_] + hbs, [BKb[bp]])
                        for kc in range(8):
                            mm(BK[bs_], wt_[:, kc, 1, c * 128:(c + 1) * 128], hT[:, kc, h0:h0 + 512], kc == 0, kc == 7, [wb_] + hbs, [BKb[bs_]])
                        t1, t1b = rt.next()
                        t2, t2b = rt.next()
                        tt("dve", t1[:, :], BK[bp], cosT[:, x0:x0 + 512], ALU.mult, [BKb[bp], Brope], [t1b])
                        tt("dve", t2[:, :], BK[bs_], sinT[:, x0:x0 + 512], ALU.mult, [BKb[bs_], Brope], [t2b])
                        for (p0, p1, dap) in dst_of(x0, 512):
                            tt("pool", dap, t1[p0:p1, :], t2[p0:p1, :], ALU.add, [t1b, t2b], [dbuf])
            wv_t, Bwv, wvd_ = wslot.next()
            dma("sp", wv_t[:, :, 0, :], wv_s, [Bc["wv"]], [Bwv], wvd_)
            for t in range(NT):
                b = pb1.next()
                for kc in range(8):
                    mm(BK[b][:, 0:256], hT[:, kc, t * 128:(t + 1) * 128], wv_t[:, kc, 0, :], kc == 0, kc == 7, [hTb[t], Bwv], [BKb[b]])
                act(Va[:, t, :, 0:64], BK[b][:, 0:256].rearrange("p (k d) -> p k d", k=4), AF.Copy, [BKb[b]], [Vb[t]])
            wo, wb0, wb1 = load_wo(awo_s, Bc["awo"])
            spair = Rot([0, 2])
            obank = Rot([4, 5])

            def chunks_of(qb):
                tq = 2 + qb
                ch = [(0, None), (1, None)]
                if qb > 0:
                    ch.append((tq - 1, mask_prev))
                ch.append((tq, None))
                if qb < 15:
                    ch.append((tq + 1, mask_next))
                return ch

            def emit_scores(qb, kv):
                chunks = chunks_of(qb)
                pt_t, pt_b = PT.next()
                ci = 0
                while ci < len(chunks):
                    n2 = min(2 if os.environ.get("ATT_EXP2", "1") == "1" else 1, len(chunks) - ci)
                    b0 = spair.next()
                    for cc in range(n2):
                        kt, msk = chunks[ci + cc]
                        first = True
                        if msk is not None:
                            mm(BK[b0 + cc][:, 0:512], ident_b[:, :], msk[:, :, :].rearrange("p g q -> p (g q)"), True, False, [Bconst], [BKb[b0 + cc]])
                            first = False
                        for gq in range(4):
                            h = 4 * kv + gq
                            mm(BK[b0 + cc][:, gq * 128:(gq + 1) * 128], Kd[:, kv, h % 2, kt * 128:(kt + 1) * 128],
                               QT[:, h // 2, qb * 128:(qb + 1) * 128], first, gq == 3, [Kb[kv], Qb[h // 2]], [BKb[b0 + cc]])
                            first = False
                    act(pt_t[:, ci:ci + n2].rearrange("p c g q -> p (c g q)"), ps[:, b0 * 512:(b0 + n2) * 512], AF.Exp,
                        [BKb[b0 + cc_] for cc_ in range(n2)], [pt_b], scale=0.125)
                    ci += n2
                return (qb, kv, chunks, pt_t, pt_b)

            cur_at = [None]
            tbk = Rot([6, 7])

            def emit_pv(item):
                qb, kv, chunks, pt_t, pt_b = item
                nch = len(chunks)
                tq = 2 + qb
                if kv == 0:
                    cur_at[0] = attn.next()
                at_t, at_b = cur_at[0]
                ob = obank.next()
                for gq in range(4):
                    for ci, (kt, msk) in enumerate(chunks):
                        mm(BK[ob][:, gq * 65:(gq + 1) * 65], pt_t[:, ci, gq, :], Va[:, kt, kv, 0:65], ci == 0, ci == nch - 1, [pt_b, Vb[kt]], [BKb[ob]])
                dn_t, dn_b = den.next()
                ov_ = BK[ob][:, 0:260].rearrange("p (g d) -> p g d", d=65)
                tt("dve", dn_t[:, 0:4], ov_[:, :, 64], esink[:, 4 * kv:4 * kv + 4], ALU.add, [BKb[ob], Bmod], [dn_b])
                P.op("dve", lambda e, a_=dn_t[:, 4:8], b_=dn_t[:, 0:4]: e.reciprocal(out=a_, in_=b_), R=[dn_b], W=[dn_b])
                if os.environ.get("ATT_BCAST", "1") == "1":
                    tt("dve", at_t[:, kv * 256:(kv + 1) * 256].rearrange("p (g d) -> p g d", d=64), ov_[:, :, 0:64],
                       dn_t[:, 4:8].unsqueeze(2).to_broadcast([128, 4, 64]), ALU.mult, [BKb[ob], dn_b], [at_b])
                else:
                    for gq in range(4):
                        h = 4 * kv + gq
                        ts("dve", at_t[:, h * 64:(h + 1) * 64], ov_[:, gq, 0:64], dn_t[:, 4 + gq:5 + gq], None, ALU.mult, None, [BKb[ob], dn_b], [at_b])
                if kv == 3:
                    ybk = tbk.next()
                    pbt = bank_bf(ybk)
                    for kc in range(8):
                        tr(pbt[:, kc * 128:(kc + 1) * 128], at_t[:, kc * 128:(kc + 1) * 128], ident_b[:, :], [at_b, Bconst], [BKb[ybk]])
                    cp("dve", hT[:, :, tq * 128:(tq + 1) * 128], pbt[:, 0:1024].rearrange("p (k q) -> p k q", k=8), [BKb[ybk]], [hTb[tq]])

            prev = None
            for qb in range(16):
                for kv in range(4):
                    cur = emit_scores(qb, kv)
                    if os.environ.get("ATT_PIPE", "1") != "1":
                        emit_pv(cur)
                        continue
                    if prev is not None:
                        emit_pv(prev)
                    prev = cur
            if prev is not None:
                emit_pv(prev)
            ybr = Rot([6, 2, 4])
            tc.slots = Rot(list(tc.base_slots.items) + alias_slots)
            for tq in range(2, NT, 2):
                items = []
                for tt_ in (tq, tq + 1):
                    ybk = ybr.next()
                    out_proj_mm(lambda kc, tt_=tt_: hT[:, kc, tt_ * 128:(tt_ + 1) * 128], [hTb[tt_]], wo, [wb0, wb1], ybk)
                    items.append((s, tt_, ybk, gc_x[0], gc_x[1]))
                post_group(tc, stage, items)
            tc.slots = tc.base_slots
        print("attn stage sbuf end", sb.off, flush=True)
        P.barrier(flush=tc.flush())
        sb.reset(m0)

    if nstages > 0:
        ffn_stage(0, 0, 0, Bc["ffn00"])
    if nstages > 1:
        rglru_stage(1, 0)
    if nstages > 2:
        ffn_stage(2, 0, 1, Bc["ffn01"])
    if nstages > 3:
        ffn_stage(3, 1, 0, Bc["ffn10"])
    if nstages > 4:
        attn_stage(4, 1)
    if nstages > 5:
        ffn_stage(5, 1, 1, Bc["ffn11"])

    if not full:
        dcp = new_dsem()
        for s in range(nseq):
            dma("sp", outp[s], resD[s], [Bres[(s, t)] for t in range(NT)], [], dcp)
    def final(e):
        for d in all_ds:
            if d.count:
                e.wait_ge(d.h, d.count)
        return e.nop()
    P.op("sp", final)

    run = P.emit(esems)
    with nc.Block() as block:
        @block.tensor
        def _(e):
            run("pe", e)

        @block.scalar
        def _(e):
            run("act", e)

        @block.vector
        def _(e):
            run("dve", e)

        @block.gpsimd
        def _(e):
            run("pool", e)

        @block.sync
        def _(e):
            run("sp", e)
    print("ops:", {k: len(v) for k, v in P.q.items()}, "marked:", {k: sum(1 for o in v if o.marked) for k, v in P.q.items()},
          "waits:", {k: sum(len(o.waits) for o in v) for k, v in P.q.items()}, "sbuf peak", sb.peak, "dsems", nds[0], flush=True)
    return nc


def rope_tables():
    inv = 1.0 / (10000.0 ** (np.arange(0, 32, 2, dtype=np.float32) / 32.0))
    pos = np.arange(SEQ)
    row = (pos // 64).astype(np.float32)
    col = (pos % 64).astype(np.float32)
    cosT = np.zeros((128, SEQ), np.float32)
    sinT = np.zeros((128, SEQ), np.float32)
    for p in range(128):
        d = p % 64
        base = row if d < 32 else col
        ang = (base * inv[d % 16]).astype(np.float32)
        cosT[p] = np.cos(ang)
        sg = -1.0 if (d % 32) < 16 else 1.0
        sinT[p] = sg * np.sin(ang)
    return cosT, sinT


def host_layout(inputs, nseq, core):
    f = lambda a: np.ascontiguousarray(np.asarray(a, dtype=np.float32))
    x = f(inputs["x"])
    ctx = f(inputs["ctx"])
    c = f(inputs["c"])
    b0 = core * nseq
    xin = np.concatenate([ctx[b0:b0 + nseq], x[b0:b0 + nseq]], axis=1)
    cond = np.concatenate([c[b0:b0 + nseq], f(inputs["c_ctx"])[None]], axis=0).reshape(-1, 128)
    small = np.concatenate([
        f(inputs["b_mod"]).reshape(-1, 128), f(inputs["norm_g"]).reshape(-1, 128),
        f(inputs["rg_conv_w"]).reshape(-1, 128), f(inputs["rg_conv_b"]).reshape(-1, 128),
        f(inputs["rg_gate_b"]).reshape(-1, 128), f(inputs["rg_lambda"]).reshape(-1, 128)], axis=0)
    assert small.shape[0] == NV
    wqkv = f(inputs["attn_w_qkv"])[0]
    wq = wqkv[:, :1024]
    wk = wqkv[:, 1024:1280]
    kdup = np.concatenate([np.concatenate([wk[:, kv * 64:(kv + 1) * 64]] * 2, axis=1) for kv in range(4)], axis=1)
    plain = np.concatenate([wq, kdup], axis=1)
    dd = np.arange(64)
    part = np.where((dd % 32) < 16, dd + 16, dd - 16)
    perm = (np.arange(1536) // 64) * 64 + part[np.arange(1536) % 64]
    sw = plain[:, perm]
    wqk2 = np.ascontiguousarray(np.stack([plain, sw], axis=1))
    cosT, sinT = rope_tables()
    return {
        "xin": np.ascontiguousarray(xin), "cond": np.ascontiguousarray(cond), "small": small,
        "sink": f(inputs["attn_sink"]).reshape(1, 16), "ropec": cosT, "ropes": sinT,
        "w_mod": f(inputs["w_mod"]), "ffn_w_in": f(inputs["ffn_w_in"]), "ffn_w_out": f(inputs["ffn_w_out"]),
        "rg_w_in": f(inputs["rg_w_in"])[0], "rg_gate_w": f(inputs["rg_gate_w"])[0], "rg_w_out": f(inputs["rg_w_out"])[0],
        "wqk2": wqk2, "wv": np.ascontiguousarray(wqkv[:, 1280:1536]), "attn_w_o": f(inputs["attn_w_o"])[0],
    }


def kernel(**inputs):
    nseq = 32 // NCORES
    nc = build_program(nseq, 6)
    in_maps = [host_layout(inputs, nseq, core) for core in range(NCORES)]
    res = run_bass_kernel_spmd(nc, in_maps, core_ids=list(range(NCORES)))
    out = np.concatenate([np.asarray(r["out"]) for r in res.results], axis=0)
    return out.astype(np.float32)
```
